# Optimizing a Trainium2 kernel written in Bass

```python
import jax, jax.numpy as jnp
from jax import lax
import numpy as np

D_MODEL = 1024
BATCH = 2
SEQ = 8192
DEPTH = 4
DEC_BATCH = 128
DEC_SEQ = 1
PAST_LEN = 8192
PAGE_SIZE = 128

N_A_LAYERS = DEPTH // 2
N_B_LAYERS = DEPTH - N_A_LAYERS
MIX_WIDTH = D_MODEL
XATTN_HEADS = 4
XATTN_WIDTH = MIX_WIDTH // 4
XATTN_HEAD_DIM = XATTN_WIDTH // XATTN_HEADS
N_MEM = 256
HGRN_WIDTH = MIX_WIDTH - XATTN_WIDTH
HGRN_HEAD_DIM = 128
HGRN_HEADS = HGRN_WIDTH // HGRN_HEAD_DIM
HGRN_CHUNK = 64
SWA_HEAD_DIM = 64
SWA_WIDTH = MIX_WIDTH - XATTN_WIDTH
SWA_HEADS = SWA_WIDTH // SWA_HEAD_DIM
SWA_KV_HEADS = 4
SWA_GROUP = SWA_HEADS // SWA_KV_HEADS
SWA_KV_WIDTH = SWA_KV_HEADS * SWA_HEAD_DIM
WINDOW = 128
ROPE_THETA = 10000.0
D_FF = 4 * D_MODEL
EPS = 1e-6

kernel_name = 'yoco_hgrn2_swa_sink_memxattn_step'


def rms_norm(x, gain):
    xf = x.astype(jnp.float32)
    y = xf * lax.rsqrt(jnp.mean(xf * xf, axis=-1, keepdims=True) + EPS)
    return (y * gain.astype(jnp.float32)).astype(x.dtype)


def rope(x, pos):
    half = x.shape[-1] // 2
    inv = ROPE_THETA ** (-jnp.arange(half, dtype=jnp.float32) / half)
    ang = pos.astype(jnp.float32)[:, None] * inv[None, :]
    cos = jnp.cos(ang)[None, :, None, :]
    sin = jnp.sin(ang)[None, :, None, :]
    xf = x.astype(jnp.float32)
    x1, x2 = xf[..., :half], xf[..., half:]
    return jnp.concatenate([x1 * cos - x2 * sin, x2 * cos + x1 * sin], axis=-1).astype(x.dtype)


def hgrn2_scan(q, k, v, log_f, s0):
    B, L, H, _ = q.shape
    C = min(HGRN_CHUNK, L)
    n = -(-L // C)
    pad = n * C - L

    def prep(a):
        a = jnp.pad(a.astype(jnp.float32), ((0, 0), (0, pad), (0, 0), (0, 0)))
        return a.reshape(B, n, C, H, a.shape[-1]).transpose(1, 0, 3, 2, 4)

    qc, kc, vc, gc = prep(q), prep(k), prep(v), prep(log_f)
    causal = jnp.tril(jnp.ones((C, C), dtype=bool))[:, :, None]

    def step(S, inp):
        qi, ki, vi, gi = inp
        b = jnp.cumsum(gi, axis=2)
        o_inter = jnp.einsum('bhtk,bhkv->bhtv', qi * jnp.exp(b), S)
        diff = b[:, :, :, None, :] - b[:, :, None, :, :]
        decay = jnp.exp(jnp.where(causal, diff, -jnp.inf))
        scores = jnp.einsum('bhtk,bhsk,bhtsk->bhts', qi, ki, decay)
        o = o_inter + jnp.einsum('bhts,bhsv->bhtv', scores, vi)
        b_last = b[:, :, -1:, :]
        S_new = jnp.exp(b_last[:, :, 0, :])[..., None] * S + jnp.einsum(
            'bhsk,bhsv->bhkv', ki * jnp.exp(b_last - b), vi)
        return S_new, o

    S_fin, o = lax.scan(step, s0.astype(jnp.float32), (qc, kc, vc, gc))
    o = o.transpose(1, 0, 3, 2, 4).reshape(B, n * C, H, -1)[:, :L]
    return o, S_fin.astype(s0.dtype)


def hgrn2_mix(proj, lb, g_gain, s0):
    B, L, _ = proj.shape
    q, f_logit, i_in, g = jnp.split(proj, 4, axis=-1)
    shp = (B, L, HGRN_HEADS, HGRN_HEAD_DIM)
    q = jax.nn.silu(q).reshape(shp)
    z = f_logit.astype(jnp.float32).reshape(shp)
    lbh = lb.astype(jnp.float32).reshape(HGRN_HEADS, HGRN_HEAD_DIM)
    log_f = jnp.log(lbh + (1.0 - lbh) * jax.nn.sigmoid(z))
    k = (1.0 - lbh) * jax.nn.sigmoid(-z)
    o, S = hgrn2_scan(q, k, i_in.reshape(shp), log_f, s0)
    o = rms_norm(o, g_gain.reshape(HGRN_HEADS, HGRN_HEAD_DIM)).reshape(B, L, HGRN_WIDTH)
    return (o * jax.nn.silu(g.astype(jnp.float32))).astype(proj.dtype), S


def swa_attend(q, k_ext, v_ext, sinks, base):
    B, L, H, d = q.shape
    QB = min(WINDOW, L)
    n = -(-L // QB)
    pad = n * QB - L
    padw = ((0, 0), (0, pad), (0, 0), (0, 0))
    q = jnp.pad(q, padw)
    k_ext = jnp.pad(k_ext, padw)
    v_ext = jnp.pad(v_ext, padw)
    kidx = jnp.arange(n)[:, None] * QB + jnp.arange(WINDOW + QB)[None, :]
    kb = k_ext[:, kidx]
    vb = v_ext[:, kidx]
    qb = q.reshape(B, n, QB, SWA_KV_HEADS, SWA_GROUP, d)
    qpos = base + jnp.arange(n * QB).reshape(n, QB)
    kpos = base - WINDOW + kidx
    rel = qpos[:, :, None] - kpos[:, None, :]
    mask = (rel >= 0) & (rel < WINDOW) & (kpos[:, None, :] >= 0)
    s = jnp.einsum('bnqhgd,bnkhd->bnhgqk', qb, kb,
                   preferred_element_type=jnp.float32) * (d ** -0.5)
    s = jnp.where(mask[None, :, None, None], s, -jnp.inf)
    sink = sinks.astype(jnp.float32).reshape(SWA_KV_HEADS, SWA_GROUP)[None, None, :, :, None, None]
    m = jnp.maximum(jnp.max(s, axis=-1, keepdims=True), sink)
    p = jnp.exp(s - m)
    den = jnp.sum(p, axis=-1, keepdims=True) + jnp.exp(sink - m)
    o = jnp.einsum('bnhgqk,bnkhd->bnqhgd', (p / den).astype(vb.dtype), vb)
    return o.reshape(B, n * QB, H, d)[:, :L]


def mem_attend(q, mk, mv):
    s = jnp.einsum('blhd,bmhd->bhlm', q, mk, preferred_element_type=jnp.float32) * (q.shape[-1] ** -0.5)
    p = jax.nn.softmax(s, axis=-1).astype(mv.dtype)
    return jnp.einsum('bhlm,bmhd->blhd', p, mv)


def lower_bounds(lb_logits):
    c = jnp.cumsum(jax.nn.softmax(lb_logits.astype(jnp.float32), axis=0), axis=0)
    return c - c[0:1]


def trunk(x, base, mem_k, mem_v, hgrn_s0, swa_k_past, swa_v_past,
          w_in_a, lb_logits, hgrn_norm, w_in_b, sinks, kv_norm, w_kv,
          w_out, norm_pre_mix, norm_post_mix, norm_pre_mlp, norm_post_mlp, w_up, w_down):
    B, L, _ = x.shape
    pos = base + jnp.arange(L)
    lbs = lower_bounds(lb_logits)
    h = x
    new_hgrn = []
    k_ext = swa_k_past
    v_ext = swa_v_past
    for l in range(DEPTH):
        if l == N_A_LAYERS:
            kv = rms_norm(h, kv_norm) @ w_kv
            k_new = rope(kv[..., :SWA_KV_WIDTH].reshape(B, L, SWA_KV_HEADS, SWA_HEAD_DIM), pos)
            v_new = kv[..., SWA_KV_WIDTH:].reshape(B, L, SWA_KV_HEADS, SWA_HEAD_DIM)
            k_ext = jnp.concatenate([swa_k_past, k_new.astype(swa_k_past.dtype)], axis=1)
            v_ext = jnp.concatenate([swa_v_past, v_new.astype(swa_v_past.dtype)], axis=1)
        hn = rms_norm(h, norm_pre_mix[l])
        if l < N_A_LAYERS:
            proj = hn @ w_in_a[l]
            tok_out, s_fin = hgrn2_mix(proj[..., :4 * HGRN_WIDTH], lbs[l], hgrn_norm[l], hgrn_s0[l])
            new_hgrn.append(s_fin)
            cq = proj[..., 4 * HGRN_WIDTH:]
        else:
            j = l - N_A_LAYERS
            proj = hn @ w_in_b[j]
            q = rope(proj[..., :SWA_WIDTH].reshape(B, L, SWA_HEADS, SWA_HEAD_DIM), pos)
            tok_out = swa_attend(q, k_ext.astype(q.dtype), v_ext.astype(q.dtype), sinks[j], base).reshape(B, L, SWA_WIDTH)
            cq = proj[..., SWA_WIDTH:]
        x_out = mem_attend(cq.reshape(B, L, XATTN_HEADS, XATTN_HEAD_DIM),
                           mem_k[l].astype(cq.dtype), mem_v[l].astype(cq.dtype)).reshape(B, L, XATTN_WIDTH)
        mix = jnp.concatenate([tok_out, x_out], axis=-1) @ w_out[l]
        h = h + rms_norm(mix, norm_post_mix[l])
        u = jnp.square(jax.nn.relu(rms_norm(h, norm_pre_mlp[l]) @ w_up[l]))
        h = h + rms_norm(u @ w_down[l], norm_post_mlp[l])
    return h, jnp.stack(new_hgrn), k_ext[:, -WINDOW:], v_ext[:, -WINDOW:]


def setup_inputs(seed: int = 0) -> dict:
    key = jax.random.key(seed)
    ks = iter(jax.random.split(key, 32))

    def nrm(shape, scale):
        return jax.random.normal(next(ks), shape, jnp.float32) * scale

    def gain(shape):
        return 1.0 + 0.05 * jax.random.normal(next(ks), shape, jnp.float32)

    mem_shape_c = (DEPTH, DEC_BATCH, N_MEM, XATTN_HEADS, XATTN_HEAD_DIM)
    swa_shape_c = (DEC_BATCH, WINDOW, SWA_KV_HEADS, SWA_HEAD_DIM)
    return {
        'x_prompt': nrm((BATCH, SEQ, D_MODEL), 1.0),
        'x_sample': nrm((DEC_BATCH, DEC_SEQ, D_MODEL), 1.0),
        'cache_mem_k': nrm(mem_shape_c, 1.0),
        'cache_mem_v': nrm(mem_shape_c, 1.0),
        'state_hgrn': nrm((N_A_LAYERS, DEC_BATCH, HGRN_HEADS, HGRN_HEAD_DIM, HGRN_HEAD_DIM), 0.5),
        'state_swa_k': nrm(swa_shape_c, 1.0),
        'state_swa_v': nrm(swa_shape_c, 1.0),
        'mem_prompt': nrm((BATCH, N_MEM, D_MODEL), 1.0),
        'mem_norm': gain((DEPTH, D_MODEL)),
        'w_mem_kv': nrm((DEPTH, D_MODEL, 2 * XATTN_WIDTH), D_MODEL ** -0.5),
        'w_in_a': nrm((N_A_LAYERS, D_MODEL, 4 * HGRN_WIDTH + XATTN_WIDTH), D_MODEL ** -0.5),
        'lb_logits': nrm((N_A_LAYERS, HGRN_WIDTH), 1.0),
        'hgrn_norm': gain((N_A_LAYERS, HGRN_WIDTH)),
        'w_in_b': nrm((N_B_LAYERS, D_MODEL, SWA_WIDTH + XATTN_WIDTH), D_MODEL ** -0.5),
        'sinks': nrm((N_B_LAYERS, SWA_HEADS), 0.5),
        'kv_norm': gain((D_MODEL,)),
        'w_kv': nrm((D_MODEL, 2 * SWA_KV_WIDTH), D_MODEL ** -0.5),
        'w_out': nrm((DEPTH, MIX_WIDTH, D_MODEL), MIX_WIDTH ** -0.5),
        'norm_pre_mix': gain((DEPTH, D_MODEL)),
        'norm_post_mix': gain((DEPTH, D_MODEL)),
        'norm_pre_mlp': gain((DEPTH, D_MODEL)),
        'norm_post_mlp': gain((DEPTH, D_MODEL)),
        'w_up': nrm((DEPTH, D_MODEL, D_FF), D_MODEL ** -0.5),
        'w_down': nrm((DEPTH, D_FF, D_MODEL), D_FF ** -0.5),
    }


def reference(x_prompt, x_sample, cache_mem_k, cache_mem_v, state_hgrn, state_swa_k, state_swa_v,
              mem_prompt, mem_norm, w_mem_kv, w_in_a, lb_logits, hgrn_norm, w_in_b, sinks,
              kv_norm, w_kv, w_out, norm_pre_mix, norm_post_mix, norm_pre_mlp, norm_post_mlp,
              w_up, w_down):
    Bp = x_prompt.shape[0]
    mk_list, mv_list = [], []
    for l in range(DEPTH):
        kv = rms_norm(mem_prompt, mem_norm[l]) @ w_mem_kv[l]
        mk_list.append(kv[..., :XATTN_WIDTH].reshape(Bp, N_MEM, XATTN_HEADS, XATTN_HEAD_DIM))
        mv_list.append(kv[..., XATTN_WIDTH:].reshape(Bp, N_MEM, XATTN_HEADS, XATTN_HEAD_DIM))
    mem_k_prompt = jnp.stack(mk_list)
    mem_v_prompt = jnp.stack(mv_list)

    hgrn0 = jnp.zeros((N_A_LAYERS, Bp, HGRN_HEADS, HGRN_HEAD_DIM, HGRN_HEAD_DIM), x_prompt.dtype)
    swa0 = jnp.zeros((Bp, WINDOW, SWA_KV_HEADS, SWA_HEAD_DIM), x_prompt.dtype)
    weights = (w_in_a, lb_logits, hgrn_norm, w_in_b, sinks, kv_norm, w_kv, w_out,
               norm_pre_mix, norm_post_mix, norm_pre_mlp, norm_post_mlp, w_up, w_down)

    y_prompt, hgrn_prompt, swa_k_prompt, swa_v_prompt = trunk(
        x_prompt, 0, mem_k_prompt, mem_v_prompt, hgrn0, swa0, swa0, *weights)
    y_sample, hgrn_sample, swa_k_sample, swa_v_sample = trunk(
        x_sample, PAST_LEN, cache_mem_k, cache_mem_v, state_hgrn, state_swa_k, state_swa_v, *weights)
    return (y_prompt, y_sample, mem_k_prompt, mem_v_prompt, hgrn_prompt, swa_k_prompt, swa_v_prompt,
            hgrn_sample, swa_k_sample, swa_v_sample)
```

```python
import math
from contextlib import ExitStack

import numpy as np
import concourse.bass as bass
import concourse.mybir as mybir
from concourse.bass_utils import run_bass_kernel_spmd

F32 = mybir.dt.float32
BF16 = mybir.dt.bfloat16
I32 = mybir.dt.int32
AF = mybir.ActivationFunctionType
ALU = mybir.AluOpType
AX = mybir.AxisListType

NCORES = 8
D = 1024
KC = 8
NTOK = 2048
NSMP = 16
TT = NTOK + NSMP
G = 512
NG = NTOK // G
DEPTH = 4
NA = 2
HH = 6
HW = 768
XW = 256
NMEM = 256
DFF = 4096
WIN = 128
EPS = 1e-6
PAST = 8192


class Buf:
    __slots__ = ("name", "w", "r")

    def __init__(self, name):
        self.name = name
        self.w = None
        self.r = {}


class Op:
    __slots__ = ("eng", "fn", "deps", "sig", "sigidx", "epoch", "is_dma", "sem", "semval", "tag")

    def __init__(self, eng, fn, epoch, is_dma=False, tag=""):
        self.eng = eng
        self.fn = fn
        self.deps = []
        self.sig = False
        self.sigidx = 0
        self.epoch = epoch
        self.is_dma = is_dma
        self.sem = None
        self.semval = 0
        self.tag = tag


class Prog:
    ENGS = ("pe", "act", "dve", "pool", "sp")

    def __init__(self, nc):
        self.nc = nc
        self.ops = []
        self.epoch = 0
        self.dma_count = {}
        self.dma_last = {}
        self.deferred = []

    def bufs(self, name, n):
        return [Buf(f"{name}{i}") for i in range(n)]

    def next_epoch(self):
        pass

    def _track(self, op, reads, writes):
        deps = {}

        def add(p, raw):
            if p is None or p is op:
                return
            if (not p.is_dma) and (not op.is_dma) and p.eng == op.eng:
                if p.eng == "pe" or not raw:
                    return
            deps[id(p)] = p

        for b in reads:
            add(b.w, True)
        for b in writes:
            add(b.w, op.eng != "pe")
            for r in b.r.values():
                if isinstance(r, list):
                    for x in r:
                        add(x, False)
                else:
                    add(r, False)
        for b in reads:
            if op.is_dma:
                b.r.setdefault("dma", []).append(op)
            else:
                b.r[op.eng] = op
        for b in writes:
            b.w = op
            b.r = {}
        for p in deps.values():
            p.sig = True
            op.deps.append(p)

    def op(self, eng, fn, reads=(), writes=(), tag=""):
        o = Op(eng, fn, self.epoch, tag=tag)
        self._track(o, reads, writes)
        self.ops.append(o)
        return o

    def dma(self, out, in_, reads=(), writes=(), sem="dma", tag="", eng="sp"):
        def fn(e, out=out, in_=in_):
            return e.dma_start(out=out, in_=in_)
        o = Op(eng, fn, self.epoch, is_dma=True, tag=tag)
        o.sem = sem
        self.dma_count[sem] = self.dma_count.get(sem, 0) + 16
        o.semval = self.dma_count[sem]
        self._track(o, reads, writes)
        prev = self.dma_last.get(sem)
        if prev is not None and all(prev is not d for d in o.deps):
            o.deps.append(prev)
        self.dma_last[sem] = o
        self.ops.append(o)
        return o

    def coll(self, fn, reads=(), writes=(), sem="cc"):
        o = Op("pool", fn, self.epoch, is_dma=True, tag="coll")
        o.sem = sem
        self.dma_count[sem] = self.dma_count.get(sem, 0) + 1
        o.semval = self.dma_count[sem]
        self._track(o, reads, writes)
        prev = self.dma_last.get(sem)
        if prev is not None and all(prev is not d for d in o.deps):
            o.deps.append(prev)
        self.dma_last[sem] = o
        self.ops.append(o)
        return o

    def defer(self, f):
        self.deferred.append(f)

    def flush(self):
        d, self.deferred = self.deferred, []
        for f in d:
            f()

    def emit(self, es):
        nc = self.nc
        self.flush()
        counters = {}
        for o in self.ops:
            if o.sig and not o.is_dma:
                k = (o.eng, o.epoch)
                counters[k] = counters.get(k, 0) + 1
                o.sigidx = counters[k]
                assert o.sigidx < 60000
        sems = {}

        def getsem(key):
            if key not in sems:
                sems[key] = es.enter_context(nc.semaphore(f"s_{key[0]}_{key[1]}" if isinstance(key, tuple) else f"d_{key}"))
            return sems[key]

        for k in counters:
            getsem(k)
        for k in self.dma_count:
            getsem(k)
        per = {e: [o for o in self.ops if o.eng == e] for e in self.ENGS}
        block = es.enter_context(nc.Block())
        final_waits = [(getsem(k), v) for k, v in self.dma_count.items()]

        def run(eng_name, e):
            waited = {}
            for o in per[eng_name]:
                for p in o.deps:
                    if p.is_dma:
                        key, val = p.sem, p.semval
                    else:
                        key, val = (p.eng, p.epoch), p.sigidx
                    if waited.get(key, 0) >= val:
                        continue
                    waited[key] = val
                    e.wait_ge(getsem(key), val)
                ins = o.fn(e)
                if o.is_dma:
                    ins.then_inc(getsem(o.sem), 16 if o.tag != "coll" else 1)
                elif o.sig:
                    ins.then_inc(getsem((o.eng, o.epoch)), 1)
            if eng_name == "sp":
                for s, v in final_waits:
                    e.wait_ge(s, v)

        @block.sync
        def _(e):
            run("sp", e)

        @block.tensor
        def _(e):
            run("pe", e)

        @block.scalar
        def _(e):
            run("act", e)

        @block.vector
        def _(e):
            run("dve", e)

        @block.gpsimd
        def _(e):
            run("pool", e)


class KB:
    def __init__(self, stop_after=None, dumps=()):
        self.stop_after = stop_after
        self.dumps = set(dumps)
        self.nc = bass.Bass("TRN2", target_bir_lowering=False)
        self.P = Prog(self.nc)
        self.es = ExitStack()
        self.slab_i = 0
        self.evac_i = 0
        self.dump_specs = {}
        self.declared = set()

    def dram_in(self, name, shape, dtype=F32):
        return self.nc.dram_tensor(name, list(shape), dtype, kind="ExternalInput").ap()

    def dram_out(self, name, shape, dtype=F32):
        return self.nc.dram_tensor(name, list(shape), dtype, kind="ExternalOutput").ap()

    def dram_tmp(self, name, shape, dtype=F32):
        return self.nc.dram_tensor(name, list(shape), dtype).ap()

    def sb(self, name, shape, dtype=F32):
        return self.es.enter_context(self.nc.sbuf_tensor(name, list(shape), dtype))

    def psum(self, name, shape, dtype=F32):
        return self.es.enter_context(self.nc.psum_tensor(name, list(shape), dtype))

    def dump(self, name, tile_ap, shape, bufs, dtype=F32):
        if name not in self.dumps:
            return
        o = self.dram_out("dbg_" + name, shape, dtype)
        self.P.dma(out=o, in_=tile_ap, reads=bufs, sem="dbg")
        self.dump_specs[name] = shape

    def evac_eng(self):
        self.evac_i += 1
        return "act" if (self.evac_i & 1) else "dve"

    def copy(self, eng, out, in_, reads, writes, tag=""):
        if eng == "act":
            return self.P.op("act", lambda e: e.activation(out=out, in_=in_, func=AF.Copy), reads, writes, tag)
        return self.P.op(eng, lambda e: e.tensor_copy(out=out, in_=in_), reads, writes, tag)

    IN_SHAPES = {
        "xp": [NTOK, D], "xs": [NSMP, D], "memp": [NMEM, D],
        "cmk": [DEPTH, NSMP, NMEM, XW], "cmv": [DEPTH, NSMP, NMEM, XW],
        "shg": [NA, NSMP, HH, 128, 128], "ssk": [NSMP, WIN, 256], "ssv": [NSMP, WIN, 256],
        "w_mem_kv": [DEPTH, D, 2 * XW], "w_in_a": [NA, D, 4 * HW + XW], "w_in_b": [DEPTH - NA, D, D],
        "w_kv": [D, 512], "w_out": [DEPTH, D, D], "w_up": [DEPTH, D, DFF], "w_down": [DEPTH, DFF, D],
        "vecA": [128, 128], "vecB": [64, 128], "sinks": [2, 12], "corec": [128, 16],
    }
    OUT_SHAPES = {
        "y_p": [NTOK, D], "y_s": [NSMP, D], "mkp": [DEPTH, NMEM, XW], "mvp": [DEPTH, NMEM, XW],
        "hgp": [NA, HH, 128, 128], "skp": [WIN, 256], "svp": [WIN, 256],
        "hgs": [NA, NSMP, HH, 128, 128], "sks": [NSMP, WIN, 256], "svs": [NSMP, WIN, 256],
    }

    def __getattr__(self, name):
        if name in KB.IN_SHAPES:
            ap = self.dram_in(name, KB.IN_SHAPES[name])
        elif name in KB.OUT_SHAPES:
            ap = self.dram_out(name, KB.OUT_SHAPES[name])
        else:
            raise AttributeError(name)
        self.__dict__[name] = ap
        self.declared.add(name)
        return ap

    def declare_io(self, full=True):
        if full:
            for n in list(KB.IN_SHAPES) + list(KB.OUT_SHAPES):
                getattr(self, n)

    def alloc_common(self):
        sb, P = self.sb, self.P
        self.hT = sb("hT", [128, KC, TT], F32)
        self.b_h = P.bufs("h", NG)
        self.b_hc = [P.bufs(f"hc{g}_", KC) for g in range(NG)]
        self.identf = sb("identf", [128, 128], F32)
        self.identb = sb("identb", [128, 128], BF16)
        self.onesb = sb("onesb", [128, 128], BF16)
        self.onesf = sb("onesf", [128, 1], F32)
        self.epsc = sb("epsc", [128, 1], F32)
        self.b_const = Buf("const")
        self.colA = sb("colA", [128, 128], F32)
        self.colB = sb("colB", [128, 64], F32)
        self.corc = sb("corc", [128, 16], F32)
        self.lbc = sb("lbc", [128, 2 * HH], F32)
        self.omlc = sb("omlc", [128, 2 * HH], F32)
        self.NST, self.NSB = 2, 3
        self.stage = [sb(f"stage{i}", [128, 8, 256], F32) for i in range(self.NST)]
        self.b_stage = P.bufs("stage", self.NST)
        self.slab = [sb(f"slab{i}", [128, 8, 256], BF16) for i in range(self.NSB)]
        self.b_slab = P.bufs("slab", self.NSB)
        self.psA = self.psum("psA", [128, 1024], F32)
        self.psB = self.psum("psB", [128, 1024], F32)
        self.psC = self.psum("psC", [128, 1024], F32)
        self.psD = self.psum("psD", [128, 1024], F32)
        self.psQ = [self.psC[:, 0:512], self.psC[:, 512:1024], self.psD[:, 0:512], self.psD[:, 512:1024]]
        self.bank = P.bufs("bank", 8)
        self.b_psA, self.b_psB = self.bank[0], self.bank[2]
        self.b_psQ = self.bank[4:8]
        self.mm_i = 0
        self.ARENA = 23040
        self.arena = sb("arena", [128, self.ARENA], F32)
        self.arena_off = 0
        self.phase_bufs = []
        self.barrier_op = None
        self.bdummy = sb("bdummy", [128, 4], F32)

    def hb(self, g):
        return [self.b_h[g]] + self.b_hc[g]

    def gcol(self, vec, l, c):
        j = vec * 32 + l * 8 + c
        return self.colA[:, j:j + 1]

    def memn_col(self, l, c):
        return self.colB[:, l * 8 + c:l * 8 + c + 1]

    def kvn_col(self, c):
        return self.colB[:, 32 + c:33 + c]

    def hgn_col(self, l, h):
        return self.colB[:, 40 + l * 6 + h:41 + l * 6 + h]

    def setup_consts(self):
        P, sb = self.P, self.sb
        bc = self.b_const
        idi = self.carve([256], F32).bitcast(I32)
        idf0 = self.carve([256], F32)
        b_t = self.pbuf("idtmp")
        P.op("pool", lambda e: e.iota(idi[:], [[1, 256]], 0, -1), [], [b_t])
        P.op("pool", lambda e: e.tensor_copy(out=idf0[:], in_=idi[:]), [b_t], [b_t])
        P.op("dve", lambda e: e.tensor_single_scalar(out=self.identf[:], in_=idf0[:, 0:128], scalar=0.0, op=ALU.is_equal), [b_t], [bc])
        P.op("dve", lambda e: e.tensor_copy(out=self.identb[:], in_=self.identf[:]), [bc], [bc])
        P.op("dve", lambda e: e.memset(self.onesb[:], 1.0), [], [bc])
        P.op("dve", lambda e: e.memset(self.onesf[:], 1.0), [], [bc])
        P.op("dve", lambda e: e.memset(self.epsc[:], EPS), [], [bc])
        self.hmask = sb("hmask", [128, 128], F32)
        P.op("dve", lambda e: e.tensor_single_scalar(out=self.hmask[:], in_=idf0[:, 0:128], scalar=0.0, op=ALU.is_ge), [b_t], [bc])
        P.op("dve", lambda e: e.memset(self.hmask[0:64, 64:128], 0.0), [bc], [bc])
        self.smask = sb("smask", [128, 256], F32)
        self.smask0 = sb("smask0", [128, 256], F32)
        t1 = self.carve([256], F32)
        P.op("dve", lambda e: e.tensor_single_scalar(out=t1[:], in_=idf0[:], scalar=1.0, op=ALU.is_ge), [b_t], [bc])
        P.op("dve", lambda e: e.tensor_single_scalar(out=self.smask[:], in_=idf0[:], scalar=128.0, op=ALU.is_le), [b_t, bc], [bc])
        P.op("dve", lambda e: e.tensor_tensor(out=self.smask[:], in0=self.smask[:], in1=t1[:], op=ALU.mult), [bc], [bc])
        P.op("dve", lambda e: e.tensor_scalar(out=self.smask[:], in0=self.smask[:], scalar1=30000.0, scalar2=-30000.0, op0=ALU.mult, op1=ALU.add), [bc], [bc])
        va = self.carve([128], F32)
        vb = self.carve([128], F32)[0:64, :]
        b_v = self.pbuf("vecs")
        P.dma(out=va[:], in_=self.vecA[:, :], writes=[b_v], sem="misc")
        P.dma(out=vb[:], in_=self.vecB[:, :], writes=[b_v], sem="misc")
        P.dma(out=self.corc[:], in_=self.corec[:, :], writes=[bc], sem="misc")
        ps = self.psQ[0]
        bq = self.b_psQ[0]
        P.op("pe", lambda e: e.transpose(ps[:, 0:128], va[:], self.identf[:]), [b_v, bc], [bq])
        P.op("pe", lambda e: e.transpose(ps[:, 128:192], vb[:], self.identf[0:64, 0:64]), [b_v, bc], [bq])
        P.op("dve", lambda e: e.tensor_copy(out=self.colA[:], in_=ps[:, 0:128]), [], [bq, bc])
        P.op("dve", lambda e: e.tensor_copy(out=self.colB[:], in_=ps[:, 128:192]), [], [bq, bc])
        d = self.carve([HH], F32)
        P.op("dve", lambda e: e.tensor_tensor(out=d[:], in0=self.colB[:, 58:64], in1=self.colB[:, 52:58], op=ALU.subtract), [bc], [b_t])
        P.op("dve", lambda e: e.memset(self.lbc[:, 0:HH], 0.0), [], [bc])
        P.op("act", lambda e: e.activation(out=self.lbc[:, HH:2 * HH], in_=d[:], func=AF.Sigmoid), [b_t, bc], [bc])
        P.op("dve", lambda e: e.tensor_scalar(out=self.omlc[:], in0=self.lbc[:], scalar1=-1.0, scalar2=1.0, op0=ALU.mult, op1=ALU.add), [bc], [bc])
        P.op("dve", lambda e: e.memset(t1[:, 0:128], -30000.0), [bc], [bc])
        P.op("dve", lambda e: e.memset(t1[:, 128:256], 0.0), [bc], [bc])
        P.op("dve", lambda e: e.scalar_tensor_tensor(out=self.smask0[:], in0=t1[:], scalar=self.corc[:, 9:10], in1=self.smask[:], op0=ALU.mult, op1=ALU.add), [bc], [bc])

    def load_transpose_tokens(self, src, ntok, dstT, dcol0, b_dst, name):
        P = self.P
        nt = (ntok + 127) // 128
        if not hasattr(self, "xin"):
            self.xin = [self.carve([D], F32) for i in range(2)]
            self.b_xin = self.pbufs("xin", 2)
            self.xin_i = 0
        for t in range(nt):
            rows = min(128, ntok - t * 128)
            i = self.xin_i % 2
            self.xin_i += 1
            xt, bx = self.xin[i], self.b_xin[i]
            P.dma(out=xt[0:rows, :], in_=src[t * 128:t * 128 + rows, :], writes=[bx], sem=f"xin{i}")
            for half in range(2):
                ps, bp = (self.psA, self.bank[0]) if half == 0 else (self.psB, self.bank[2])
                for j in range(4):
                    c = half * 4 + j
                    P.op("pe", lambda e, ps=ps, j=j, c=c, xt=xt, rows=rows: e.transpose(
                        ps[:, j * 128:j * 128 + rows], xt[0:rows, c * 128:(c + 1) * 128], self.identf[0:rows, 0:rows]),
                        [bx, self.b_const], [bp])
                eng = "act" if half == 0 else "dve"
                out = dstT[:, half * 4:half * 4 + 4, dcol0 + t * 128:dcol0 + t * 128 + rows]
                in_ = ps[:, 0:512].rearrange("p (j r) -> p j r", r=128)[:, :, 0:rows]
                self.copy(eng, out, in_, [], [bp, b_dst(t) if callable(b_dst) else b_dst])

    def get_slab(self, W2d, r0, c0, ncols=256, swap=False):
        P = self.P
        i = self.slab_i
        self.slab_i += 1
        si = i % self.NST
        st, bs = self.stage[si], self.b_stage[si]
        src = W2d[r0:r0 + 1024, c0:c0 + ncols].rearrange("(kc p) n -> p kc n", p=128)
        P.dma(out=st[:, :, 0:ncols], in_=src, writes=[bs], sem=f"stage{si}")
        outs = []
        for variant in ([False, True] if swap else [False]):
            j = getattr(self, "slabb_i", 0)
            self.slabb_i = j + 1
            sl, bl = self.slab[j % self.NSB], self.b_slab[j % self.NSB]
            if not variant:
                h = ncols // 2
                P.op("act", lambda e, sl=sl, st=st, h=h: e.activation(out=sl[:, :, 0:h], in_=st[:, :, 0:h], func=AF.Copy), [bs], [bl])
                P.op("dve", lambda e, sl=sl, st=st, h=h: e.tensor_copy(out=sl[:, :, h:ncols], in_=st[:, :, h:ncols]), [bs], [bl])
            else:
                nh = ncols // 64
                so = sl[:, :, 0:ncols].rearrange("p k (h t f) -> p k h t f", t=2, f=32)
                si_ = st[:, :, 0:ncols].rearrange("p k (h t f) -> p k h t f", t=2, f=32)
                P.op("dve", lambda e, so=so, si_=si_: e.tensor_copy(out=so[:, :, :, 0, :], in_=si_[:, :, :, 1, :]), [bs], [bl])
                P.op("act", lambda e, so=so, si_=si_: e.activation(out=so[:, :, :, 1, :], in_=si_[:, :, :, 0, :], func=AF.Copy), [bs], [bl])
            outs.append((sl, bl))
        return outs if swap else outs[0]

    def acc_tile(self):
        self.mm_i += 1
        return (self.psA, self.bank[0:2]) if (self.mm_i & 1) else (self.psB, self.bank[2:4])

    def phase_begin(self):
        self.arena_off = 0
        self.phase_bufs = []

    def carve(self, shape, dtype=F32):
        n = 1
        for d in shape:
            n *= d
        words = (n * (2 if dtype == BF16 else 4) + 3) // 4
        words = (words + 7) // 8 * 8
        assert self.arena_off + words <= self.ARENA, ("arena overflow", self.arena_off, words)
        ap = self.arena[:, self.arena_off:self.arena_off + words]
        self.arena_off += words
        if dtype == BF16:
            ap = ap.bitcast(BF16)
        ap = ap[:, 0:n]
        if len(shape) == 2:
            ap = ap.rearrange("p (a b) -> p a b", b=shape[1])
        elif len(shape) == 3:
            ap = ap.rearrange("p (a b c) -> p a b c", b=shape[1], c=shape[2])
        return ap

    def pbuf(self, name):
        b = Buf(name)
        b.w = self.barrier_op
        self.phase_bufs.append(b)
        return b

    def pbufs(self, name, n):
        return [self.pbuf(f"{name}{i}") for i in range(n)]

    def phase_end(self):
        bufs = list(self.phase_bufs)
        self.barrier_op = self.P.op("dve", lambda e: e.memset(self.bdummy[:], 0.0), [], bufs, tag="barrier")
        self.phase_bufs = []

    def linear_fm(self, W2d, d_in, col0, nblocks, xT, b_x, ncols, evac, swap=False):
        P = self.P
        kq_n = d_in // 1024
        cols = [(0, min(ncols, 512))] + ([(512, ncols)] if ncols > 512 else [])

        def mms(ps, bp, sl, bl, blk, kq):
            for kc in range(8):
                first = (kq == 0 and kc == 0)
                last = (kq == kq_n - 1 and kc == 7)
                for (a, b) in cols:
                    P.op("pe", lambda e, ps=ps, sl=sl, kc=kc, blk=blk, a=a, b=b, kk=kq * 8 + kc, first=first, last=last:
                         e.matmul(ps[:, a:b], lhsT=sl[:, kc, blk * 128:(blk + 1) * 128], rhs=xT[:, kk, a:b],
                                  start=first, stop=last), [bl] + list(b_x), list(bp))

        reqs = []
        npairs = (nblocks + 1) // 2
        for pair in range(npairs):
            nb_here = min(2, nblocks - pair * 2)
            for kq in range(kq_n):
                reqs.append((pair, kq, nb_here))
        got = {}

        def fetch(i):
            if i < len(reqs) and i not in got:
                pair, kq, nb_here = reqs[i]
                got[i] = self.get_slab(W2d, kq * 1024, col0 + pair * 256, nb_here * 128, swap=swap)
        fetch(0)
        tiles = [(self.psA, self.bank[0:2]), (self.psB, self.bank[2:4])]
        for i, (pair, kq, nb_here) in enumerate(reqs):
            cur = got.pop(i)
            if self.NSB >= 3 and not swap:
                fetch(i + 1)
            if kq_n == 1:
                variants = cur if swap else [cur]
                for vi, (sl, bl) in enumerate(variants):
                    for blk in range(nb_here):
                        ps, bp = self.acc_tile()
                        mms(ps, bp, sl, bl, blk, 0)
                        evac(pair * 2 + blk, ps, bp, vi)
                if swap:
                    fetch(i + 1)
            else:
                sl, bl = cur
                for blk in range(nb_here):
                    mms(tiles[blk][0], tiles[blk][1], sl, bl, blk, kq)
                if kq == kq_n - 1:
                    for blk in range(nb_here):
                        evac(pair * 2 + blk, tiles[blk][0], tiles[blk][1], 0)

    def norm_stats(self, srcT, C, ncols, inv_n, b_src, rstd, b_rstd, scratch=None):
        P = self.P
        if scratch is not None:
            sq, sqt, b_sqs = scratch
        else:
            if self.__dict__.get("sq_phase") is not self.phase_bufs:
                self.sq_phase = self.phase_bufs
                self.sqscr = self.carve([KC, 528], BF16)
                self.b_sq = self.pbuf("sq")
                self.sqt_ = self.carve([528], F32)
            sq, sqt, b_sqs = self.sqscr, self.sqt_, [self.b_sq]
        ps, bps = self.psD, [self.b_psQ[2], self.b_psQ[3]]
        cols = [(0, min(ncols, 512))] + ([(512, ncols)] if ncols > 512 else [])
        b_ch = [Buf(f"sqc{c}") for c in range(C)]
        for c in range(C):
            claim = list(b_sqs) if c < 2 else []
            if c % 2 == 0:
                P.op("act", lambda e, c=c: e.activation(out=sq[:, c, 0:ncols], in_=srcT[:, c, :], func=AF.Square), list(b_src), claim + [b_ch[c]])
            else:
                P.op("dve", lambda e, c=c: e.tensor_tensor(out=sq[:, c, 0:ncols], in0=srcT[:, c, :], in1=srcT[:, c, :], op=ALU.mult), list(b_src), claim + [b_ch[c]])
            for (a_, b_) in cols:
                P.op("pe", lambda e, c=c, a_=a_, b_=b_: e.matmul(ps[:, a_:b_], lhsT=self.onesb[:], rhs=sq[:, c, a_:b_], start=(c == 0), stop=(c == C - 1)),
                     [b_ch[c], self.b_const], bps)
        P.op("act", lambda e: e.activation(out=sqt[:, 0:ncols], in_=ps[:, 0:ncols], func=AF.Sqrt, bias=self.epsc[:, 0:1], scale=inv_n), [self.b_const], bps + list(b_sqs))
        P.op("dve", lambda e: e.reciprocal(out=rstd[:, 0:ncols], in_=sqt[:, 0:ncols]), list(b_sqs), [b_rstd])

    def apply_norm(self, srcT_fn, C, ncols, rstd, b_rstd, gain_fn, out_fn, b_src, b_out):
        P = self.P
        for c in range(C):
            P.op("dve", lambda e, c=c: e.scalar_tensor_tensor(out=out_fn(c), in0=srcT_fn(c), scalar=gain_fn(c), in1=rstd[:, 0:ncols],
                                                              op0=ALU.mult, op1=ALU.mult), list(b_src) + [b_rstd, self.b_const], [b_out[c] if isinstance(b_out, list) else b_out])

    def mem_kv(self):
        P, sb = self.P, self.sb
        memT = self.carve([KC, NMEM], F32)
        b_mem = self.pbuf("memT")
        self.load_transpose_tokens(self.memp, NMEM, memT, 0, b_mem, "mem")
        rstd = self.carve([NMEM], F32)
        b_r = self.pbuf("mrstd")
        self.norm_stats(memT[:, :, :], KC, NMEM, 1.0 / D, [b_mem], rstd, b_r)
        import os
        cut = 0
        if cut == 1:
            self.dump("rstd", rstd[:], [128, NMEM], [b_r])
            return
        mnT = self.carve([KC, NMEM], BF16)
        b_mn = self.pbuf("mnT")
        self.mkT = sb("mkT", [128, DEPTH, 2, NMEM], BF16)
        self.mvb = sb("mvb", [128, DEPTH, 2, XW], BF16)
        self.b_mkv = Buf("mkv")
        stg = [self.carve([256], F32) for i in range(2)]
        b_stg = self.pbufs("mstg", 2)
        si = 0
        for l in range(DEPTH):
            self.apply_norm(lambda c: memT[:, c, :], KC, NMEM, rstd, b_r, lambda c, l=l: self.memn_col(l, c),
                            lambda c: mnT[:, c, :], [b_mem], b_mn)
            if cut == 2:
                self.dump("mnT", mnT[:], [128, KC, NMEM], [b_mn], BF16)
                return
            for part in range(2):
                sl, bl = self.get_slab(self.w_mem_kv[l], 0, part * 256, 256)
                if cut == 3:
                    self.dump("slab", sl[:], [128, 8, 256], [bl], BF16)
                    return
                for mt in range(2):
                    ps, bp = self.psQ[mt], self.b_psQ[mt]
                    for kc in range(8):
                        P.op("pe", lambda e, ps=ps, kc=kc, mt=mt, sl=sl: e.matmul(ps[:, 0:256], lhsT=mnT[:, kc, mt * 128:(mt + 1) * 128],
                                                                                   rhs=sl[:, kc, :], start=(kc == 0), stop=(kc == 7)), [b_mn, bl], [bp])
                    st, bs = stg[si % 2], b_stg[si % 2]
                    si += 1
                    if cut == 4:
                        return
                    self.copy("act", st[:], ps[:, 0:256], [], [bp, bs])
                    if cut == 5:
                        return
                    dst = (self.mkp if part == 0 else self.mvp)[l, mt * 128:(mt + 1) * 128, :]
                    P.dma(out=dst, in_=st[:], reads=[bs], sem=f"mstg{(si - 1) % 2}")
                    if cut == 6:
                        return
                    if part == 1:
                        self.copy("dve", self.mvb[:, l, mt, :], st[:], [bs], [self.b_mkv])
                if part == 0:
                    for blk in range(2):
                        ps, bp = self.psQ[2 + blk], self.b_psQ[2 + blk]
                        for kc in range(8):
                            P.op("pe", lambda e, ps=ps, kc=kc, blk=blk, sl=sl: e.matmul(ps[:, 0:256], lhsT=sl[:, kc, blk * 128:(blk + 1) * 128],
                                                                                         rhs=mnT[:, kc, :], start=(kc == 0), stop=(kc == 7)), [b_mn, bl], [bp])
                        self.copy("dve", self.mkT[:, l, blk, :], ps[:, 0:256], [], [bp, self.b_mkv])
                    if cut == 7:
                        return
            if cut == 8 + l:
                return

    def alloc_a_state(self):
        sb, P = self.sb, self.P
        self.Sf = sb("Sf", [128, HH, 128], F32)
        self.Sb = sb("Sb", [128, HH, 4, 128], BF16)
        self.b_Sf = P.bufs("Sf", HH)
        self.b_Sb = [P.bufs(f"Sb{h}_", 4) for h in range(HH)]
        self.Gacc = sb("Gacc", [128, HH], F32)
        self.b_G = P.bufs("G", HH)
        self.Sst_b = sb("Sst_b", [128, HH, 128], BF16)
        self.b_Sst = Buf("Sst")
        self.spill_q = [self.dram_tmp(f"spq{g}", [128, HH, 512], BF16) for g in range(NG)]
        self.spill_o = [self.dram_tmp(f"spo{g}", [128, HH, 528], BF16) for g in range(NG)]
        self.b_spq = P.bufs("spq", NG)
        self.b_spo = P.bufs("spo", NG)

    def norm_group(self, g, ncols, vec, l, scratch=None, gain_fn=None):
        c0 = g * G
        rstd = self.carve([528], F32)
        b_r = self.pbuf("rstd")
        hnT = self.carve([KC, 528], BF16)
        b_hn = self.pbufs("hn", KC)
        self.norm_stats(self.hT[:, :, c0:c0 + ncols], KC, ncols, 1.0 / D, self.hb(g), rstd, b_r, scratch)
        self.apply_norm(lambda c: self.hT[:, c, c0:c0 + ncols], KC, ncols, rstd, b_r, gain_fn or (lambda c: self.gcol(vec, l, c)),
                        lambda c: hnT[:, c, 0:ncols], self.hb(g), b_hn)
        return hnT, b_hn

    def a_phase1(self, l, g):
        P = self.P
        last = (g == NG - 1)
        ncols = 528 if last else 512
        W = self.w_in_a[l]
        self.phase_begin()
        prod = self.carve([4, HH, 512], BF16)
        qe, ke, kd, qs = prod[:, 0], prod[:, 1], prod[:, 2], prod[:, 3]
        b_qe, b_ke, b_kd, b_qs = (self.pbufs(n, HH) for n in ("qe", "ke", "kd", "qs"))
        flat = prod.rearrange("p a h t -> p (a h t)")
        sq_scr = flat[:, 0:KC * 528].rearrange("p (c t) -> p c t", t=528)
        sqt_scr = flat[:, 2 * HH * 512:2 * HH * 512 + 1056].bitcast(F32)
        hnT, b_hn = self.norm_group(g, ncols, 0, l, scratch=(sq_scr, sqt_scr, b_qe + b_ke + b_kd))
        qT = self.carve([HH, 528], BF16)
        b_q = self.pbufs("q", HH)
        kkT = self.carve([HH, 528], BF16)
        b_kk = self.pbufs("kk", HH)
        Bc = self.carve([HH, 512], F32)
        b_Bc = self.pbufs("Bc", HH)
        Vtok = self.carve([4, HW], BF16)
        b_V = self.pbufs("V", 4)
        scr_all = self.carve([4, 528], F32)
        scr = [scr_all[:, i, :] for i in range(4)]
        b_scr = self.pbufs("scr", 4)
        lb = lambda h: self.lbc[:, l * HH + h:l * HH + h + 1]
        oml = lambda h: self.omlc[:, l * HH + h:l * HH + h + 1]

        pc = 512
        def evac_q(nb, ps, bp, vi):
            P.op("act", lambda e: e.activation(out=qT[:, nb, 0:pc], in_=ps[:, 0:pc], func=AF.Silu), [], list(bp) + [b_q[nb]])
        self.linear_fm(W, D, 0, HH, hnT, b_hn, pc, evac_q)
        import os
        cut = 0

        def docut(k):
            if cut == k:
                self.phase_end()
                self.cut_hit = True
                return True
            return False
        if docut(1):
            return

        zi = [0]

        def evac_z(nb, ps, bp, vi):
            i = zi[0] % 2
            zi[0] += 1
            sA, bA = scr[2 * i], b_scr[2 * i]
            sB, bB = scr[2 * i + 1], b_scr[2 * i + 1]
            P.op("act", lambda e: e.activation(out=sA[:, 0:pc], in_=ps[:, 0:pc], func=AF.Sigmoid), [], list(bp) + [bA])
            P.op("dve", lambda e: e.tensor_scalar(out=sB[:, 0:pc], in0=sA[:, 0:pc], scalar1=oml(nb), scalar2=lb(nb), op0=ALU.mult, op1=ALU.add),
                 [bA, self.b_const], [bB])
            P.op("pool", lambda e: e.tensor_scalar(out=kkT[:, nb, 0:pc], in0=sB[:, 0:pc], scalar1=-1.0, scalar2=1.0, op0=ALU.mult, op1=ALU.add), [bB], [b_kk[nb]])
            P.op("act", lambda e: e.activation(out=sA[:, 0:512], in_=sB[:, 0:512], func=AF.Ln), [bB], [bA])
            P.op("dve", lambda e: e.tensor_tensor_scan(out=Bc[:, nb, :], data0=self.onesf[:, 0:1].to_broadcast([128, 512]), data1=sA[:, 0:512], initial=0.0,
                                                       op0=ALU.mult, op1=ALU.add), [bA, self.b_const], [b_Bc[nb]])
        self.linear_fm(W, D, HW, HH, hnT, b_hn, pc, evac_z)
        if docut(2):
            return

        ntile = 4
        for sidx in range(3):
            sl, bl = self.get_slab(W, 0, 2 * HW + sidx * 256, 256)
            for t in range(ntile):
                rows = 128
                tc0 = t * 128
                q4 = (sidx * 5 + t) % 2
                ps, bp = self.psQ[q4], self.b_psQ[q4]
                for kc in range(8):
                    P.op("pe", lambda e, ps=ps, kc=kc, rows=rows, tc0=tc0, sl=sl: e.matmul(ps[0:rows, 0:256], lhsT=hnT[:, kc, tc0:tc0 + rows], rhs=sl[:, kc, :],
                                                                                         start=(kc == 0), stop=(kc == 7)), [bl] + b_hn, [bp])
                self.copy(self.evac_eng(), Vtok[:, t, sidx * 256:(sidx + 1) * 256], ps[:, 0:256], [], [bp, b_V[t]])

        if docut(3):
            return
        qsD = self.carve([2, 512], BF16)
        b_qsD = self.pbufs("qsD", 2)
        ex = [self.carve([512], BF16) for _ in range(4)]
        b_ex = self.pbufs("ex", 4)
        el = self.carve([HH, 8], F32)
        b_el = self.pbufs("el", HH)
        Bs = self.carve([HH, 8], F32)
        exi = [0]

        def nex():
            i = exi[0] % 4
            exi[0] += 1
            return ex[i], b_ex[i]

        def nscr():
            i = zi[0] % 4
            zi[0] += 1
            return scr[i], b_scr[i]

        for h in range(HH):
            B3 = Bc[:, h, :].rearrange("p (c j) -> p c j", j=64)
            v3 = lambda ap: ap[:, 0:512].rearrange("p (c j) -> p c j", j=64)
            d1, bd1 = nscr()
            P.op("dve", lambda e, d1=d1, B3=B3: e.tensor_tensor(out=v3(d1), in0=B3, in1=B3[:, :, 31:32].to_broadcast([128, 8, 64]), op=ALU.subtract), [b_Bc[h]], [bd1])
            e1, be1 = nex()
            P.op("act", lambda e, e1=e1, d1=d1: e.activation(out=e1[:, :], in_=d1[:, 0:512], func=AF.Exp), [bd1], [be1])
            P.op("pool", lambda e, e1=e1, h=h: e.tensor_tensor(out=qe[:, h, :], in0=qT[:, h, 0:512], in1=e1[:, :], op=ALU.mult), [be1, b_q[h]], [b_qe[h]])
            e2, be2 = nex()
            P.op("act", lambda e, e2=e2, d1=d1: e.activation(out=e2[:, :], in_=d1[:, 0:512], func=AF.Exp, scale=-1.0), [bd1], [be2])
            P.op("pool", lambda e, e2=e2, h=h: e.tensor_tensor(out=ke[:, h, :], in0=kkT[:, h, 0:512], in1=e2[:, :], op=ALU.mult), [be2, b_kk[h]], [b_ke[h]])
            d2, bd2 = nscr()
            P.op("dve", lambda e, d2=d2, B3=B3: e.tensor_tensor(out=v3(d2), in0=B3, in1=B3[:, :, 63:64].to_broadcast([128, 8, 64]), op=ALU.subtract), [b_Bc[h]], [bd2])
            e3, be3 = nex()
            P.op("act", lambda e, e3=e3, d2=d2: e.activation(out=e3[:, :], in_=d2[:, 0:512], func=AF.Exp, scale=-1.0), [bd2], [be3])
            P.op("pool", lambda e, e3=e3, h=h: e.tensor_tensor(out=kd[:, h, :], in0=kkT[:, h, 0:512], in1=e3[:, :], op=ALU.mult), [be3, b_kk[h]], [b_kd[h]])
            P.op("dve", lambda e, h=h: e.memset(Bs[:, h, 0:1], 0.0), [], [b_el[h]])
            P.op("dve", lambda e, h=h, B3=B3: e.tensor_copy(out=Bs[:, h, 1:8], in_=B3[:, 0:7, 63]), [b_Bc[h]], [b_el[h]])
            d3, bd3 = nscr()
            P.op("dve", lambda e, d3=d3, B3=B3, h=h: e.tensor_tensor(out=v3(d3), in0=B3, in1=Bs[:, h, :].unsqueeze(2).to_broadcast([128, 8, 64]), op=ALU.subtract),
                 [b_Bc[h], b_el[h]], [bd3])
            e4, be4 = nex()
            P.op("act", lambda e, e4=e4, d3=d3: e.activation(out=e4[:, :], in_=d3[:, 0:512], func=AF.Exp), [bd3], [be4])
            P.op("pool", lambda e, e4=e4, h=h: e.tensor_tensor(out=qs[:, h, :], in0=qT[:, h, 0:512], in1=e4[:, :], op=ALU.mult), [be4, b_q[h]], [b_qs[h]])
            P.op("dve", lambda e, h=h, B3=B3: e.tensor_tensor(out=el[:, h, :], in0=B3[:, :, 63], in1=Bs[:, h, :], op=ALU.subtract), [b_Bc[h], b_el[h]], [b_el[h]])
            P.op("act", lambda e, h=h: e.activation(out=el[:, h, :], in_=el[:, h, :], func=AF.Exp), [b_el[h]], [b_el[h]])
            e5, be5 = nex()
            P.op("act", lambda e, e5=e5, h=h: e.activation(out=e5[:, :], in_=Bc[:, h, :], func=AF.Exp, bias=self.Gacc[:, h:h + 1]), [b_Bc[h], self.b_G[h]], [be5])
            P.op("pool", lambda e, e5=e5, h=h: e.tensor_tensor(out=qsD[:, h % 2, :], in0=qT[:, h, 0:512], in1=e5[:, :], op=ALU.mult), [be5, b_q[h]], [b_qsD[h % 2]])
            P.dma(out=self.spill_q[g][:, h, :], in_=qsD[:, h % 2, :], reads=[b_qsD[h % 2]], writes=[self.b_spq[g]], sem=f"spq{h % 2}")
            P.op("dve", lambda e, h=h: e.tensor_tensor(out=self.Gacc[:, h:h + 1], in0=self.Gacc[:, h:h + 1], in1=Bc[:, h, 511:512], op=ALU.add), [b_Bc[h], self.b_G[h]], [self.b_G[h]])

        if docut(4):
            return
        oloc = hnT[:, 0:HH, :]
        b_ol = b_hn[0:HH]
        NR = 4 * 3
        A_t = [self.carve([2, 128], BF16) for _ in range(3)]
        b_A = self.pbufs("A", 3)
        kdt = [self.carve([2, 128], BF16) for _ in range(2)]
        b_kdt = self.pbufs("kdt", 2)
        ps_sc, b_sc = self.psQ[0], self.b_psQ[0]
        ps_o, b_o = self.psQ[1], self.b_psQ[1]
        ps_dsc = [self.psQ[2], self.psQ[3]]
        b_dsc = [self.b_psQ[2], self.b_psQ[3]]
        ps_tr, b_tr = self.psA[:, 0:512].bitcast(BF16), self.bank[0]

        def stage1(r):
            t, hp = divmod(r, 3)
            i = r % 2
            for j in range(2):
                h = hp * 2 + j
                tc = slice(t * 128, (t + 1) * 128)
                P.op("pe", lambda e, j=j, h=h, tc=tc: e.matmul(ps_sc[:, j * 128:(j + 1) * 128], lhsT=ke[:, h, tc], rhs=qe[:, h, tc], start=True, stop=True),
                     [b_ke[h], b_qe[h]], [b_sc])
                P.op("pe", lambda e, j=j, h=h, tc=tc: e.transpose(ps_tr[:, j * 128:(j + 1) * 128], kd[:, h, tc], self.identb[:]), [b_kd[h], self.b_const], [b_tr])
            ia = r % 3
            P.op("dve", lambda e, ia=ia: e.tensor_tensor(out=A_t[ia][:, :, :], in0=ps_sc[:, 0:256].rearrange("p (j t) -> p j t", t=128),
                                                         in1=self.hmask[:, :].unsqueeze(1).to_broadcast([128, 2, 128]), op=ALU.mult), [self.b_const], [b_sc, b_A[ia]])
            self.copy("act", kdt[i][:, :, :], ps_tr[:, 0:256].rearrange("p (j t) -> p j t", t=128), [], [b_tr, b_kdt[i]])

        def stage2(r):
            t, hp = divmod(r, 3)
            i = r % 2
            for j in range(2):
                h = hp * 2 + j
                for c in range(2):
                    rs = slice(c * 64, (c + 1) * 64)
                    P.op("pe", lambda e, j=j, h=h, c=c, rs=rs, t=t, i=i: e.matmul(ps_dsc[c][:, j * 128:(j + 1) * 128], lhsT=kdt[i][rs, j, :],
                                                                               rhs=Vtok[rs, t, h * 128:(h + 1) * 128], start=True, stop=True),
                         [b_kdt[i], b_V[t]], [b_dsc[c]])
            for j in range(2):
                h = hp * 2 + j
                for c in range(2):
                    n = g * 8 + t * 2 + c
                    slot = (n + 1) % 4
                    P.op("dve", lambda e, j=j, h=h, c=c, t=t: e.scalar_tensor_tensor(out=self.Sf[:, h, :], in0=self.Sf[:, h, :], scalar=el[:, h, 2 * t + c:2 * t + c + 1],
                                                                                  in1=ps_dsc[c][:, j * 128:(j + 1) * 128], op0=ALU.mult, op1=ALU.add),
                         [b_el[h]], [b_dsc[c], self.b_Sf[h]])
                    self.copy("act", self.Sb[:, h, slot, :], self.Sf[:, h, :], [self.b_Sf[h]], [self.b_Sb[h][slot]])

        def stage3(r):
            t, hp = divmod(r, 3)
            i = r % 2
            for j in range(2):
                h = hp * 2 + j
                n0 = g * 8 + t * 2
                o_ = ps_o[:, j * 128:(j + 1) * 128]
                ia = r % 3
                P.op("pe", lambda e, o_=o_, h=h, t=t, ia=ia, j=j: e.matmul(o_, lhsT=Vtok[:, t, h * 128:(h + 1) * 128], rhs=A_t[ia][:, j, :], start=True, stop=False),
                     [b_V[t], b_A[ia]], [b_o])
                for c in range(2):
                    slot = (n0 + c) % 4
                    P.op("pe", lambda e, o_=o_, h=h, t=t, c=c, slot=slot: e.matmul(o_[:, c * 64:(c + 1) * 64], lhsT=self.Sb[:, h, slot, :],
                                                                                  rhs=qs[:, h, t * 128 + c * 64:t * 128 + (c + 1) * 64], start=False, stop=(c == 1)),
                         [self.b_Sb[h][slot], b_qs[h]], [b_o])
            hp2 = hp * 2
            self.copy("act" if r % 2 else "dve", oloc[:, hp2:hp2 + 2, t * 128:(t + 1) * 128], ps_o[:, 0:256].rearrange("p (j t) -> p j t", t=128), [], [b_o, b_ol[hp2], b_ol[hp2 + 1]])

        for r in range(NR + 2):
            if r < NR:
                stage1(r)
            if 0 <= r - 1 < NR:
                stage2(r - 1)
            if 0 <= r - 2 < NR:
                stage3(r - 2)
        self.a_ctx = dict(oloc=oloc, b_ol=b_ol)
        if last:
            P.op("pool", lambda e: e.tensor_copy(out=oloc[:, :, 512:528], in_=self.dec_o[:, :, :]), [self.b_deco], b_ol)
        P.dma(out=self.spill_o[g][:, :, 0:ncols], in_=oloc[:, :, 0:ncols], reads=b_ol, writes=[self.b_spo[g]], sem="spo")
        if self.stop_after == f"a1_{l}_{g}":
            self.dump("oloc", oloc[:, :, 0:ncols], [128, HH, ncols], b_ol, BF16)
            self.dump("Sf", self.Sf[:, :, :], [128, HH, 128], self.b_Sf)
            self.dump("Bc", Bc[:, :, :], [128, HH, 512], b_Bc)
            self.dump("qT", qT[:, :, 0:ncols], [128, HH, ncols], b_q, BF16)
            self.dump("Vtok", Vtok[:, :, :], [128, 4, HW], b_V, BF16)
        self.phase_end()

    def attn_pair(self, q_ap, b_q, K_fn, b_k, V_fn, b_v, out_ap, b_out, mask=None, nsink=None, sinkc=None, scale=0.125):
        self.att_jobs.append(dict(q=q_ap, bq=list(b_q), K=K_fn, bk=list(b_k), V=V_fn, bv=list(b_v), out=out_ap, bout=list(b_out),
                                  mask=mask, nsink=nsink, sinkc=sinkc, scale=scale))

    def attn_run(self):
        P = self.P
        jobs, self.att_jobs = self.att_jobs, []
        ps_s = [self.psQ[0], self.psQ[1]]
        b_s = [self.b_psQ[0], self.b_psQ[1]]
        ps_t, b_t = self.psQ[2].bitcast(BF16), self.b_psQ[2]
        ps_o, b_o = self.psQ[3], self.b_psQ[3]

        def st1a(i):
            J, A = jobs[i], self.att[i % 2]
            b, scale = A["b"], J["scale"]
            for j in range(2):
                P.op("pe", lambda e, j=j: e.matmul(ps_s[j][:, 0:256], lhsT=J["q"][j * 64:(j + 1) * 64, :], rhs=J["K"](j), start=True, stop=True),
                     J["bq"] + J["bk"], [b_s[j]])
            ps2 = self.psC[:, :].rearrange("p (j t) -> p j t", t=512)[:, :, 0:256]
            if J["mask"] is not None:
                P.op("dve", lambda e: e.tensor_tensor(out=A["s"][:, :, :], in0=ps2, in1=J["mask"].unsqueeze(1).to_broadcast([128, 2, 256]), op=ALU.add),
                     [self.b_const], [b_s[0], b_s[1], A["bs"][0], A["bs"][1]])
            else:
                P.op("dve", lambda e: e.tensor_copy(out=A["s"][:, :, :], in_=ps2), [], [b_s[0], b_s[1], A["bs"][0], A["bs"][1]])
            P.op("dve", lambda e: e.reduce_max(out=A["mx"][:, 0:2], in_=A["s"][:, :, :], axis=AX.X), [A["bs"][0], A["bs"][1]], [A["bst"][0], A["bst"][1]])
            for j in range(2):
                if J["nsink"] is not None:
                    P.op("dve", lambda e, j=j: e.tensor_scalar(out=A["nm"][:, j:j + 1], in0=A["mx"][:, j:j + 1], scalar1=-scale, scalar2=J["nsink"][j], op0=ALU.mult, op1=ALU.min),
                         [self.b_const], [A["bst"][j]])
                else:
                    P.op("dve", lambda e, j=j: e.tensor_scalar(out=A["nm"][:, j:j + 1], in0=A["mx"][:, j:j + 1], scalar1=-scale, scalar2=None, op0=ALU.mult), [], [A["bst"][j]])

        def st1b(i):
            J, A = jobs[i], self.att[i % 2]
            scale = J["scale"]
            for j in range(2):
                P.op("act", lambda e, j=j: e.activation(out=A["s"][:, j, :], in_=A["s"][:, j, :], func=AF.Exp, bias=A["nm"][:, j:j + 1], scale=scale, accum_out=A["rs"][:, j:j + 1]),
                     [A["bst"][j]], [A["bs"][j], A["brs"][j]])
                if J["nsink"] is not None:
                    P.op("act", lambda e, j=j: e.activation(out=A["es"][:, j:j + 1], in_=A["nm"][:, j:j + 1], func=AF.Exp, bias=J["sinkc"][j]), [self.b_const, A["bst"][j]], [A["brs"][j]])
            if J["nsink"] is not None:
                P.op("dve", lambda e: e.tensor_tensor(out=A["rs"][:, 0:2], in0=A["rs"][:, 0:2], in1=A["es"][:, 0:2], op=ALU.add), [], [A["brs"][0], A["brs"][1]])
            P.op("dve", lambda e: e.reciprocal(out=A["r"][:, 0:2], in_=A["rs"][:, 0:2]), [], [A["brs"][0], A["brs"][1]])
            for j in range(2):
                P.op("act", lambda e, j=j: e.activation(out=A["pn"][:, j, :], in_=A["s"][:, j, :], func=AF.Copy, scale=A["r"][:, j:j + 1]), [A["bs"][j], A["brs"][j]], [A["bpn"]])

        def st2(i):
            A = self.att[i % 2]
            for j in range(2):
                for kc in range(2):
                    P.op("pe", lambda e, j=j, kc=kc: e.transpose(ps_t[:, (2 * j + kc) * 128:(2 * j + kc + 1) * 128], A["pn"][:, j, kc * 128:(kc + 1) * 128], self.identb[:]),
                         [A["bpn"], self.b_const], [b_t])
            self.copy("act", A["pT"][:, :, :], ps_t[:, 0:512].rearrange("p (a t) -> p a t", t=128), [], [b_t, A["bpT"]])

        def st3(i):
            J, A = jobs[i], self.att[i % 2]
            for j in range(2):
                for kc in range(2):
                    P.op("pe", lambda e, j=j, kc=kc: e.matmul(ps_o[j * 64:(j + 1) * 64, 0:128], lhsT=J["V"](j, kc), rhs=A["pT"][:, 2 * j + kc, :], start=(kc == 0), stop=(kc == 1)),
                         [A["bpT"]] + J["bv"], [b_o])
            self.copy("dve", J["out"], ps_o[:, 0:128], [], [b_o] + J["bout"])

        n = len(jobs)
        for r in range(n + 3):
            if r < n:
                st1a(r)
            if 0 <= r - 1 < n:
                st1b(r - 1)
            if 0 <= r - 2 < n:
                st2(r - 2)
            if 0 <= r - 3 < n:
                st3(r - 3)

    def attn_alloc(self):
        c = self.carve
        self.att = [dict(s=c([2, 256], F32), pn=c([2, 256], BF16), pT=c([4, 128], BF16),
                         mx=c([2], F32), nm=c([2], F32), rs=c([2], F32), es=c([2], F32), r=c([2], F32),
                         b=self.pbuf("att"), bpn=self.pbuf("attpn"), bpT=self.pbuf("attpT"),
                         bs=self.pbufs("atts", 2), bst=self.pbufs("attst", 2), brs=self.pbufs("attrs", 2)) for _ in range(2)]
        self.att_jobs = []

    def post_norm_add(self, mixT, b_mix, vec, l, g, ncols, scratch):
        P = self.P
        c0 = g * G
        rstd = self.carve([528], F32)
        b_r = self.pbuf("rstd2")
        self.norm_stats(mixT[:, :, 0:ncols], KC, ncols, 1.0 / D, b_mix, rstd, b_r, scratch)
        for c in range(KC):
            P.op("dve", lambda e, c=c: e.scalar_tensor_tensor(out=mixT[:, c, 0:ncols], in0=mixT[:, c, 0:ncols], scalar=self.gcol(vec, l, c), in1=rstd[:, 0:ncols],
                                                              op0=ALU.mult, op1=ALU.mult), [b_r, self.b_const], [b_mix[c]])
            P.op("pool" if c % 2 == 0 else "dve", lambda e, c=c: e.tensor_tensor(out=self.hT[:, c, c0:c0 + ncols], in0=self.hT[:, c, c0:c0 + ncols], in1=mixT[:, c, 0:ncols], op=ALU.add),
                 [b_mix[c]], [self.b_hc[g][c]])

    def mix_out(self, l, g, ncols, catT, b_cat, scratch):
        mixT = self.carve([KC, 528], F32)
        b_mix = self.pbufs("mix", KC)

        def evac(nb, ps, bp, vi):
            self.copy(self.evac_eng(), mixT[:, nb, 0:ncols], ps[:, 0:ncols], [], list(bp) + [b_mix[nb]])
        self.linear_fm(self.w_out[l], D, 0, KC, catT, b_cat, ncols, evac)
        self.post_norm_add(mixT, b_mix, 1, l, g, ncols, scratch)

    def mlp_phase(self, l, g):
        P = self.P
        last = (g == NG - 1)
        ncols = 528 if last else 512
        self.phase_begin()
        uT = self.carve([32, 528], BF16)
        b_u = self.pbufs("u", 32)
        uflat = uT.rearrange("p a t -> p (a t)")
        sq_scr = uflat[:, 0:KC * 528].rearrange("p (c t) -> p c t", t=528)
        sqt_scr = uflat[:, KC * 528:KC * 528 + 1056].bitcast(F32)
        scratch = (sq_scr, sqt_scr, b_u[0:10])
        hnT, b_hn = self.norm_group(g, ncols, 2, l, scratch=scratch)

        def evac_u(nb, ps, bp, vi):
            eng = self.evac_eng()
            if eng == "act":
                P.op("act", lambda e: e.activation(out=uT[:, nb, 0:ncols], in_=ps[:, 0:ncols], func=AF.Relu), [], list(bp) + [b_u[nb]])
            else:
                P.op("dve", lambda e: e.tensor_scalar(out=uT[:, nb, 0:ncols], in0=ps[:, 0:ncols], scalar1=0.0, scalar2=None, op0=ALU.max), [], list(bp) + [b_u[nb]])
            P.op("pool", lambda e: e.tensor_tensor(out=uT[:, nb, 0:ncols], in0=uT[:, nb, 0:ncols], in1=uT[:, nb, 0:ncols], op=ALU.mult), [], [b_u[nb]])
        self.linear_fm(self.w_up[l], D, 0, 32, hnT, b_hn, ncols, evac_u)
        mixT = self.carve([KC, 528], F32)
        b_mix = self.pbufs("mix", KC)

        def evac_d(nb, ps, bp, vi):
            self.copy(self.evac_eng(), mixT[:, nb, 0:ncols], ps[:, 0:ncols], [], list(bp) + [b_mix[nb]])
        self.linear_fm(self.w_down[l], DFF, 0, KC, uT, b_u, ncols, evac_d)
        self.post_norm_add(mixT, b_mix, 3, l, g, ncols, (hnT, sqt_scr, b_hn + b_u[0:10]))
        self.phase_end()

    def mem_attn_prompt(self, l, cqT, b_cq, catT, b_cat):
        for t in range(4):
            for blk in range(2):
                self.attn_pair(cqT[:, blk, t * 128:(t + 1) * 128], b_cq,
                               lambda j, blk=blk: self.mkT[j * 64:(j + 1) * 64, l, blk, :], [self.b_mkv],
                               lambda j, kc, blk=blk: self.mvb[:, l, kc, (2 * blk + j) * 64:(2 * blk + j + 1) * 64], [self.b_mkv],
                               catT[:, 6 + blk, t * 128:(t + 1) * 128], [b_cat[6 + blk]])

    def a_phase2(self, l, g):
        P = self.P
        last = (g == NG - 1)
        ncols = 528 if last else 512
        W = self.w_in_a[l]
        self.phase_begin()
        catT = self.carve([KC, 528], BF16)
        b_cat = self.pbufs("cat", KC)
        sqt_scr = self.carve([528], F32)
        hnT, b_hn = self.norm_group(g, ncols, 0, l, scratch=(catT, sqt_scr, b_cat))
        gateT = self.carve([HH, 528], BF16)
        b_gt = self.pbufs("gate", HH)
        cqT = self.carve([2, 528], BF16)
        b_cq = self.pbufs("cq", 2)

        def evac_g(nb, ps, bp, vi):
            P.op("act", lambda e: e.activation(out=gateT[:, nb, 0:ncols], in_=ps[:, 0:ncols], func=AF.Silu), [], list(bp) + [b_gt[nb]])
        self.linear_fm(W, D, 3 * HW, HH, hnT, b_hn, ncols, evac_g)

        def evac_c(nb, ps, bp, vi):
            self.copy(self.evac_eng(), cqT[:, nb, 0:ncols], ps[:, 0:ncols], [], list(bp) + [b_cq[nb]])
        self.linear_fm(W, D, 4 * HW, 2, hnT, b_hn, ncols, evac_c)
        oloc = self.carve([HH, 528], BF16)
        qsD = self.carve([HH, 512], BF16)
        b_ld = self.pbuf("ld")
        P.dma(out=oloc[:, :, 0:ncols], in_=self.spill_o[g][:, :, 0:ncols], reads=[self.b_spo[g]], writes=[b_ld], sem="ld2o")
        P.dma(out=qsD[:, :, :], in_=self.spill_q[g][:, :, :], reads=[self.b_spq[g]], writes=[b_ld], sem="ld2q")
        o = self.carve([HH, 528], F32)
        b_o = self.pbufs("o", HH)
        rst = self.carve([528], F32)
        b_rst = self.pbuf("rst")
        for h in range(HH):
            ps, bp = self.acc_tile()
            P.op("pe", lambda e, h=h, ps=ps: e.matmul(ps[:, 0:512], lhsT=self.Sst_b[:, h, :], rhs=qsD[:, h, :], start=True, stop=True), [self.b_Sst, b_ld], [bp[0]])
            P.op("dve", lambda e, h=h, ps=ps: e.tensor_tensor(out=o[:, h, 0:512], in0=ps[:, 0:512], in1=oloc[:, h, 0:512], op=ALU.add), [b_ld], [bp[0], b_o[h]])
            if last:
                P.op("pool", lambda e, h=h: e.tensor_copy(out=o[:, h, 512:528], in_=oloc[:, h, 512:528]), [b_ld], [b_o[h]])
            self.norm_stats(o[:, h:h + 1, 0:ncols], 1, ncols, 1.0 / 128, [b_o[h]], rst, b_rst, scratch=(hnT, sqt_scr, b_hn))
            P.op("dve", lambda e, h=h: e.scalar_tensor_tensor(out=o[:, h, 0:ncols], in0=o[:, h, 0:ncols], scalar=self.hgn_col(l, h), in1=rst[:, 0:ncols],
                                                              op0=ALU.mult, op1=ALU.mult), [b_rst, self.b_const], [b_o[h]])
            P.op("pool", lambda e, h=h: e.tensor_tensor(out=catT[:, h, 0:ncols], in0=o[:, h, 0:ncols], in1=gateT[:, h, 0:ncols], op=ALU.mult), [b_o[h], b_gt[h]], [b_cat[h]])
        self.attn_alloc()
        self.mem_attn_prompt(l, cqT, b_cq, catT, b_cat)
        self.attn_run()
        if last:
            P.op("pool", lambda e: e.tensor_copy(out=catT[:, 6:8, 512:528], in_=self.dec_x[:, :, :]), [self.b_decx], [b_cat[6], b_cat[7]])
        self.mix_out(l, g, ncols, catT, b_cat, (hnT, sqt_scr, b_hn))
        if self.stop_after == f"a2_{l}_{g}":
            self.dump("catT", catT[:, :, 0:ncols], [128, KC, ncols], b_cat, BF16)
        self.phase_end()

    def a_exchange(self, l):
        P = self.P
        self.phase_begin()
        X = self.carve([774], F32)
        b_X = self.pbuf("X")
        for h in range(HH):
            P.op("dve", lambda e, h=h: e.tensor_copy(out=X[:, h * 128:(h + 1) * 128], in_=self.Sf[:, h, :]), [self.b_Sf[h]], [b_X])
            P.op("dve", lambda e, h=h: e.tensor_copy(out=X[:, 768 + h:769 + h], in_=self.Gacc[:, h:h + 1]), [self.b_G[h]], [b_X])
        gin = self.dram_tmp(f"gin{l}", [128, 774])
        gout = self.dram_tmp(f"gout{l}", [512, 774])
        b_gin, b_gout = Buf("gin"), Buf("gout")
        P.dma(out=gin[:, :], in_=X[:, :], reads=[b_X], writes=[b_gin], sem="xch")
        P.coll(lambda e: e.collective_compute("AllGather", ALU.bypass, replica_groups=[[0, 1, 2, 3], [4, 5, 6, 7]], ins=[gin[:, :]], outs=[gout[:, :]]),
               reads=[b_gin], writes=[b_gout])
        Y = self.carve([4, 774], F32)
        b_Y = self.pbuf("Y")
        P.dma(out=Y[:, :, :], in_=gout.rearrange("(r p) n -> p r n", p=128), reads=[b_gout], writes=[b_Y], sem="xch")
        T = self.carve([HH, 128], F32)
        Sst = self.carve([HH, 128], F32)
        Fj = self.carve([HH], F32)
        b_T = self.pbuf("T")
        P.op("dve", lambda e: e.memset(T[:, :, :], 0.0), [], [b_T])
        P.op("dve", lambda e: e.memset(Sst[:, :, :], 0.0), [], [b_T])
        for j in range(4):
            P.op("dve", lambda e, j=j: e.scalar_tensor_tensor(out=Sst[:, :, :], in0=T[:, :, :], scalar=self.corc[:, 1 + j:2 + j], in1=Sst[:, :, :], op0=ALU.mult, op1=ALU.add),
                 [self.b_const], [b_T])
            P.op("act", lambda e, j=j: e.activation(out=Fj[:, :], in_=Y[:, j, 768:774], func=AF.Exp), [b_Y], [b_T])
            P.op("dve", lambda e, j=j: e.tensor_tensor(out=T[:, :, :], in0=T[:, :, :], in1=Fj[:, :].unsqueeze(2).to_broadcast([128, HH, 128]), op=ALU.mult), [], [b_T])
            P.op("dve", lambda e, j=j: e.tensor_tensor(out=T[:, :, :], in0=T[:, :, :], in1=Y[:, j, 0:768].rearrange("p (h v) -> p h v", v=128), op=ALU.add), [b_Y], [b_T])
        P.op("dve", lambda e: e.tensor_copy(out=self.Sst_b[:, :, :], in_=Sst[:, :, :]), [b_T], [self.b_Sst])
        P.dma(out=self.hgp[l].rearrange("h k v -> k h v"), in_=T[:, :, :], reads=[b_T], sem="xch")
        self.phase_end()

    def setup_rope_consts(self):
        P, sb = self.P, self.sb
        self.invf = sb("invf", [128, 1], F32)
        self.sgn = sb("sgn", [128, 1], F32)
        self.sinkb = sb("sinkb", [128, 24], F32)
        self.nsinkb = sb("nsinkb", [128, 24], F32)
        pi = sb("rp_i", [128, 1], I32)
        pf = sb("rp_f", [128, 1], F32)
        ge = sb("rp_ge", [128, 1], F32)
        bc = self.b_const
        for half in range(2):
            P.op("pool", lambda e, half=half: e.iota(pi[half * 64:(half + 1) * 64, :], [[0, 1]], 0, 1), [], [bc])
        P.op("pool", lambda e: e.tensor_copy(out=pf[:], in_=pi[:]), [bc], [bc])
        P.op("dve", lambda e: e.tensor_single_scalar(out=ge[:], in_=pf[:], scalar=32.0, op=ALU.is_ge), [bc], [bc])
        P.op("dve", lambda e: e.tensor_scalar(out=self.sgn[:], in0=ge[:], scalar1=2.0, scalar2=-1.0, op0=ALU.mult, op1=ALU.add), [bc], [bc])
        P.op("dve", lambda e: e.scalar_tensor_tensor(out=pf[:], in0=ge[:], scalar=-32.0, in1=pf[:], op0=ALU.mult, op1=ALU.add), [bc], [bc])
        P.op("act", lambda e: e.activation(out=self.invf[:], in_=pf[:], func=AF.Exp, scale=-math.log(10000.0) / 32.0), [bc], [bc])
        P.dma(out=self.sinkb[:], in_=self.sinks.rearrange("a b -> (a b)").partition_broadcast(128), writes=[bc], sem="misc")
        P.op("dve", lambda e: e.tensor_scalar(out=self.nsinkb[:], in0=self.sinkb[:], scalar1=-1.0, scalar2=None, op0=ALU.mult), [bc], [bc])

    def rope_tables(self, g, ncols):
        P = self.P
        Ct = self.carve([528], F32)
        St = self.carve([528], F32)
        pi = self.carve([528], F32).bitcast(I32)
        y = self.carve([528], F32)
        kf = self.carve([528], F32)
        b = self.pbuf("rope")
        P.op("pool", lambda e: e.iota(pi[:, :], [[1, 528]], g * G, 0), [], [b])
        P.op("pool", lambda e: e.tensor_copy(out=y[:, :], in_=pi[:, :]), [b], [b])
        P.op("dve", lambda e: e.tensor_scalar(out=y[:, :], in0=y[:, :], scalar1=self.corc[:, 0:1], scalar2=None, op0=ALU.add), [b, self.b_const], [b])
        if ncols > 512:
            P.op("dve", lambda e: e.memset(y[:, 512:528], float(PAST)), [b], [b])
        P.op("dve", lambda e: e.tensor_scalar(out=y[:, :], in0=y[:, :], scalar1=self.invf[:, 0:1], scalar2=None, op0=ALU.mult), [b, self.b_const], [b])
        P.op("dve", lambda e: e.tensor_scalar(out=y[:, :], in0=y[:, :], scalar1=1.0 / (2 * math.pi), scalar2=0.5, op0=ALU.mult, op1=ALU.add), [b], [b])
        for which, out in ((0, St), (1, Ct)):
            if which == 1:
                P.op("dve", lambda e: e.tensor_scalar(out=y[:, :], in0=y[:, :], scalar1=0.25, scalar2=None, op0=ALU.add), [b], [b])
            P.op("dve", lambda e: e.tensor_copy(out=pi[:, :], in_=y[:, :]), [b], [b])
            P.op("dve", lambda e: e.tensor_copy(out=kf[:, :], in_=pi[:, :]), [b], [b])
            P.op("dve", lambda e: e.tensor_tensor(out=kf[:, :], in0=y[:, :], in1=kf[:, :], op=ALU.subtract), [b], [b])
            P.op("dve", lambda e, out=out: e.tensor_single_scalar(out=out[:, :], in_=kf[:, :], scalar=0.0, op=ALU.is_lt), [b], [b])
            P.op("dve", lambda e, out=out: e.tensor_tensor(out=kf[:, :], in0=kf[:, :], in1=out[:, :], op=ALU.add), [b], [b])
            P.op("dve", lambda e: e.tensor_scalar(out=kf[:, :], in0=kf[:, :], scalar1=-0.5, scalar2=2 * math.pi, op0=ALU.add, op1=ALU.mult), [b], [b])
            if which == 0:
                P.op("act", lambda e, out=out: e.activation(out=out[:, :], in_=kf[:, :], func=AF.Sin), [b], [b])
                P.op("dve", lambda e, out=out: e.tensor_scalar(out=out[:, :], in0=out[:, :], scalar1=self.sgn[:, 0:1], scalar2=None, op0=ALU.mult), [b, self.b_const], [b])
            else:
                P.op("act", lambda e, out=out: e.activation(out=out[:, :], in_=kf[:, :], func=AF.Sin), [b], [b])
        return Ct, St, b

    def kv_pass(self, g):
        P = self.P
        last = (g == NG - 1)
        ncols = 528 if last else 512
        c0 = g * G
        self.phase_begin()
        Ct, St, b_rope = self.rope_tables(g, ncols)
        hnT, b_hn = self.norm_group(g, ncols, 0, 0, gain_fn=self.kvn_col)
        (sn, bn), (ss, bs) = self.get_slab(self.w_kv, 0, 0, 256, swap=True)
        Kr = self.carve([4, 528], BF16)
        b_K = self.pbufs("Kr", 4)
        t1 = self.carve([528], F32)
        t2 = self.carve([528], F32)
        b_t = self.pbuf("kt")
        cols = [(0, 512)] + ([(512, 528)] if last else [])
        if last:
            Kf = self.carve([4, 128 + NSMP], F32)
            b_Kf = self.pbuf("Kf")
        wdup = self.carve([8, 8, 128], BF16)
        b_wd = self.pbufs("wdup", 8)
        for vi, (sl, bl) in enumerate(((sn, bn), (ss, bs))):
            for hd in range(4):
                w = wdup[:, vi * 4 + hd]
                for half in range(2):
                    P.op("pool", lambda e, w=w, sl=sl, hd=hd, half=half: e.tensor_copy(out=w[:, :, half * 64:(half + 1) * 64], in_=sl[:, :, hd * 64:(hd + 1) * 64]),
                         [bl], [b_wd[vi * 4 + hd]])
        for hd in range(4):
            tiles = []
            for vi in range(2):
                ps, bp = self.acc_tile()
                w, bw = wdup[:, vi * 4 + hd], b_wd[vi * 4 + hd]
                for kc in range(8):
                    for (a, b) in cols:
                        P.op("pe", lambda e, ps=ps, w=w, kc=kc, a=a, b=b: e.matmul(ps[:, a:b], lhsT=w[:, kc, :], rhs=hnT[:, kc, a:b],
                                                                                 start=(kc == 0), stop=(kc == 7)), [bw] + b_hn, list(bp))
                tiles.append((ps, bp))
            (pn, bpn), (psw, bpsw) = tiles
            P.op("dve", lambda e, pn=pn: e.tensor_tensor(out=t1[:, 0:ncols], in0=pn[:, 0:ncols], in1=Ct[:, 0:ncols], op=ALU.mult), [b_rope], list(bpn) + [b_t])
            P.op("dve", lambda e, psw=psw: e.tensor_tensor(out=t2[:, 0:ncols], in0=psw[:, 0:ncols], in1=St[:, 0:ncols], op=ALU.mult), [b_rope], list(bpsw) + [b_t])
            P.op("pool", lambda e, hd=hd: e.tensor_tensor(out=Kr[:, hd, 0:ncols], in0=t1[:, 0:ncols], in1=t2[:, 0:ncols], op=ALU.add), [b_t], [b_K[hd]])
            if last:
                P.op("pool", lambda e, hd=hd: e.tensor_tensor(out=Kf[:, hd, :], in0=t1[:, 384:528], in1=t2[:, 384:528], op=ALU.add), [b_t], [b_Kf])
        P.dma(out=self.kT_d[:, :, c0:c0 + ncols], in_=Kr[:, :, 0:ncols], reads=b_K, writes=[self.b_kTd], sem="kvst")
        sv, bv = self.get_slab(self.w_kv, 0, 256, 256)
        Vt = self.carve([5, 256], BF16)
        b_Vt = self.pbuf("Vt")
        if last:
            Vf = self.carve([2, 256], F32)
            b_Vf = self.pbuf("Vf")
        for t in range(4 + (1 if last else 0)):
            rows = 128 if t < 4 else NSMP
            ps, bp = self.psQ[t % 2], self.b_psQ[t % 2]
            for kc in range(8):
                P.op("pe", lambda e, ps=ps, kc=kc, rows=rows, t=t: e.matmul(ps[0:rows, 0:256], lhsT=hnT[:, kc, t * 128:t * 128 + rows], rhs=sv[:, kc, :],
                                                                           start=(kc == 0), stop=(kc == 7)), [bv] + b_hn, [bp])
            if last and t >= 3:
                self.copy("act", Vf[0:rows, t - 3, :], ps[0:rows, 0:256], [], [bp, b_Vf])
                self.copy("dve", Vt[0:rows, t, :], Vf[0:rows, t - 3, :], [b_Vf], [b_Vt])
            else:
                self.copy(self.evac_eng(), Vt[0:rows, t, :], ps[0:rows, 0:256], [], [bp, b_Vt])
        P.dma(out=self.v_d[c0:c0 + 512, :].rearrange("(t p) n -> p t n", p=128), in_=Vt[:, 0:4, :], reads=[b_Vt], writes=[self.b_vd], sem="kvst")
        if last:
            P.dma(out=self.svp[:, :], in_=Vf[:, 0, :], reads=[b_Vf], sem="kvo")
            P.dma(out=self.vnew_d[:, :], in_=Vf[0:NSMP, 1, :], reads=[b_Vf], writes=[self.b_newd], sem="kvo")
            Kt = self.carve([2, 256], F32)
            b_Kt = self.pbuf("Kt")
            ps, bp = self.psQ[2], self.b_psQ[2]
            for hd in range(4):
                P.op("pe", lambda e, hd=hd: e.transpose(ps[:, hd * 64:(hd + 1) * 64], Kf[0:64, hd, 0:128], self.identf[0:64, 0:64]), [b_Kf, self.b_const], [bp])
            self.copy("act", Kt[:, 0, :], ps[:, 0:256], [], [bp, b_Kt])
            ps2, bp2 = self.psQ[3], self.b_psQ[3]
            for hd in range(4):
                P.op("pe", lambda e, hd=hd: e.transpose(ps2[0:NSMP, hd * 64:(hd + 1) * 64], Kf[0:64, hd, 128:128 + NSMP], self.identf[0:64, 0:64]), [b_Kf, self.b_const], [bp2])
            self.copy("dve", Kt[0:NSMP, 1, :], ps2[0:NSMP, 0:256], [], [bp2, b_Kt])
            P.dma(out=self.skp[:, :], in_=Kt[:, 0, :], reads=[b_Kt], sem="kvo")
            P.dma(out=self.knew_d[:, :], in_=Kt[0:NSMP, 1, :], reads=[b_Kt], writes=[self.b_newd], sem="kvo")
            P.dma(out=self.kTnew_d[:, :, :], in_=Kr[:, :, 512:528], reads=b_K, writes=[self.b_newd], sem="kvo")
            P.dma(out=self.hin[:, 0:512].rearrange("p (h t) -> p h t", t=128), in_=Kr[:, :, 384:512], reads=b_K, writes=[self.b_hin], sem="kvo")
            P.dma(out=self.hin[:, 512:768], in_=Vt[:, 3, :], reads=[b_Vt], writes=[self.b_hin], sem="kvo")
        self.phase_end()

    def kv_alloc(self):
        P = self.P
        self.kT_d = self.dram_tmp("kT_d", [128, 4, NTOK + NSMP], BF16)
        self.v_d = self.dram_tmp("v_d", [NTOK, 256], BF16)
        self.knew_d = self.dram_tmp("knew_d", [NSMP, 256], F32)
        self.vnew_d = self.dram_tmp("vnew_d", [NSMP, 256], F32)
        self.kTnew_d = self.dram_tmp("kTnew_d", [128, 4, NSMP], BF16)
        self.hin = self.dram_tmp("hin", [128, 768], BF16)
        self.hout = self.dram_tmp("hout", [512, 768], BF16)
        self.b_kTd, self.b_vd, self.b_newd, self.b_hin, self.b_hout = Buf("kTd"), Buf("vd"), Buf("newd"), Buf("hin"), Buf("hout")

    def kv_exchange(self):
        P = self.P
        P.coll(lambda e: e.collective_compute("AllGather", ALU.bypass, replica_groups=[[0, 1, 2, 3], [4, 5, 6, 7]], ins=[self.hin[:, :]], outs=[self.hout[:, :]]),
               reads=[self.b_hin], writes=[self.b_hout])

    def b_phase(self, l, g):
        P = self.P
        jl = l - NA
        last = (g == NG - 1)
        ncols = 528 if last else 512
        c0 = g * G
        W = self.w_in_b[jl]
        self.phase_begin()
        Ct, St, b_rope = self.rope_tables(g, ncols)
        catT = self.carve([KC, 528], BF16)
        b_cat = self.pbufs("cat", KC)
        sqt_scr = self.carve([528], F32)
        hnT, b_hn = self.norm_group(g, ncols, 0, l, scratch=(catT, sqt_scr, b_cat))
        qr = self.carve([6, 528], BF16)
        b_qr = self.pbufs("qr", 6)
        cqT = self.carve([2, 528], BF16)
        b_cq = self.pbufs("cq", 2)
        tq = self.carve([2, 528], F32)
        b_tq = self.pbufs("tq", 2)
        t2 = self.carve([528], F32)
        b_t2 = self.pbuf("t2")

        def evac_q(nb, ps, bp, vi):
            i = nb % 2
            if vi == 0:
                P.op("dve", lambda e: e.tensor_tensor(out=tq[:, i, 0:ncols], in0=ps[:, 0:ncols], in1=Ct[:, 0:ncols], op=ALU.mult), [b_rope], list(bp) + [b_tq[i]])
            else:
                P.op("dve", lambda e: e.tensor_tensor(out=t2[:, 0:ncols], in0=ps[:, 0:ncols], in1=St[:, 0:ncols], op=ALU.mult), [b_rope], list(bp) + [b_t2])
                P.op("pool", lambda e: e.tensor_tensor(out=qr[:, nb, 0:ncols], in0=tq[:, i, 0:ncols], in1=t2[:, 0:ncols], op=ALU.add), [b_tq[i], b_t2], [b_qr[nb]])
        self.linear_fm(W, D, 0, 6, hnT, b_hn, ncols, evac_q, swap=True)

        def evac_c(nb, ps, bp, vi):
            self.copy(self.evac_eng(), cqT[:, nb, 0:ncols], ps[:, 0:ncols], [], list(bp) + [b_cq[nb]])
        self.linear_fm(W, D, HW, 2, hnT, b_hn, ncols, evac_c)
        Kw = self.carve([4, 640], BF16)
        Vw = self.carve([5, 256], BF16)
        b_Kw, b_Vw = self.pbuf("Kw"), self.pbuf("Vw")
        if g == 0:
            Yh = self.carve([4, 768], BF16)
            b_Yh = self.pbuf("Yh")
            P.dma(out=Yh[:, :, :], in_=self.hout.rearrange("(r p) n -> p r n", p=128), reads=[self.b_hout], writes=[b_Yh], sem="ldk")
            hk = self.carve([768], BF16)
            P.op("dve", lambda e: e.tensor_scalar(out=hk[:, :], in0=Yh[:, 0, :], scalar1=self.corc[:, 5:6], scalar2=None, op0=ALU.mult), [b_Yh, self.b_const], [b_Yh])
            for j in range(1, 4):
                P.op("dve", lambda e, j=j: e.scalar_tensor_tensor(out=hk[:, :], in0=Yh[:, j, :], scalar=self.corc[:, 5 + j:6 + j], in1=hk[:, :], op0=ALU.mult, op1=ALU.add),
                     [self.b_const], [b_Yh])
            P.op("pool", lambda e: e.tensor_copy(out=Kw[:, :, 0:128], in_=hk[:, 0:512].rearrange("p (h t) -> p h t", t=128)), [b_Yh], [b_Kw])
            P.op("pool", lambda e: e.tensor_copy(out=Vw[:, 0, :], in_=hk[:, 512:768]), [b_Yh], [b_Vw])
            P.dma(out=Kw[:, :, 128:640], in_=self.kT_d[:, :, 0:512], reads=[self.b_kTd], writes=[b_Kw], sem="ldk")
            P.dma(out=Vw[:, 1:5, :], in_=self.v_d[0:512, :].rearrange("(t p) n -> p t n", p=128), reads=[self.b_vd], writes=[b_Vw], sem="ldv")
        else:
            P.dma(out=Kw[:, :, :], in_=self.kT_d[:, :, c0 - 128:c0 + 512], reads=[self.b_kTd], writes=[b_Kw], sem="ldk")
            P.dma(out=Vw[:, :, :], in_=self.v_d[c0 - 128:c0 + 512, :].rearrange("(t p) n -> p t n", p=128), reads=[self.b_vd], writes=[b_Vw], sem="ldv")
        self.attn_alloc()
        for t in range(4):
            mask = self.smask0[:, :] if (g == 0 and t == 0) else self.smask[:, :]
            for blk in range(6):
                hs = [2 * blk, 2 * blk + 1]
                self.attn_pair(qr[:, blk, t * 128:(t + 1) * 128], [b_qr[blk]],
                               lambda j, hs=hs, t=t: Kw[j * 64:(j + 1) * 64, hs[j] // 3, t * 128:t * 128 + 256], [b_Kw],
                               lambda j, kc, hs=hs, t=t: Vw[:, t + kc, (hs[j] // 3) * 64:(hs[j] // 3 + 1) * 64], [b_Vw],
                               catT[:, blk, t * 128:(t + 1) * 128], [b_cat[blk]], mask=mask,
                               nsink=[self.nsinkb[:, jl * 12 + h:jl * 12 + h + 1] for h in hs], sinkc=[self.sinkb[:, jl * 12 + h:jl * 12 + h + 1] for h in hs])
        self.mem_attn_prompt(l, cqT, b_cq, catT, b_cat)
        self.attn_run()
        if last:
            P.op("pool", lambda e: e.tensor_copy(out=catT[:, 0:6, 512:528], in_=self.dec_o[:, :, :]), [self.b_deco], b_cat[0:6])
            P.op("pool", lambda e: e.tensor_copy(out=catT[:, 6:8, 512:528], in_=self.dec_x[:, :, :]), [self.b_decx], [b_cat[6], b_cat[7]])
        self.mix_out(l, g, ncols, catT, b_cat, (hnT, sqt_scr, b_hn))
        self.phase_end()

    def write_outputs(self):
        P = self.P
        self.phase_begin()
        st = [self.carve([D], F32) for _ in range(2)]
        b_st = self.pbufs("yst", 2)
        for t in range(17):
            rows = 128 if t < 16 else NSMP
            col0 = t * 128
            i = t % 2
            for half in range(2):
                ps, bp = (self.psA, self.bank[0]) if half == 0 else (self.psB, self.bank[2])
                for j in range(4):
                    c = half * 4 + j
                    P.op("pe", lambda e, ps=ps, j=j, c=c, rows=rows, col0=col0: e.transpose(ps[0:rows, j * 128:(j + 1) * 128], self.hT[:, c, col0:col0 + rows], self.identf[:, :]),
                         self.hb(min(t // 4, NG - 1)) + [self.b_const], [bp])
                self.copy("act" if half == 0 else "dve", st[i][0:rows, half * 512:(half + 1) * 512], ps[0:rows, 0:512], [], [bp, b_st[i]])
            dst = self.y_p[col0:col0 + 128, :] if t < 16 else self.y_s[:, :]
            P.dma(out=dst, in_=st[i][0:rows, :], reads=[b_st[i]], sem=f"yst{i}")
        self.phase_end()

    def dec_alloc(self):
        self.dec_o = self.sb("dec_o", [128, HH, NSMP], BF16)
        self.dec_x = self.sb("dec_x", [128, 2, NSMP], BF16)
        self.b_deco, self.b_decx = Buf("dec_o"), Buf("dec_x")

    def dec_norm(self, vec, l):
        rstd = self.carve([NSMP], F32)
        b_r = self.pbuf("drstd")
        hn = self.carve([KC, NSMP], BF16)
        b_hn = self.pbufs("dhn", KC)
        g = NG - 1
        self.norm_stats(self.hT[:, :, NTOK:TT], KC, NSMP, 1.0 / D, self.hb(g), rstd, b_r)
        self.apply_norm(lambda c: self.hT[:, c, NTOK:TT], KC, NSMP, rstd, b_r, lambda c: self.gcol(vec, l, c), lambda c: hn[:, c, :], self.hb(g), b_hn)
        return hn, b_hn

    def sel_tiles(self):
        self.selt = [self.carve([128], F32) for _ in range(2)]
        self.b_sel = self.pbufs("sel", 2)
        self.sel_i = 0

    def sel(self, s_):
        i = self.sel_i % 2
        self.sel_i += 1
        t, b = self.selt[i], self.b_sel[i]
        self.P.op("dve", lambda e: e.tensor_copy(out=t[0:NSMP, :], in_=self.identf[0:NSMP, s_:s_ + 1].to_broadcast([NSMP, 128])), [self.b_const], [b])
        return t, b

    def a_decode_phase(self, l):
        P = self.P
        W = self.w_in_a[l]
        self.phase_begin()
        hn, b_hn = self.dec_norm(0, l)
        qs_ = self.carve([HH, NSMP], F32)
        fs = self.carve([HH, NSMP], F32)
        kks = self.carve([HH, NSMP], F32)
        tmp = self.carve([NSMP], F32)
        Vs = self.carve([HW], F32)
        b_s = self.pbuf("dsm")
        lb = lambda h: self.lbc[:, l * HH + h:l * HH + h + 1]
        oml = lambda h: self.omlc[:, l * HH + h:l * HH + h + 1]

        def evac_q(nb, ps, bp, vi):
            P.op("act", lambda e: e.activation(out=qs_[:, nb, :], in_=ps[:, 0:NSMP], func=AF.Silu), [], list(bp) + [b_s])
        self.linear_fm(W, D, 0, HH, hn, b_hn, NSMP, evac_q)

        def evac_z(nb, ps, bp, vi):
            P.op("act", lambda e: e.activation(out=tmp[:, :], in_=ps[:, 0:NSMP], func=AF.Sigmoid), [], list(bp) + [b_s])
            P.op("dve", lambda e: e.tensor_scalar(out=fs[:, nb, :], in0=tmp[:, :], scalar1=oml(nb), scalar2=lb(nb), op0=ALU.mult, op1=ALU.add), [self.b_const], [b_s])
            P.op("dve", lambda e: e.tensor_scalar(out=kks[:, nb, :], in0=fs[:, nb, :], scalar1=-1.0, scalar2=1.0, op0=ALU.mult, op1=ALU.add), [], [b_s])
        self.linear_fm(W, D, HW, HH, hn, b_hn, NSMP, evac_z)
        for sidx in range(3):
            sl, bl = self.get_slab(W, 0, 2 * HW + sidx * 256, 256)
            ps, bp = self.psQ[sidx % 2], self.b_psQ[sidx % 2]
            for kc in range(8):
                P.op("pe", lambda e, ps=ps, kc=kc, sl=sl: e.matmul(ps[0:NSMP, 0:256], lhsT=hn[:, kc, :], rhs=sl[:, kc, :], start=(kc == 0), stop=(kc == 7)), [bl] + b_hn, [bp])
            self.copy("dve", Vs[0:NSMP, sidx * 256:(sidx + 1) * 256], ps[0:NSMP, 0:256], [], [bp, b_s])
        Sst = [self.carve([2, HH, 128], F32) for _ in range(4)]
        b_St = self.pbufs("dSt", 4)
        T1_2 = [self.carve([HH, 128], F32) for _ in range(2)]
        b_T1_2 = self.pbufs("dT1", 2)
        self.sel_tiles()
        ps_o, b_po = self.psQ[2], self.b_psQ[2]
        for s_ in range(NSMP):
            ci, si = (s_ // 2) % 4, s_ % 2
            if si == 0:
                P.dma(out=Sst[ci][:, :, :, :], in_=self.shg[l, s_:s_ + 2].rearrange("s h k v -> k s h v"), writes=[b_St[ci]], sem=f"dst{ci}")
            st, bst = self.sel(s_)
            ps, bp = self.acc_tile()
            P.op("pe", lambda e, ps=ps, st=st: e.matmul(ps[:, 0:512], lhsT=st[0:NSMP, :], rhs=Vs[0:NSMP, 0:512], start=True, stop=True), [bst, b_s], [bp[0]])
            P.op("pe", lambda e, ps=ps, st=st: e.matmul(ps[:, 512:768], lhsT=st[0:NSMP, :], rhs=Vs[0:NSMP, 512:768], start=True, stop=True), [bst, b_s], [bp[1]])
            T1, b_T1 = T1_2[s_ % 2], b_T1_2[s_ % 2]
            P.op("dve", lambda e, ps=ps, s_=s_, T1=T1: e.tensor_tensor(out=T1[:, :, :], in0=ps[:, 0:768].rearrange("p (h v) -> p h v", v=128),
                                                                      in1=kks[:, :, s_:s_ + 1].to_broadcast([128, HH, 128]), op=ALU.mult), [b_s], list(bp) + [b_T1])
            S_ = Sst[ci][:, si]
            P.op("dve", lambda e, S_=S_, s_=s_: e.tensor_tensor(out=S_, in0=S_, in1=fs[:, :, s_:s_ + 1].to_broadcast([128, HH, 128]), op=ALU.mult), [b_s], [b_St[ci]])
            P.op("dve", lambda e, S_=S_, T1=T1: e.tensor_tensor(out=S_, in0=S_, in1=T1[:, :, :], op=ALU.add), [b_T1], [b_St[ci]])
            for h in range(HH):
                P.op("pe", lambda e, S_=S_, h=h, s_=s_: e.matmul(ps_o[:, h * NSMP + s_:h * NSMP + s_ + 1], lhsT=S_[:, h, :], rhs=qs_[:, h, s_:s_ + 1], start=True, stop=True),
                     [b_St[ci], b_s], [b_po])
            if si == 1:
                P.dma(out=self.hgs[l, s_ - 1:s_ + 1].rearrange("s h k v -> k s h v"), in_=Sst[ci][:, :, :, :], reads=[b_St[ci]], sem=f"dst{ci}")
        self.copy("act", self.dec_o[:, :, :], ps_o[:, 0:HH * NSMP].rearrange("p (h s) -> p h s", s=NSMP), [], [b_po, self.b_deco])
        self.phase_end()

    def mem_decode_phase(self, l, W, col0):
        P = self.P
        self.phase_begin()
        KVs = [self.carve([NSMP, XW], F32) for _ in range(4)]
        b_KVs = self.pbufs("dKV", 4)
        for i in range(4):
            src = (self.cmk if i < 2 else self.cmv)[l, :, (i % 2) * 128:(i % 2 + 1) * 128, :].rearrange("s m d -> m s d")
            P.dma(out=KVs[i][:, :, :], in_=src, writes=[b_KVs[i]], sem=f"dkv{i}")
        hn, b_hn = self.dec_norm(0, l)
        cq = self.carve([2, NSMP], BF16)
        b_c = self.pbuf("dcq")

        def evac_c(nb, ps, bp, vi):
            self.copy("act", cq[:, nb, :], ps[:, 0:NSMP], [], list(bp) + [b_c])
        self.linear_fm(W, D, col0, 2, hn, b_hn, NSMP, evac_c)
        CQt = self.carve([XW], F32)
        ps_t, b_t = self.psQ[0].bitcast(BF16), self.b_psQ[0]
        for blk in range(2):
            P.op("pe", lambda e, blk=blk: e.transpose(ps_t[0:NSMP, blk * 128:(blk + 1) * 128], cq[:, blk, :], self.identb[:, :]), [b_c, self.b_const], [b_t])
        self.copy("dve", CQt[0:NSMP, :], ps_t[0:NSMP, 0:256], [], [b_t, b_c])
        prod2 = [self.carve([XW], F32) for _ in range(2)]
        prodb2 = [self.carve([XW], BF16) for _ in range(2)]
        b_pr2 = self.pbufs("dprod", 2)
        b_pb2 = self.pbufs("dprodb", 2)
        sc = self.carve([2, 4, NSMP], F32)
        b_sc = self.pbuf("dsc")
        self.sel_tiles()
        for mh in range(2):
            KV, b_KV = KVs[mh], b_KVs[mh]
            for s_ in range(NSMP):
                st, bst = self.sel(s_)
                ps, bp = self.psQ[1 + s_ % 2], self.b_psQ[1 + s_ % 2]
                P.op("pe", lambda e, ps=ps, st=st: e.matmul(ps[:, 0:256], lhsT=st[0:NSMP, :], rhs=CQt[0:NSMP, :], start=True, stop=True), [bst, b_c], [bp])
                prod, b_pr = prod2[s_ % 2], b_pr2[s_ % 2]
                P.op("dve", lambda e, ps=ps, s_=s_, prod=prod, KV=KV: e.tensor_tensor(out=prod[:, :], in0=ps[:, 0:256], in1=KV[:, s_, :], op=ALU.mult), [b_KV], [bp, b_pr])
                P.op("dve", lambda e, mh=mh, s_=s_, prod=prod: e.tensor_reduce(out=sc[:, mh, :, s_], in_=prod[:, :].rearrange("p (h d) -> p h d", d=64), axis=AX.X, op=ALU.add), [b_pr], [b_sc])
        ps_s, b_ps = self.psQ[3], self.b_psQ[3]
        for mh in range(2):
            P.op("pe", lambda e, mh=mh: e.transpose(ps_s[0:64, mh * 128:(mh + 1) * 128], sc[:, mh].rearrange("p h s -> p (h s)"), self.identf[:, :]), [b_sc, self.b_const], [b_ps])
        p_ = self.carve([NMEM], F32)
        st_ = self.carve([4], F32)
        b_p = self.pbuf("dp")
        P.op("dve", lambda e: e.reduce_max(out=st_[0:64, 0:1], in_=ps_s[0:64, 0:256], axis=AX.X), [], [b_ps, b_p])
        P.op("dve", lambda e: e.tensor_scalar(out=st_[0:64, 1:2], in0=st_[0:64, 0:1], scalar1=-0.125, scalar2=None, op0=ALU.mult), [], [b_p])
        P.op("act", lambda e: e.activation(out=p_[0:64, :], in_=ps_s[0:64, 0:256], func=AF.Exp, bias=st_[0:64, 1:2], scale=0.125, accum_out=st_[0:64, 2:3]), [], [b_ps, b_p])
        P.op("dve", lambda e: e.reciprocal(out=st_[0:64, 3:4], in_=st_[0:64, 2:3]), [], [b_p])
        P.op("dve", lambda e: e.tensor_scalar(out=p_[0:64, :], in0=p_[0:64, :], scalar1=st_[0:64, 3:4], scalar2=None, op0=ALU.mult), [], [b_p])
        PT = self.carve([2, 4, NSMP], F32)
        b_PT = self.pbuf("dPT")
        ps_b, b_pb = self.psQ[0], self.b_psQ[0]
        for mh in range(2):
            P.op("pe", lambda e, mh=mh: e.transpose(ps_b[:, mh * 64:(mh + 1) * 64], p_[0:64, mh * 128:(mh + 1) * 128], self.identf[0:64, 0:64]), [b_p, self.b_const], [b_pb])
        self.copy("act", PT[:, :, :, :], ps_b[:, 0:128].rearrange("p (a h s) -> p a h s", h=4, s=NSMP), [], [b_pb, b_PT])
        ps_x, b_px = self.psQ[3], self.b_psQ[3]
        for mh in range(2):
            KV, b_KV = KVs[2 + mh], b_KVs[2 + mh]
            for s_ in range(NSMP):
                prodb, b_pb = prodb2[s_ % 2], b_pb2[s_ % 2]
                P.op("dve", lambda e, mh=mh, s_=s_, prodb=prodb, KV=KV: e.tensor_tensor(out=prodb[:, :].rearrange("p (h d) -> p h d", d=64), in0=KV[:, s_, :].rearrange("p (h d) -> p h d", d=64),
                                                                                 in1=PT[:, mh, :, s_:s_ + 1].to_broadcast([128, 4, 64]), op=ALU.mult), [b_KV, b_PT], [b_pb])
                for blk in range(2):
                    P.op("pe", lambda e, mh=mh, s_=s_, blk=blk, prodb=prodb: e.matmul(ps_x[:, (mh * 2 + blk) * NSMP + s_:(mh * 2 + blk) * NSMP + s_ + 1], lhsT=prodb[:, blk * 128:(blk + 1) * 128],
                                                                                      rhs=self.onesb[:, 0:1], start=True, stop=True), [b_pb, self.b_const], [b_px])
        xt = self.carve([2, NSMP], F32)
        P.op("dve", lambda e: e.tensor_copy(out=xt[:, :, :], in_=ps_x[:, 0:2 * NSMP].rearrange("p (b s) -> p b s", s=NSMP)), [], [b_px, b_p])
        P.op("dve", lambda e: e.tensor_tensor(out=self.dec_x[:, :, :], in0=ps_x[:, 2 * NSMP:4 * NSMP].rearrange("p (b s) -> p b s", s=NSMP), in1=xt[:, :, :], op=ALU.add),
             [b_p], [b_px, self.b_decx])
        self.phase_end()

    def swa_decode_phase(self, jl):
        P = self.P
        l = NA + jl
        W = self.w_in_b[jl]
        self.phase_begin()
        Kx = self.carve([NSMP, 256], F32)
        Vx = self.carve([NSMP, 256], F32)
        b_Kx, b_Vx = self.pbuf("dKx"), self.pbuf("dVx")
        P.dma(out=Kx[0:127, :, :], in_=self.ssk[:, 1:128, :].rearrange("s e d -> e s d"), writes=[b_Kx], sem="dkx")
        P.dma(out=Kx[127:128, :, :], in_=self.knew_d[:, :].rearrange("(o s) d -> o s d", o=1), reads=[self.b_newd], writes=[b_Kx], sem="dkx")
        P.dma(out=Vx[0:127, :, :], in_=self.ssv[:, 1:128, :].rearrange("s e d -> e s d"), writes=[b_Vx], sem="dvx")
        P.dma(out=Vx[127:128, :, :], in_=self.vnew_d[:, :].rearrange("(o s) d -> o s d", o=1), reads=[self.b_newd], writes=[b_Vx], sem="dvx")
        if jl == 0:
            P.dma(out=self.sks.rearrange("s e d -> e s d"), in_=Kx[:, :, :], reads=[b_Kx], sem="dkx")
            P.dma(out=self.svs.rearrange("s e d -> e s d"), in_=Vx[:, :, :], reads=[b_Vx], sem="dvx")
        Ct, St, b_rope = self.rope_tables(NG - 1, 528)
        hn, b_hn = self.dec_norm(0, l)
        qr = self.carve([6, NSMP], F32)
        tq = self.carve([2, NSMP], F32)
        t2 = self.carve([NSMP], F32)
        b_q = self.pbuf("dq")

        def evac_q(nb, ps, bp, vi):
            i = nb % 2
            if vi == 0:
                P.op("dve", lambda e: e.tensor_tensor(out=tq[:, i, :], in0=ps[:, 0:NSMP], in1=Ct[:, 512:528], op=ALU.mult), [b_rope], list(bp) + [b_q])
            else:
                P.op("dve", lambda e: e.tensor_tensor(out=t2[:, :], in0=ps[:, 0:NSMP], in1=St[:, 512:528], op=ALU.mult), [b_rope], list(bp) + [b_q])
                P.op("dve", lambda e: e.tensor_tensor(out=qr[:, nb, :], in0=tq[:, i, :], in1=t2[:, :], op=ALU.add), [], [b_q])
        self.linear_fm(W, D, 0, 6, hn, b_hn, NSMP, evac_q, swap=True)
        Qt = self.carve([HW], F32)
        ps_t, b_t = self.psQ[0], self.b_psQ[0]
        ps_t2, b_t2 = self.psQ[1], self.b_psQ[1]
        for blk in range(6):
            pp, bb = (ps_t, b_t) if blk < 4 else (ps_t2, b_t2)
            P.op("pe", lambda e, blk=blk, pp=pp: e.transpose(pp[0:NSMP, (blk % 4) * 128:(blk % 4 + 1) * 128], qr[:, blk, :], self.identf[:, :]), [b_q, self.b_const], [bb])
        self.copy("dve", Qt[0:NSMP, 0:512], ps_t[0:NSMP, 0:512], [], [b_t, b_q])
        self.copy("dve", Qt[0:NSMP, 512:768], ps_t2[0:NSMP, 0:256], [], [b_t2, b_q])
        prod2 = [self.carve([HW], F32) for _ in range(2)]
        prodb2 = [self.carve([HW], BF16) for _ in range(2)]
        b_pr2 = self.pbufs("dprod", 2)
        b_pb2 = self.pbufs("dprodb", 2)
        sc = self.carve([12, NSMP], F32)
        b_sc = self.pbuf("dsc")
        self.sel_tiles()
        for s_ in range(NSMP):
            st, bst = self.sel(s_)
            ps, bp = self.acc_tile()
            P.op("pe", lambda e, ps=ps, st=st: e.matmul(ps[:, 0:512], lhsT=st[0:NSMP, :], rhs=Qt[0:NSMP, 0:512], start=True, stop=True), [bst, b_q], [bp[0]])
            P.op("pe", lambda e, ps=ps, st=st: e.matmul(ps[:, 512:768], lhsT=st[0:NSMP, :], rhs=Qt[0:NSMP, 512:768], start=True, stop=True), [bst, b_q], [bp[1]])
            prod, b_pr = prod2[s_ % 2], b_pr2[s_ % 2]
            P.op("dve", lambda e, ps=ps, s_=s_, prod=prod: e.tensor_tensor(out=prod[:, :].rearrange("p (g i d) -> p g i d", i=3, d=64), in0=ps[:, 0:768].rearrange("p (g i d) -> p g i d", i=3, d=64),
                                                                          in1=Kx[:, s_, :].rearrange("p (g d) -> p g d", d=64).unsqueeze(2).to_broadcast([128, 4, 3, 64]), op=ALU.mult),
                 [b_Kx], list(bp) + [b_pr])
            P.op("dve", lambda e, s_=s_, prod=prod: e.tensor_reduce(out=sc[:, :, s_], in_=prod[:, :].rearrange("p (h d) -> p h d", d=64), axis=AX.X, op=ALU.add), [b_pr], [b_sc])
        Hsel = self.carve([12 * NSMP], F32)
        skc = self.carve([1], F32)
        b_hs = self.pbuf("dHs")
        P.op("dve", lambda e: e.tensor_copy(out=Hsel[0:12, :].rearrange("p (h s) -> p h s", s=NSMP), in_=self.identf[0:12, 0:12].unsqueeze(2).to_broadcast([12, 12, NSMP])), [self.b_const], [b_hs])
        P.dma(out=skc[0:12, :], in_=self.sinks[jl, :].rearrange("(h o) -> h o", o=1), writes=[b_hs], sem="dsk")
        sk2 = self.carve([4], F32)
        ps_k, b_pk = self.psQ[2], self.b_psQ[2]
        P.op("pe", lambda e: e.matmul(ps_k[:, 0:1], lhsT=Hsel[0:12, 0:128], rhs=skc[0:12, 0:1], start=True, stop=True), [b_hs], [b_pk])
        P.op("pe", lambda e: e.matmul(ps_k[0:64, 1:2], lhsT=Hsel[0:12, 128:192], rhs=skc[0:12, 0:1], start=True, stop=True), [b_hs], [b_pk])
        P.op("dve", lambda e: e.memset(sk2[:, :], 0.0), [], [b_hs])
        P.op("dve", lambda e: e.tensor_copy(out=sk2[:, 0:1], in_=ps_k[:, 0:1]), [], [b_pk, b_hs])
        P.op("dve", lambda e: e.tensor_copy(out=sk2[0:64, 1:2], in_=ps_k[0:64, 1:2]), [], [b_pk, b_hs])
        P.op("dve", lambda e: e.tensor_scalar(out=sk2[:, 2:4], in0=sk2[:, 0:2], scalar1=-1.0, scalar2=None, op0=ALU.mult), [], [b_hs])
        scf = sc.rearrange("p h s -> p (h s)")
        ps_s = [self.psQ[3], self.psQ[2]]
        b_ps = [self.b_psQ[3], self.b_psQ[2]]
        p_ = self.carve([2, 128], F32)
        st_ = self.carve([2, 4], F32)
        b_p = self.pbuf("dp")
        PT = self.carve([12 * NSMP], F32)
        b_PT = self.pbuf("dPT")
        ps_b, b_pb = self.psQ[0], self.b_psQ[0]
        for rb, (r0, nr) in enumerate(((0, 128), (128, 64))):
            P.op("pe", lambda e, rb=rb, r0=r0, nr=nr: e.transpose(ps_s[rb][0:nr, 0:128], scf[:, r0:r0 + nr], self.identf[:, :]), [b_sc, self.b_const], [b_ps[rb]])
            P.op("dve", lambda e, rb=rb, nr=nr: e.reduce_max(out=st_[0:nr, rb, 0:1], in_=ps_s[rb][0:nr, 0:128], axis=AX.X), [], [b_ps[rb], b_p])
            P.op("dve", lambda e, rb=rb, nr=nr: e.tensor_scalar(out=st_[0:nr, rb, 1:2], in0=st_[0:nr, rb, 0:1], scalar1=-0.125, scalar2=sk2[0:nr, 2 + rb:3 + rb], op0=ALU.mult, op1=ALU.min),
                 [b_hs], [b_p])
            P.op("act", lambda e, rb=rb, nr=nr: e.activation(out=p_[0:nr, rb, :], in_=ps_s[rb][0:nr, 0:128], func=AF.Exp, bias=st_[0:nr, rb, 1:2], scale=0.125, accum_out=st_[0:nr, rb, 2:3]),
                 [], [b_ps[rb], b_p])
            P.op("act", lambda e, rb=rb, nr=nr: e.activation(out=st_[0:nr, rb, 3:4], in_=st_[0:nr, rb, 1:2], func=AF.Exp, bias=sk2[0:nr, rb:rb + 1]), [b_hs], [b_p])
            P.op("dve", lambda e, rb=rb, nr=nr: e.tensor_tensor(out=st_[0:nr, rb, 2:3], in0=st_[0:nr, rb, 2:3], in1=st_[0:nr, rb, 3:4], op=ALU.add), [], [b_p])
            P.op("dve", lambda e, rb=rb, nr=nr: e.reciprocal(out=st_[0:nr, rb, 3:4], in_=st_[0:nr, rb, 2:3]), [], [b_p])
            P.op("dve", lambda e, rb=rb, nr=nr: e.tensor_scalar(out=p_[0:nr, rb, :], in0=p_[0:nr, rb, :], scalar1=st_[0:nr, rb, 3:4], scalar2=None, op0=ALU.mult), [], [b_p])
            P.op("pe", lambda e, rb=rb, r0=r0, nr=nr: e.transpose(ps_b[:, r0:r0 + nr], p_[0:nr, rb, :], self.identf[0:nr, 0:nr]), [b_p, self.b_const], [b_pb])
        self.copy("act", PT[:, :], ps_b[:, 0:192], [], [b_pb, b_PT])
        PT3 = PT.rearrange("p (h s) -> p h s", s=NSMP)
        ps_x, b_px = self.psQ[3], self.b_psQ[3]
        for s_ in range(NSMP):
            prodb, b_pb = prodb2[s_ % 2], b_pb2[s_ % 2]
            P.op("dve", lambda e, s_=s_, prodb=prodb: e.tensor_tensor(out=prodb[:, :].rearrange("p (g i d) -> p g i d", i=3, d=64),
                                                                     in0=Vx[:, s_, :].rearrange("p (g d) -> p g d", d=64).unsqueeze(2).to_broadcast([128, 4, 3, 64]),
                                                                     in1=PT3[:, :, s_].rearrange("p (g i) -> p g i", i=3).unsqueeze(3).to_broadcast([128, 4, 3, 64]), op=ALU.mult),
                 [b_Vx, b_PT], [b_pb])
            for blk in range(6):
                P.op("pe", lambda e, s_=s_, blk=blk, prodb=prodb: e.matmul(ps_x[:, blk * NSMP + s_:blk * NSMP + s_ + 1], lhsT=prodb[:, blk * 128:(blk + 1) * 128], rhs=self.onesb[:, 0:1],
                                                                           start=True, stop=True), [b_pb, self.b_const], [b_px])
        self.copy("act", self.dec_o[:, :, :], ps_x[:, 0:6 * NSMP].rearrange("p (h s) -> p h s", s=NSMP), [], [b_px, self.b_deco])
        self.phase_end()

    def build(self):
        P = self.P
        self.declare_io(full=self.stop_after is None)
        self.alloc_common()
        self.phase_begin()
        self.setup_consts()
        if self.stop_after == "consts":
            self.dump("colA", self.colA[:], [128, 128], [self.b_const])
            self.dump("smask0", self.smask0[:], [128, 256], [self.b_const])
            self.dump("hmask", self.hmask[:], [128, 128], [self.b_const])
            self.dump("lbc", self.lbc[:], [128, 12], [self.b_const])
            return self.finish()
        self.load_transpose_tokens(self.xp, NTOK, self.hT, 0, lambda t: self.b_h[t // 4], "xp")
        self.load_transpose_tokens(self.xs, NSMP, self.hT, NTOK, self.b_h[NG - 1], "xs")
        self.dump("hT0", self.hT[:, :, :], [128, KC, TT], self.b_h)
        if self.stop_after == "xT":
            return self.finish()
        self.mem_kv()
        self.phase_end()
        if self.stop_after == "memkv":
            return self.finish()
        self.alloc_a_state()
        self.dec_alloc()
        for l in range(NA):
            self.P.next_epoch()
            for h in range(HH):
                self.P.op("dve", lambda e, h=h: e.memset(self.Sf[:, h, :], 0.0), [], [self.b_Sf[h]])
                self.P.op("dve", lambda e, h=h: e.memset(self.Sb[:, h, 0, :], 0.0), [], [self.b_Sb[h][0]])
                self.P.op("dve", lambda e, h=h: e.memset(self.Gacc[:, h:h + 1], 0.0), [], [self.b_G[h]])
            self.a_decode_phase(l)
            if self.stop_after == f"adec_{l}":
                self.dump("dec_o", self.dec_o[:, :, :], [128, HH, NSMP], [self.b_deco], BF16)
                return self.finish()
            for g in range(NG):
                self.a_phase1(l, g)
                if getattr(self, "cut_hit", False):
                    return self.finish()
                if self.stop_after == f"a1_{l}_{g}":
                    return self.finish()
            self.a_exchange(l)
            self.mem_decode_phase(l, self.w_in_a[l], 4 * HW)
            for g in range(NG):
                self.a_phase2(l, g)
                if self.stop_after == f"a2_{l}_{g}":
                    self.dump("hT", self.hT[:, :, :], [128, KC, TT], self.b_h)
                    return self.finish()
                self.mlp_phase(l, g)
                if self.stop_after == f"a3_{l}_{g}":
                    self.dump("hT", self.hT[:, :, :], [128, KC, TT], self.b_h)
                    return self.finish()
        self.P.next_epoch()
        self.setup_rope_consts()
        self.kv_alloc()
        for g in range(NG):
            self.kv_pass(g)
        self.kv_exchange()
        if self.stop_after == "kv":
            return self.finish()
        for l in range(NA, DEPTH):
            self.P.next_epoch()
            self.swa_decode_phase(l - NA)
            self.mem_decode_phase(l, self.w_in_b[l - NA], HW)
            for g in range(NG):
                self.b_phase(l, g)
                self.mlp_phase(l, g)
                if self.stop_after == f"b_{l}_{g}":
                    self.dump("hT", self.hT[:, :, :], [128, KC, TT], self.b_h)
                    return self.finish()
        self.P.next_epoch()
        self.write_outputs()
        return self.finish()

    def finish(self):
        self.P.emit(self.es)
        return self.nc


def make_in_maps(inp):
    f = lambda a: np.ascontiguousarray(np.asarray(a, dtype=np.float32))
    vecA = np.concatenate([f(inp[k]).reshape(32, 128) for k in ("norm_pre_mix", "norm_post_mix", "norm_pre_mlp", "norm_post_mlp")], 0)
    vecB = np.concatenate([f(inp["mem_norm"]).reshape(32, 128), f(inp["kv_norm"]).reshape(8, 128),
                           f(inp["hgrn_norm"]).reshape(12, 128), f(inp["lb_logits"]).reshape(12, 128)], 0)
    shared = {k: f(inp[k]) for k in ("w_mem_kv", "w_in_a", "w_in_b", "w_kv", "w_out", "w_up", "w_down", "sinks")}
    shared["vecA"] = f(vecA)
    shared["vecB"] = f(vecB)
    maps = []
    for c in range(NCORES):
        b, r = c // 4, c % 4
        m = dict(shared)
        m["xp"] = f(inp["x_prompt"][b, r * NTOK:(r + 1) * NTOK])
        m["xs"] = f(inp["x_sample"][c * NSMP:(c + 1) * NSMP, 0])
        m["memp"] = f(inp["mem_prompt"][b])
        m["cmk"] = f(np.asarray(inp["cache_mem_k"])[:, c * NSMP:(c + 1) * NSMP].reshape(DEPTH, NSMP, NMEM, XW))
        m["cmv"] = f(np.asarray(inp["cache_mem_v"])[:, c * NSMP:(c + 1) * NSMP].reshape(DEPTH, NSMP, NMEM, XW))
        m["shg"] = f(np.asarray(inp["state_hgrn"])[:, c * NSMP:(c + 1) * NSMP])
        m["ssk"] = f(np.asarray(inp["state_swa_k"])[c * NSMP:(c + 1) * NSMP].reshape(NSMP, WIN, 256))
        m["ssv"] = f(np.asarray(inp["state_swa_v"])[c * NSMP:(c + 1) * NSMP].reshape(NSMP, WIN, 256))
        cc = np.zeros((128, 16), np.float32)
        cc[:, 0] = r * NTOK
        cc[:, 1 + r] = 1.0
        if r > 0:
            cc[:, 5 + (r - 1)] = 1.0
        cc[:, 9] = 1.0 if r == 0 else 0.0
        m["corec"] = cc
        maps.append(m)
    return maps


def assemble(results):
    R = results
    y_p = np.stack([np.concatenate([R[b * 4 + r]["y_p"] for r in range(4)], 0) for b in range(2)], 0)
    y_s = np.concatenate([R[c]["y_s"] for c in range(NCORES)], 0)[:, None, :]
    mk = np.stack([R[b * 4]["mkp"] for b in range(2)], 1).reshape(DEPTH, 2, NMEM, 4, 64)
    mv = np.stack([R[b * 4]["mvp"] for b in range(2)], 1).reshape(DEPTH, 2, NMEM, 4, 64)
    hgp = np.stack([R[b * 4 + 3]["hgp"] for b in range(2)], 1)
    skp = np.stack([R[b * 4 + 3]["skp"] for b in range(2)], 0).reshape(2, WIN, 4, 64)
    svp = np.stack([R[b * 4 + 3]["svp"] for b in range(2)], 0).reshape(2, WIN, 4, 64)
    hgs = np.concatenate([R[c]["hgs"] for c in range(NCORES)], 1)
    sks = np.concatenate([R[c]["sks"] for c in range(NCORES)], 0).reshape(NCORES * NSMP, WIN, 4, 64)
    svs = np.concatenate([R[c]["svs"] for c in range(NCORES)], 0).reshape(NCORES * NSMP, WIN, 4, 64)
    outs = (y_p, y_s, mk, mv, hgp, skp, svp, hgs, sks, svs)
    return tuple(np.ascontiguousarray(o, dtype=np.float32) for o in outs)


def kernel(**inputs):
    kb = KB()
    nc = kb.build()
    maps = make_in_maps(inputs)
    res = run_bass_kernel_spmd(nc, maps, core_ids=list(range(NCORES)))
    return assemble(res.results)
```

```python
import math
from contextlib import ExitStack

import numpy as np
import concourse.bass as bass
import concourse.mybir as mybir
from concourse.bass_utils import run_bass_kernel_spmd

F32 = mybir.dt.float32
BF16 = mybir.dt.bfloat16
I32 = mybir.dt.int32
AF = mybir.ActivationFunctionType
ALU = mybir.AluOpType
AX = mybir.AxisListType

NCORES = 8
D = 1024
KC = 8
NTOK = 2048
NSMP = 16
TT = NTOK + NSMP
G = 512
NG = NTOK // G
DEPTH = 4
NA = 2
HH = 6
HW = 768
XW = 256
NMEM = 256
DFF = 4096
WIN = 128
EPS = 1e-6
PAST = 8192


class Buf:
    __slots__ = ("name", "w", "r")

    def __init__(self, name):
        self.name = name
        self.w = None
        self.r = {}


class Op:
    __slots__ = ("eng", "fn", "deps", "sig", "sigidx", "epoch", "is_dma", "sem", "semval", "tag")

    def __init__(self, eng, fn, epoch, is_dma=False, tag=""):
        self.eng = eng
        self.fn = fn
        self.deps = []
        self.sig = False
        self.sigidx = 0
        self.epoch = epoch
        self.is_dma = is_dma
        self.sem = None
        self.semval = 0
        self.tag = tag


class Prog:
    ENGS = ("pe", "act", "dve", "pool", "sp")

    def __init__(self, nc):
        self.nc = nc
        self.ops = []
        self.epoch = 0
        self.dma_count = {}
        self.dma_last = {}
        self.deferred = []

    def bufs(self, name, n):
        return [Buf(f"{name}{i}") for i in range(n)]

    def next_epoch(self):
        pass

    def _track(self, op, reads, writes):
        deps = {}

        def add(p, raw):
            if p is None or p is op:
                return
            if (not p.is_dma) and (not op.is_dma) and p.eng == op.eng:
                if p.eng == "pe" or not raw:
                    return
            deps[id(p)] = p

        for b in reads:
            add(b.w, True)
        for b in writes:
            add(b.w, op.eng != "pe")
            for r in b.r.values():
                if isinstance(r, list):
                    for x in r:
                        add(x, False)
                else:
                    add(r, False)
        for b in reads:
            if op.is_dma:
                b.r.setdefault("dma", []).append(op)
            else:
                b.r[op.eng] = op
        for b in writes:
            b.w = op
            b.r = {}
        for p in deps.values():
            p.sig = True
            op.deps.append(p)

    def op(self, eng, fn, reads=(), writes=(), tag=""):
        o = Op(eng, fn, self.epoch, tag=tag)
        self._track(o, reads, writes)
        self.ops.append(o)
        return o

    def dma(self, out, in_, reads=(), writes=(), sem="dma", tag="", eng="sp"):
        def fn(e, out=out, in_=in_):
            return e.dma_start(out=out, in_=in_)
        o = Op(eng, fn, self.epoch, is_dma=True, tag=tag)
        o.sem = sem
        self.dma_count[sem] = self.dma_count.get(sem, 0) + 16
        o.semval = self.dma_count[sem]
        self._track(o, reads, writes)
        prev = self.dma_last.get(sem)
        if prev is not None and all(prev is not d for d in o.deps):
            o.deps.append(prev)
        self.dma_last[sem] = o
        self.ops.append(o)
        return o

    def coll(self, fn, reads=(), writes=(), sem="cc"):
        o = Op("pool", fn, self.epoch, is_dma=True, tag="coll")
        o.sem = sem
        self.dma_count[sem] = self.dma_count.get(sem, 0) + 1
        o.semval = self.dma_count[sem]
        self._track(o, reads, writes)
        prev = self.dma_last.get(sem)
        if prev is not None and all(prev is not d for d in o.deps):
            o.deps.append(prev)
        self.dma_last[sem] = o
        self.ops.append(o)
        return o

    def defer(self, f):
        self.deferred.append(f)

    def flush(self):
        d, self.deferred = self.deferred, []
        for f in d:
            f()

    def emit(self, es):
        nc = self.nc
        self.flush()
        counters = {}
        for o in self.ops:
            if o.sig and not o.is_dma:
                k = (o.eng, o.epoch)
                counters[k] = counters.get(k, 0) + 1
                o.sigidx = counters[k]
                assert o.sigidx < 60000
        sems = {}

        def getsem(key):
            if key not in sems:
                sems[key] = es.enter_context(nc.semaphore(f"s_{key[0]}_{key[1]}" if isinstance(key, tuple) else f"d_{key}"))
            return sems[key]

        for k in counters:
            getsem(k)
        for k in self.dma_count:
            getsem(k)
        per = {e: [o for o in self.ops if o.eng == e] for e in self.ENGS}
        block = es.enter_context(nc.Block())
        final_waits = [(getsem(k), v) for k, v in self.dma_count.items()]

        def run(eng_name, e):
            waited = {}
            for o in per[eng_name]:
                for p in o.deps:
                    if p.is_dma:
                        key, val = p.sem, p.semval
                    else:
                        key, val = (p.eng, p.epoch), p.sigidx
                    if waited.get(key, 0) >= val:
                        continue
                    waited[key] = val
                    e.wait_ge(getsem(key), val)
                ins = o.fn(e)
                if o.is_dma:
                    ins.then_inc(getsem(o.sem), 16 if o.tag != "coll" else 1)
                elif o.sig:
                    ins.then_inc(getsem((o.eng, o.epoch)), 1)
            if eng_name == "sp":
                for s, v in final_waits:
                    e.wait_ge(s, v)

        @block.sync
        def _(e):
            run("sp", e)

        @block.tensor
        def _(e):
            run("pe", e)

        @block.scalar
        def _(e):
            run("act", e)

        @block.vector
        def _(e):
            run("dve", e)

        @block.gpsimd
        def _(e):
            run("pool", e)


class KB:
    def __init__(self, stop_after=None, dumps=()):
        self.stop_after = stop_after
        self.dumps = set(dumps)
        self.nc = bass.Bass("TRN2", target_bir_lowering=False)
        self.P = Prog(self.nc)
        self.es = ExitStack()
        self.slab_i = 0
        self.evac_i = 0
        self.dump_specs = {}
        self.declared = set()

    def dram_in(self, name, shape, dtype=F32):
        return self.nc.dram_tensor(name, list(shape), dtype, kind="ExternalInput").ap()

    def dram_out(self, name, shape, dtype=F32):
        return self.nc.dram_tensor(name, list(shape), dtype, kind="ExternalOutput").ap()

    def dram_tmp(self, name, shape, dtype=F32):
        return self.nc.dram_tensor(name, list(shape), dtype).ap()

    def sb(self, name, shape, dtype=F32):
        return self.es.enter_context(self.nc.sbuf_tensor(name, list(shape), dtype))

    def psum(self, name, shape, dtype=F32):
        return self.es.enter_context(self.nc.psum_tensor(name, list(shape), dtype))

    def dump(self, name, tile_ap, shape, bufs, dtype=F32):
        if name not in self.dumps:
            return
        o = self.dram_out("dbg_" + name, shape, dtype)
        self.P.dma(out=o, in_=tile_ap, reads=bufs, sem="dbg")
        self.dump_specs[name] = shape

    def evac_eng(self):
        self.evac_i += 1
        return "act" if (self.evac_i & 1) else "dve"

    def copy(self, eng, out, in_, reads, writes, tag=""):
        if eng == "act":
            return self.P.op("act", lambda e: e.activation(out=out, in_=in_, func=AF.Copy), reads, writes, tag)
        return self.P.op(eng, lambda e: e.tensor_copy(out=out, in_=in_), reads, writes, tag)

    IN_SHAPES = {
        "xp": [NTOK, D], "xs": [NSMP, D], "memp": [NMEM, D],
        "cmk": [DEPTH, NSMP, NMEM, XW], "cmv": [DEPTH, NSMP, NMEM, XW],
        "shg": [NA, NSMP, HH, 128, 128], "ssk": [NSMP, WIN, 256], "ssv": [NSMP, WIN, 256],
        "w_mem_kv": [DEPTH, D, 2 * XW], "w_in_a": [NA, D, 4 * HW + XW], "w_in_b": [DEPTH - NA, D, D],
        "w_kv": [D, 512], "w_out": [DEPTH, D, D], "w_up": [DEPTH, D, DFF], "w_down": [DEPTH, DFF, D],
        "vecA": [128, 128], "vecB": [64, 128], "sinks": [2, 12], "corec": [128, 16],
    }
    OUT_SHAPES = {
        "y_p": [NTOK, D], "y_s": [NSMP, D], "mkp": [DEPTH, NMEM, XW], "mvp": [DEPTH, NMEM, XW],
        "hgp": [NA, HH, 128, 128], "skp": [WIN, 256], "svp": [WIN, 256],
        "hgs": [NA, NSMP, HH, 128, 128], "sks": [NSMP, WIN, 256], "svs": [NSMP, WIN, 256],
    }

    def __getattr__(self, name):
        if name in KB.IN_SHAPES:
            ap = self.dram_in(name, KB.IN_SHAPES[name])
        elif name in KB.OUT_SHAPES:
            ap = self.dram_out(name, KB.OUT_SHAPES[name])
        else:
            raise AttributeError(name)
        self.__dict__[name] = ap
        self.declared.add(name)
        return ap

    def declare_io(self, full=True):
        if full:
            for n in list(KB.IN_SHAPES) + list(KB.OUT_SHAPES):
                getattr(self, n)

    def alloc_common(self):
        sb, P = self.sb, self.P
        self.hT = sb("hT", [128, KC, TT], F32)
        self.b_h = P.bufs("h", NG)
        self.b_hc = [P.bufs(f"hc{g}_", KC) for g in range(NG)]
        self.identf = sb("identf", [128, 128], F32)
        self.identb = sb("identb", [128, 128], BF16)
        self.onesb = sb("onesb", [128, 128], BF16)
        self.onesf = sb("onesf", [128, 1], F32)
        self.epsc = sb("epsc", [128, 1], F32)
        self.b_const = Buf("const")
        self.colA = sb("colA", [128, 128], F32)
        self.colB = sb("colB", [128, 64], F32)
        self.corc = sb("corc", [128, 16], F32)
        self.lbc = sb("lbc", [128, 2 * HH], F32)
        self.omlc = sb("omlc", [128, 2 * HH], F32)
        self.NST, self.NSB = 2, 3
        self.stage = [sb(f"stage{i}", [128, 8, 256], F32) for i in range(self.NST)]
        self.b_stage = P.bufs("stage", self.NST)
        self.slab = [sb(f"slab{i}", [128, 8, 256], BF16) for i in range(self.NSB)]
        self.b_slab = P.bufs("slab", self.NSB)
        self.psA = self.psum("psA", [128, 1024], F32)
        self.psB = self.psum("psB", [128, 1024], F32)
        self.psC = self.psum("psC", [128, 1024], F32)
        self.psD = self.psum("psD", [128, 1024], F32)
        self.psQ = [self.psC[:, 0:512], self.psC[:, 512:1024], self.psD[:, 0:512], self.psD[:, 512:1024]]
        self.bank = P.bufs("bank", 8)
        self.b_psA, self.b_psB = self.bank[0], self.bank[2]
        self.b_psQ = self.bank[4:8]
        self.mm_i = 0
        self.ARENA = 23040
        self.arena = sb("arena", [128, self.ARENA], F32)
        self.arena_off = 0
        self.phase_bufs = []
        self.barrier_op = None
        self.bdummy = sb("bdummy", [128, 4], F32)

    def hb(self, g):
        return [self.b_h[g]] + self.b_hc[g]

    def gcol(self, vec, l, c):
        j = vec * 32 + l * 8 + c
        return self.colA[:, j:j + 1]

    def memn_col(self, l, c):
        return self.colB[:, l * 8 + c:l * 8 + c + 1]

    def kvn_col(self, c):
        return self.colB[:, 32 + c:33 + c]

    def hgn_col(self, l, h):
        return self.colB[:, 40 + l * 6 + h:41 + l * 6 + h]

    def setup_consts(self):
        P, sb = self.P, self.sb
        bc = self.b_const
        idi = self.carve([256], F32).bitcast(I32)
        idf0 = self.carve([256], F32)
        b_t = self.pbuf("idtmp")
        P.op("pool", lambda e: e.iota(idi[:], [[1, 256]], 0, -1), [], [b_t])
        P.op("pool", lambda e: e.tensor_copy(out=idf0[:], in_=idi[:]), [b_t], [b_t])
        P.op("dve", lambda e: e.tensor_single_scalar(out=self.identf[:], in_=idf0[:, 0:128], scalar=0.0, op=ALU.is_equal), [b_t], [bc])
        P.op("dve", lambda e: e.tensor_copy(out=self.identb[:], in_=self.identf[:]), [bc], [bc])
        P.op("dve", lambda e: e.memset(self.onesb[:], 1.0), [], [bc])
        P.op("dve", lambda e: e.memset(self.onesf[:], 1.0), [], [bc])
        P.op("dve", lambda e: e.memset(self.epsc[:], EPS), [], [bc])
        self.hmask = sb("hmask", [128, 128], F32)
        P.op("dve", lambda e: e.tensor_single_scalar(out=self.hmask[:], in_=idf0[:, 0:128], scalar=0.0, op=ALU.is_ge), [b_t], [bc])
        P.op("dve", lambda e: e.memset(self.hmask[0:64, 64:128], 0.0), [bc], [bc])
        self.smask = sb("smask", [128, 256], F32)
        self.smask0 = sb("smask0", [128, 256], F32)
        t1 = self.carve([256], F32)
        P.op("dve", lambda e: e.tensor_single_scalar(out=t1[:], in_=idf0[:], scalar=1.0, op=ALU.is_ge), [b_t], [bc])
        P.op("dve", lambda e: e.tensor_single_scalar(out=self.smask[:], in_=idf0[:], scalar=128.0, op=ALU.is_le), [b_t, bc], [bc])
        P.op("dve", lambda e: e.tensor_tensor(out=self.smask[:], in0=self.smask[:], in1=t1[:], op=ALU.mult), [bc], [bc])
        P.op("dve", lambda e: e.tensor_scalar(out=self.smask[:], in0=self.smask[:], scalar1=30000.0, scalar2=-30000.0, op0=ALU.mult, op1=ALU.add), [bc], [bc])
        va = self.carve([128], F32)
        vb = self.carve([128], F32)[0:64, :]
        b_v = self.pbuf("vecs")
        P.dma(out=va[:], in_=self.vecA[:, :], writes=[b_v], sem="misc")
        P.dma(out=vb[:], in_=self.vecB[:, :], writes=[b_v], sem="misc")
        P.dma(out=self.corc[:], in_=self.corec[:, :], writes=[bc], sem="misc")
        ps = self.psQ[0]
        bq = self.b_psQ[0]
        P.op("pe", lambda e: e.transpose(ps[:, 0:128], va[:], self.identf[:]), [b_v, bc], [bq])
        P.op("pe", lambda e: e.transpose(ps[:, 128:192], vb[:], self.identf[0:64, 0:64]), [b_v, bc], [bq])
        P.op("dve", lambda e: e.tensor_copy(out=self.colA[:], in_=ps[:, 0:128]), [], [bq, bc])
        P.op("dve", lambda e: e.tensor_copy(out=self.colB[:], in_=ps[:, 128:192]), [], [bq, bc])
        d = self.carve([HH], F32)
        P.op("dve", lambda e: e.tensor_tensor(out=d[:], in0=self.colB[:, 58:64], in1=self.colB[:, 52:58], op=ALU.subtract), [bc], [b_t])
        P.op("dve", lambda e: e.memset(self.lbc[:, 0:HH], 0.0), [], [bc])
        P.op("act", lambda e: e.activation(out=self.lbc[:, HH:2 * HH], in_=d[:], func=AF.Sigmoid), [b_t, bc], [bc])
        P.op("dve", lambda e: e.tensor_scalar(out=self.omlc[:], in0=self.lbc[:], scalar1=-1.0, scalar2=1.0, op0=ALU.mult, op1=ALU.add), [bc], [bc])
        P.op("dve", lambda e: e.memset(t1[:, 0:128], -30000.0), [bc], [bc])
        P.op("dve", lambda e: e.memset(t1[:, 128:256], 0.0), [bc], [bc])
        P.op("dve", lambda e: e.scalar_tensor_tensor(out=self.smask0[:], in0=t1[:], scalar=self.corc[:, 9:10], in1=self.smask[:], op0=ALU.mult, op1=ALU.add), [bc], [bc])

    def load_transpose_tokens(self, src, ntok, dstT, dcol0, b_dst, name):
        P = self.P
        nt = (ntok + 127) // 128
        if not hasattr(self, "xin"):
            self.xin = [self.carve([D], F32) for i in range(2)]
            self.b_xin = self.pbufs("xin", 2)
            self.xin_i = 0
        for t in range(nt):
            rows = min(128, ntok - t * 128)
            i = self.xin_i % 2
            self.xin_i += 1
            xt, bx = self.xin[i], self.b_xin[i]
            P.dma(out=xt[0:rows, :], in_=src[t * 128:t * 128 + rows, :], writes=[bx], sem=f"xin{i}")
            for half in range(2):
                ps, bp = (self.psA, self.bank[0]) if half == 0 else (self.psB, self.bank[2])
                for j in range(4):
                    c = half * 4 + j
                    P.op("pe", lambda e, ps=ps, j=j, c=c, xt=xt, rows=rows: e.transpose(
                        ps[:, j * 128:j * 128 + rows], xt[0:rows, c * 128:(c + 1) * 128], self.identf[0:rows, 0:rows]),
                        [bx, self.b_const], [bp])
                eng = "act" if half == 0 else "dve"
                out = dstT[:, half * 4:half * 4 + 4, dcol0 + t * 128:dcol0 + t * 128 + rows]
                in_ = ps[:, 0:512].rearrange("p (j r) -> p j r", r=128)[:, :, 0:rows]
                self.copy(eng, out, in_, [], [bp, b_dst(t) if callable(b_dst) else b_dst])

    def get_slab(self, W2d, r0, c0, ncols=256, swap=False):
        P = self.P
        i = self.slab_i
        self.slab_i += 1
        si = i % self.NST
        st, bs = self.stage[si], self.b_stage[si]
        src = W2d[r0:r0 + 1024, c0:c0 + ncols].rearrange("(kc p) n -> p kc n", p=128)
        P.dma(out=st[:, :, 0:ncols], in_=src, writes=[bs], sem=f"stage{si}")
        outs = []
        for variant in ([False, True] if swap else [False]):
            j = getattr(self, "slabb_i", 0)
            self.slabb_i = j + 1
            sl, bl = self.slab[j % self.NSB], self.b_slab[j % self.NSB]
            if not variant:
                h = ncols // 2
                P.op("act", lambda e, sl=sl, st=st, h=h: e.activation(out=sl[:, :, 0:h], in_=st[:, :, 0:h], func=AF.Copy), [bs], [bl])
                P.op("dve", lambda e, sl=sl, st=st, h=h: e.tensor_copy(out=sl[:, :, h:ncols], in_=st[:, :, h:ncols]), [bs], [bl])
            else:
                nh = ncols // 64
                so = sl[:, :, 0:ncols].rearrange("p k (h t f) -> p k h t f", t=2, f=32)
                si_ = st[:, :, 0:ncols].rearrange("p k (h t f) -> p k h t f", t=2, f=32)
                P.op("dve", lambda e, so=so, si_=si_: e.tensor_copy(out=so[:, :, :, 0, :], in_=si_[:, :, :, 1, :]), [bs], [bl])
                P.op("act", lambda e, so=so, si_=si_: e.activation(out=so[:, :, :, 1, :], in_=si_[:, :, :, 0, :], func=AF.Copy), [bs], [bl])
            outs.append((sl, bl))
        return outs if swap else outs[0]

    def acc_tile(self):
        self.mm_i += 1
        return (self.psA, self.bank[0:2]) if (self.mm_i & 1) else (self.psB, self.bank[2:4])

    def phase_begin(self):
        self.arena_off = 0
        self.phase_bufs = []

    def carve(self, shape, dtype=F32):
        n = 1
        for d in shape:
            n *= d
        words = (n * (2 if dtype == BF16 else 4) + 3) // 4
        words = (words + 7) // 8 * 8
        assert self.arena_off + words <= self.ARENA, ("arena overflow", self.arena_off, words)
        ap = self.arena[:, self.arena_off:self.arena_off + words]
        self.arena_off += words
        if dtype == BF16:
            ap = ap.bitcast(BF16)
        ap = ap[:, 0:n]
        if len(shape) == 2:
            ap = ap.rearrange("p (a b) -> p a b", b=shape[1])
        elif len(shape) == 3:
            ap = ap.rearrange("p (a b c) -> p a b c", b=shape[1], c=shape[2])
        return ap

    def pbuf(self, name):
        b = Buf(name)
        b.w = self.barrier_op
        self.phase_bufs.append(b)
        return b

    def pbufs(self, name, n):
        return [self.pbuf(f"{name}{i}") for i in range(n)]

    def phase_end(self):
        bufs = list(self.phase_bufs)
        self.barrier_op = self.P.op("dve", lambda e: e.memset(self.bdummy[:], 0.0), [], bufs, tag="barrier")
        self.phase_bufs = []

    def linear_fm(self, W2d, d_in, col0, nblocks, xT, b_x, ncols, evac, swap=False):
        P = self.P
        kq_n = d_in // 1024
        cols = [(0, min(ncols, 512))] + ([(512, ncols)] if ncols > 512 else [])

        def mms(ps, bp, sl, bl, blk, kq):
            for kc in range(8):
                first = (kq == 0 and kc == 0)
                last = (kq == kq_n - 1 and kc == 7)
                for (a, b) in cols:
                    P.op("pe", lambda e, ps=ps, sl=sl, kc=kc, blk=blk, a=a, b=b, kk=kq * 8 + kc, first=first, last=last:
                         e.matmul(ps[:, a:b], lhsT=sl[:, kc, blk * 128:(blk + 1) * 128], rhs=xT[:, kk, a:b],
                                  start=first, stop=last), [bl] + list(b_x), list(bp))

        reqs = []
        npairs = (nblocks + 1) // 2
        for pair in range(npairs):
            nb_here = min(2, nblocks - pair * 2)
            for kq in range(kq_n):
                reqs.append((pair, kq, nb_here))
        got = {}

        def fetch(i):
            if i < len(reqs) and i not in got:
                pair, kq, nb_here = reqs[i]
                got[i] = self.get_slab(W2d, kq * 1024, col0 + pair * 256, nb_here * 128, swap=swap)
        fetch(0)
        tiles = [(self.psA, self.bank[0:2]), (self.psB, self.bank[2:4])]
        for i, (pair, kq, nb_here) in enumerate(reqs):
            cur = got.pop(i)
            if self.NSB >= 3 and not swap:
                fetch(i + 1)
            if kq_n == 1:
                variants = cur if swap else [cur]
                for vi, (sl, bl) in enumerate(variants):
                    for blk in range(nb_here):
                        ps, bp = self.acc_tile()
                        mms(ps, bp, sl, bl, blk, 0)
                        evac(pair * 2 + blk, ps, bp, vi)
                if swap:
                    fetch(i + 1)
            else:
                sl, bl = cur
                for blk in range(nb_here):
                    mms(tiles[blk][0], tiles[blk][1], sl, bl, blk, kq)
                if kq == kq_n - 1:
                    for blk in range(nb_here):
                        evac(pair * 2 + blk, tiles[blk][0], tiles[blk][1], 0)

    def norm_stats(self, srcT, C, ncols, inv_n, b_src, rstd, b_rstd, scratch=None):
        P = self.P
        if scratch is not None:
            sq, sqt, b_sqs = scratch
        else:
            if self.__dict__.get("sq_phase") is not self.phase_bufs:
                self.sq_phase = self.phase_bufs
                self.sqscr = self.carve([KC, 528], BF16)
                self.b_sq = self.pbuf("sq")
                self.sqt_ = self.carve([528], F32)
            sq, sqt, b_sqs = self.sqscr, self.sqt_, [self.b_sq]
        ps, bps = self.psD, [self.b_psQ[2], self.b_psQ[3]]
        cols = [(0, min(ncols, 512))] + ([(512, ncols)] if ncols > 512 else [])
        b_ch = [Buf(f"sqc{c}") for c in range(C)]
        for c in range(C):
            claim = list(b_sqs) if c < 2 else []
            if c % 2 == 0:
                P.op("act", lambda e, c=c: e.activation(out=sq[:, c, 0:ncols], in_=srcT[:, c, :], func=AF.Square), list(b_src), claim + [b_ch[c]])
            else:
                P.op("dve", lambda e, c=c: e.tensor_tensor(out=sq[:, c, 0:ncols], in0=srcT[:, c, :], in1=srcT[:, c, :], op=ALU.mult), list(b_src), claim + [b_ch[c]])
            for (a_, b_) in cols:
                P.op("pe", lambda e, c=c, a_=a_, b_=b_: e.matmul(ps[:, a_:b_], lhsT=self.onesb[:], rhs=sq[:, c, a_:b_], start=(c == 0), stop=(c == C - 1)),
                     [b_ch[c], self.b_const], bps)
        P.op("act", lambda e: e.activation(out=sqt[:, 0:ncols], in_=ps[:, 0:ncols], func=AF.Sqrt, bias=self.epsc[:, 0:1], scale=inv_n), [self.b_const], bps + list(b_sqs))
        P.op("dve", lambda e: e.reciprocal(out=rstd[:, 0:ncols], in_=sqt[:, 0:ncols]), list(b_sqs), [b_rstd])

    def apply_norm(self, srcT_fn, C, ncols, rstd, b_rstd, gain_fn, out_fn, b_src, b_out):
        P = self.P
        for c in range(C):
            P.op("dve", lambda e, c=c: e.scalar_tensor_tensor(out=out_fn(c), in0=srcT_fn(c), scalar=gain_fn(c), in1=rstd[:, 0:ncols],
                                                              op0=ALU.mult, op1=ALU.mult), list(b_src) + [b_rstd, self.b_const], [b_out[c] if isinstance(b_out, list) else b_out])

    def mem_kv(self):
        P, sb = self.P, self.sb
        memT = self.carve([KC, NMEM], F32)
        b_mem = self.pbuf("memT")
        self.load_transpose_tokens(self.memp, NMEM, memT, 0, b_mem, "mem")
        rstd = self.carve([NMEM], F32)
        b_r = self.pbuf("mrstd")
        self.norm_stats(memT[:, :, :], KC, NMEM, 1.0 / D, [b_mem], rstd, b_r)
        import os
        cut = 0
        if cut == 1:
            self.dump("rstd", rstd[:], [128, NMEM], [b_r])
            return
        mnT = self.carve([KC, NMEM], BF16)
        b_mn = self.pbuf("mnT")
        self.mkT = sb("mkT", [128, DEPTH, 2, NMEM], BF16)
        self.mvb = sb("mvb", [128, DEPTH, 2, XW], BF16)
        self.b_mkv = Buf("mkv")
        stg = [self.carve([256], F32) for i in range(2)]
        b_stg = self.pbufs("mstg", 2)
        si = 0
        for l in range(DEPTH):
            self.apply_norm(lambda c: memT[:, c, :], KC, NMEM, rstd, b_r, lambda c, l=l: self.memn_col(l, c),
                            lambda c: mnT[:, c, :], [b_mem], b_mn)
            if cut == 2:
                self.dump("mnT", mnT[:], [128, KC, NMEM], [b_mn], BF16)
                return
            for part in range(2):
                sl, bl = self.get_slab(self.w_mem_kv[l], 0, part * 256, 256)
                if cut == 3:
                    self.dump("slab", sl[:], [128, 8, 256], [bl], BF16)
                    return
                for mt in range(2):
                    ps, bp = self.psQ[mt], self.b_psQ[mt]
                    for kc in range(8):
                        P.op("pe", lambda e, ps=ps, kc=kc, mt=mt, sl=sl: e.matmul(ps[:, 0:256], lhsT=mnT[:, kc, mt * 128:(mt + 1) * 128],
                                                                                   rhs=sl[:, kc, :], start=(kc == 0), stop=(kc == 7)), [b_mn, bl], [bp])
                    st, bs = stg[si % 2], b_stg[si % 2]
                    si += 1
                    if cut == 4:
                        return
                    self.copy("act", st[:], ps[:, 0:256], [], [bp, bs])
                    if cut == 5:
                        return
                    dst = (self.mkp if part == 0 else self.mvp)[l, mt * 128:(mt + 1) * 128, :]
                    P.dma(out=dst, in_=st[:], reads=[bs], sem=f"mstg{(si - 1) % 2}")
                    if cut == 6:
                        return
                    if part == 1:
                        self.copy("dve", self.mvb[:, l, mt, :], st[:], [bs], [self.b_mkv])
                if part == 0:
                    for blk in range(2):
                        ps, bp = self.psQ[2 + blk], self.b_psQ[2 + blk]
                        for kc in range(8):
                            P.op("pe", lambda e, ps=ps, kc=kc, blk=blk, sl=sl: e.matmul(ps[:, 0:256], lhsT=sl[:, kc, blk * 128:(blk + 1) * 128],
                                                                                         rhs=mnT[:, kc, :], start=(kc == 0), stop=(kc == 7)), [b_mn, bl], [bp])
                        self.copy("dve", self.mkT[:, l, blk, :], ps[:, 0:256], [], [bp, self.b_mkv])
                    if cut == 7:
                        return
            if cut == 8 + l:
                return

    def alloc_a_state(self):
        sb, P = self.sb, self.P
        self.Sf = sb("Sf", [128, HH, 128], F32)
        self.Sb = sb("Sb", [128, HH, 4, 128], BF16)
        self.b_Sf = P.bufs("Sf", HH)
        self.b_Sb = [P.bufs(f"Sb{h}_", 4) for h in range(HH)]
        self.Gacc = sb("Gacc", [128, HH], F32)
        self.b_G = P.bufs("G", HH)
        self.Sst_b = sb("Sst_b", [128, HH, 128], BF16)
        self.b_Sst = Buf("Sst")
        self.spill_q = [self.dram_tmp(f"spq{g}", [128, HH, 512], BF16) for g in range(NG)]
        self.spill_o = [self.dram_tmp(f"spo{g}", [128, HH, 528], BF16) for g in range(NG)]
        self.b_spq = P.bufs("spq", NG)
        self.b_spo = P.bufs("spo", NG)

    def norm_group(self, g, ncols, vec, l, scratch=None, gain_fn=None):
        c0 = g * G
        rstd = self.carve([528], F32)
        b_r = self.pbuf("rstd")
        hnT = self.carve([KC, 528], BF16)
        b_hn = self.pbufs("hn", KC)
        self.norm_stats(self.hT[:, :, c0:c0 + ncols], KC, ncols, 1.0 / D, self.hb(g), rstd, b_r, scratch)
        self.apply_norm(lambda c: self.hT[:, c, c0:c0 + ncols], KC, ncols, rstd, b_r, gain_fn or (lambda c: self.gcol(vec, l, c)),
                        lambda c: hnT[:, c, 0:ncols], self.hb(g), b_hn)
        return hnT, b_hn

    def a_phase1(self, l, g):
        P = self.P
        last = (g == NG - 1)
        ncols = 528 if last else 512
        W = self.w_in_a[l]
        self.phase_begin()
        prod = self.carve([4, HH, 512], BF16)
        qe, ke, kd, qs = prod[:, 0], prod[:, 1], prod[:, 2], prod[:, 3]
        b_qe, b_ke, b_kd, b_qs = (self.pbufs(n, HH) for n in ("qe", "ke", "kd", "qs"))
        flat = prod.rearrange("p a h t -> p (a h t)")
        sq_scr = flat[:, 0:KC * 528].rearrange("p (c t) -> p c t", t=528)
        sqt_scr = flat[:, 2 * HH * 512:2 * HH * 512 + 1056].bitcast(F32)
        hnT, b_hn = self.norm_group(g, ncols, 0, l, scratch=(sq_scr, sqt_scr, b_qe + b_ke + b_kd))
        qT = self.carve([HH, 528], BF16)
        b_q = self.pbufs("q", HH)
        kkT = self.carve([HH, 528], BF16)
        b_kk = self.pbufs("kk", HH)
        Bc = self.carve([HH, 512], F32)
        b_Bc = self.pbufs("Bc", HH)
        Vtok = self.carve([4, HW], BF16)
        b_V = self.pbufs("V", 4)
        scr_all = self.carve([4, 528], F32)
        scr = [scr_all[:, i, :] for i in range(4)]
        b_scr = self.pbufs("scr", 4)
        lb = lambda h: self.lbc[:, l * HH + h:l * HH + h + 1]
        oml = lambda h: self.omlc[:, l * HH + h:l * HH + h + 1]

        pc = 512
        def evac_q(nb, ps, bp, vi):
            P.op("act", lambda e: e.activation(out=qT[:, nb, 0:pc], in_=ps[:, 0:pc], func=AF.Silu), [], list(bp) + [b_q[nb]])
        self.linear_fm(W, D, 0, HH, hnT, b_hn, pc, evac_q)
        import os
        cut = 0

        def docut(k):
            if cut == k:
                self.phase_end()
                self.cut_hit = True
                return True
            return False
        if docut(1):
            return

        zi = [0]

        def evac_z(nb, ps, bp, vi):
            i = zi[0] % 2
            zi[0] += 1
            sA, bA = scr[2 * i], b_scr[2 * i]
            sB, bB = scr[2 * i + 1], b_scr[2 * i + 1]
            P.op("act", lambda e: e.activation(out=sA[:, 0:pc], in_=ps[:, 0:pc], func=AF.Sigmoid), [], list(bp) + [bA])
            P.op("dve", lambda e: e.tensor_scalar(out=sB[:, 0:pc], in0=sA[:, 0:pc], scalar1=oml(nb), scalar2=lb(nb), op0=ALU.mult, op1=ALU.add),
                 [bA, self.b_const], [bB])
            P.op("pool", lambda e: e.tensor_scalar(out=kkT[:, nb, 0:pc], in0=sB[:, 0:pc], scalar1=-1.0, scalar2=1.0, op0=ALU.mult, op1=ALU.add), [bB], [b_kk[nb]])
            P.op("act", lambda e: e.activation(out=sA[:, 0:512], in_=sB[:, 0:512], func=AF.Ln), [bB], [bA])
            P.op("dve", lambda e: e.tensor_tensor_scan(out=Bc[:, nb, :], data0=self.onesf[:, 0:1].to_broadcast([128, 512]), data1=sA[:, 0:512], initial=0.0,
                                                       op0=ALU.mult, op1=ALU.add), [bA, self.b_const], [b_Bc[nb]])
        self.linear_fm(W, D, HW, HH, hnT, b_hn, pc, evac_z)
        if docut(2):
            return

        ntile = 4
        for sidx in range(3):
            sl, bl = self.get_slab(W, 0, 2 * HW + sidx * 256, 256)
            for t in range(ntile):
                rows = 128
                tc0 = t * 128
                q4 = (sidx * 5 + t) % 2
                ps, bp = self.psQ[q4], self.b_psQ[q4]
                for kc in range(8):
                    P.op("pe", lambda e, ps=ps, kc=kc, rows=rows, tc0=tc0, sl=sl: e.matmul(ps[0:rows, 0:256], lhsT=hnT[:, kc, tc0:tc0 + rows], rhs=sl[:, kc, :],
                                                                                         start=(kc == 0), stop=(kc == 7)), [bl] + b_hn, [bp])
                self.copy(self.evac_eng(), Vtok[:, t, sidx * 256:(sidx + 1) * 256], ps[:, 0:256], [], [bp, b_V[t]])

        if docut(3):
            return
        qsD = self.carve([2, 512], BF16)
        b_qsD = self.pbufs("qsD", 2)
        ex = [self.carve([512], BF16) for _ in range(4)]
        b_ex = self.pbufs("ex", 4)
        el = self.carve([HH, 8], F32)
        b_el = self.pbufs("el", HH)
        Bs = self.carve([HH, 8], F32)
        exi = [0]

        def nex():
            i = exi[0] % 4
            exi[0] += 1
            return ex[i], b_ex[i]

        def nscr():
            i = zi[0] % 4
            zi[0] += 1
            return scr[i], b_scr[i]

        for h in range(HH):
            B3 = Bc[:, h, :].rearrange("p (c j) -> p c j", j=64)
            v3 = lambda ap: ap[:, 0:512].rearrange("p (c j) -> p c j", j=64)
            d1, bd1 = nscr()
            P.op("dve", lambda e, d1=d1, B3=B3: e.tensor_tensor(out=v3(d1), in0=B3, in1=B3[:, :, 31:32].to_broadcast([128, 8, 64]), op=ALU.subtract), [b_Bc[h]], [bd1])
            e1, be1 = nex()
            P.op("act", lambda e, e1=e1, d1=d1: e.activation(out=e1[:, :], in_=d1[:, 0:512], func=AF.Exp), [bd1], [be1])
            P.op("pool", lambda e, e1=e1, h=h: e.tensor_tensor(out=qe[:, h, :], in0=qT[:, h, 0:512], in1=e1[:, :], op=ALU.mult), [be1, b_q[h]], [b_qe[h]])
            e2, be2 = nex()
            P.op("act", lambda e, e2=e2, d1=d1: e.activation(out=e2[:, :], in_=d1[:, 0:512], func=AF.Exp, scale=-1.0), [bd1], [be2])
            P.op("pool", lambda e, e2=e2, h=h: e.tensor_tensor(out=ke[:, h, :], in0=kkT[:, h, 0:512], in1=e2[:, :], op=ALU.mult), [be2, b_kk[h]], [b_ke[h]])
            d2, bd2 = nscr()
            P.op("dve", lambda e, d2=d2, B3=B3: e.tensor_tensor(out=v3(d2), in0=B3, in1=B3[:, :, 63:64].to_broadcast([128, 8, 64]), op=ALU.subtract), [b_Bc[h]], [bd2])
            e3, be3 = nex()
            P.op("act", lambda e, e3=e3, d2=d2: e.activation(out=e3[:, :], in_=d2[:, 0:512], func=AF.Exp, scale=-1.0), [bd2], [be3])
            P.op("pool", lambda e, e3=e3, h=h: e.tensor_tensor(out=kd[:, h, :], in0=kkT[:, h, 0:512], in1=e3[:, :], op=ALU.mult), [be3, b_kk[h]], [b_kd[h]])
            P.op("dve", lambda e, h=h: e.memset(Bs[:, h, 0:1], 0.0), [], [b_el[h]])
            P.op("dve", lambda e, h=h, B3=B3: e.tensor_copy(out=Bs[:, h, 1:8], in_=B3[:, 0:7, 63]), [b_Bc[h]], [b_el[h]])
            d3, bd3 = nscr()
            P.op("dve", lambda e, d3=d3, B3=B3, h=h: e.tensor_tensor(out=v3(d3), in0=B3, in1=Bs[:, h, :].unsqueeze(2).to_broadcast([128, 8, 64]), op=ALU.subtract),
                 [b_Bc[h], b_el[h]], [bd3])
            e4, be4 = nex()
            P.op("act", lambda e, e4=e4, d3=d3: e.activation(out=e4[:, :], in_=d3[:, 0:512], func=AF.Exp), [bd3], [be4])
            P.op("pool", lambda e, e4=e4, h=h: e.tensor_tensor(out=qs[:, h, :], in0=qT[:, h, 0:512], in1=e4[:, :], op=ALU.mult), [be4, b_q[h]], [b_qs[h]])
            P.op("dve", lambda e, h=h, B3=B3: e.tensor_tensor(out=el[:, h, :], in0=B3[:, :, 63], in1=Bs[:, h, :], op=ALU.subtract), [b_Bc[h], b_el[h]], [b_el[h]])
            P.op("act", lambda e, h=h: e.activation(out=el[:, h, :], in_=el[:, h, :], func=AF.Exp), [b_el[h]], [b_el[h]])
            e5, be5 = nex()
            P.op("act", lambda e, e5=e5, h=h: e.activation(out=e5[:, :], in_=Bc[:, h, :], func=AF.Exp, bias=self.Gacc[:, h:h + 1]), [b_Bc[h], self.b_G[h]], [be5])
            P.op("pool", lambda e, e5=e5, h=h: e.tensor_tensor(out=qsD[:, h % 2, :], in0=qT[:, h, 0:512], in1=e5[:, :], op=ALU.mult), [be5, b_q[h]], [b_qsD[h % 2]])
            P.dma(out=self.spill_q[g][:, h, :], in_=qsD[:, h % 2, :], reads=[b_qsD[h % 2]], writes=[self.b_spq[g]], sem=f"spq{h % 2}")
            P.op("dve", lambda e, h=h: e.tensor_tensor(out=self.Gacc[:, h:h + 1], in0=self.Gacc[:, h:h + 1], in1=Bc[:, h, 511:512], op=ALU.add), [b_Bc[h], self.b_G[h]], [self.b_G[h]])

        if docut(4):
            return
        oloc = hnT[:, 0:HH, :]
        b_ol = b_hn[0:HH]
        NR = 4 * 3
        A_t = [self.carve([2, 128], BF16) for _ in range(3)]
        b_A = self.pbufs("A", 3)
        kdt = [self.carve([2, 128], BF16) for _ in range(2)]
        b_kdt = self.pbufs("kdt", 2)
        ps_sc, b_sc = self.psQ[0], self.b_psQ[0]
        ps_o, b_o = self.psQ[1], self.b_psQ[1]
        ps_dsc = [self.psQ[2], self.psQ[3]]
        b_dsc = [self.b_psQ[2], self.b_psQ[3]]
        ps_tr, b_tr = self.psA[:, 0:512].bitcast(BF16), self.bank[0]

        def stage1(r):
            t, hp = divmod(r, 3)
            i = r % 2
            for j in range(2):
                h = hp * 2 + j
                tc = slice(t * 128, (t + 1) * 128)
                P.op("pe", lambda e, j=j, h=h, tc=tc: e.matmul(ps_sc[:, j * 128:(j + 1) * 128], lhsT=ke[:, h, tc], rhs=qe[:, h, tc], start=True, stop=True),
                     [b_ke[h], b_qe[h]], [b_sc])
                P.op("pe", lambda e, j=j, h=h, tc=tc: e.transpose(ps_tr[:, j * 128:(j + 1) * 128], kd[:, h, tc], self.identb[:]), [b_kd[h], self.b_const], [b_tr])
            ia = r % 3
            P.op("dve", lambda e, ia=ia: e.tensor_tensor(out=A_t[ia][:, :, :], in0=ps_sc[:, 0:256].rearrange("p (j t) -> p j t", t=128),
                                                         in1=self.hmask[:, :].unsqueeze(1).to_broadcast([128, 2, 128]), op=ALU.mult), [self.b_const], [b_sc, b_A[ia]])
            self.copy("act", kdt[i][:, :, :], ps_tr[:, 0:256].rearrange("p (j t) -> p j t", t=128), [], [b_tr, b_kdt[i]])

        def stage2(r):
            t, hp = divmod(r, 3)
            i = r % 2
            for j in range(2):
                h = hp * 2 + j
                for c in range(2):
                    rs = slice(c * 64, (c + 1) * 64)
                    P.op("pe", lambda e, j=j, h=h, c=c, rs=rs, t=t, i=i: e.matmul(ps_dsc[c][:, j * 128:(j + 1) * 128], lhsT=kdt[i][rs, j, :],
                                                                               rhs=Vtok[rs, t, h * 128:(h + 1) * 128], start=True, stop=True),
                         [b_kdt[i], b_V[t]], [b_dsc[c]])
            for j in range(2):
                h = hp * 2 + j
                for c in range(2):
                    n = g * 8 + t * 2 + c
                    slot = (n + 1) % 4
                    P.op("dve", lambda e, j=j, h=h, c=c, t=t: e.scalar_tensor_tensor(out=self.Sf[:, h, :], in0=self.Sf[:, h, :], scalar=el[:, h, 2 * t + c:2 * t + c + 1],
                                                                                  in1=ps_dsc[c][:, j * 128:(j + 1) * 128], op0=ALU.mult, op1=ALU.add),
                         [b_el[h]], [b_dsc[c], self.b_Sf[h]])
                    self.copy("act", self.Sb[:, h, slot, :], self.Sf[:, h, :], [self.b_Sf[h]], [self.b_Sb[h][slot]])

        def stage3(r):
            t, hp = divmod(r, 3)
            i = r % 2
            for j in range(2):
                h = hp * 2 + j
                n0 = g * 8 + t * 2
                o_ = ps_o[:, j * 128:(j + 1) * 128]
                ia = r % 3
                P.op("pe", lambda e, o_=o_, h=h, t=t, ia=ia, j=j: e.matmul(o_, lhsT=Vtok[:, t, h * 128:(h + 1) * 128], rhs=A_t[ia][:, j, :], start=True, stop=False),
                     [b_V[t], b_A[ia]], [b_o])
                for c in range(2):
                    slot = (n0 + c) % 4
                    P.op("pe", lambda e, o_=o_, h=h, t=t, c=c, slot=slot: e.matmul(o_[:, c * 64:(c + 1) * 64], lhsT=self.Sb[:, h, slot, :],
                                                                                  rhs=qs[:, h, t * 128 + c * 64:t * 128 + (c + 1) * 64], start=False, stop=(c == 1)),
                         [self.b_Sb[h][slot], b_qs[h]], [b_o])
            hp2 = hp * 2
            self.copy("act" if r % 2 else "dve", oloc[:, hp2:hp2 + 2, t * 128:(t + 1) * 128], ps_o[:, 0:256].rearrange("p (j t) -> p j t", t=128), [], [b_o, b_ol[hp2], b_ol[hp2 + 1]])

        for r in range(NR + 2):
            if r < NR:
                stage1(r)
            if 0 <= r - 1 < NR:
                stage2(r - 1)
            if 0 <= r - 2 < NR:
                stage3(r - 2)
        self.a_ctx = dict(oloc=oloc, b_ol=b_ol)
        if last:
            P.op("pool", lambda e: e.tensor_copy(out=oloc[:, :, 512:528], in_=self.dec_o[:, :, :]), [self.b_deco], b_ol)
        P.dma(out=self.spill_o[g][:, :, 0:ncols], in_=oloc[:, :, 0:ncols], reads=b_ol, writes=[self.b_spo[g]], sem="spo")
        if self.stop_after == f"a1_{l}_{g}":
            self.dump("oloc", oloc[:, :, 0:ncols], [128, HH, ncols], b_ol, BF16)
            self.dump("Sf", self.Sf[:, :, :], [128, HH, 128], self.b_Sf)
            self.dump("Bc", Bc[:, :, :], [128, HH, 512], b_Bc)
            self.dump("qT", qT[:, :, 0:ncols], [128, HH, ncols], b_q, BF16)
            self.dump("Vtok", Vtok[:, :, :], [128, 4, HW], b_V, BF16)
        self.phase_end()

    def attn_pair(self, q_ap, b_q, K_fn, b_k, V_fn, b_v, out_ap, b_out, mask=None, nsink=None, sinkc=None, scale=0.125):
        self.att_jobs.append(dict(q=q_ap, bq=list(b_q), K=K_fn, bk=list(b_k), V=V_fn, bv=list(b_v), out=out_ap, bout=list(b_out),
                                  mask=mask, nsink=nsink, sinkc=sinkc, scale=scale))

    def attn_run(self):
        P = self.P
        jobs, self.att_jobs = self.att_jobs, []
        ps_s = [self.psQ[0], self.psQ[1]]
        b_s = [self.b_psQ[0], self.b_psQ[1]]
        ps_t, b_t = self.psQ[2].bitcast(BF16), self.b_psQ[2]
        ps_o, b_o = self.psQ[3], self.b_psQ[3]

        def st1a(i):
            J, A = jobs[i], self.att[i % 2]
            b, scale = A["b"], J["scale"]
            for j in range(2):
                P.op("pe", lambda e, j=j: e.matmul(ps_s[j][:, 0:256], lhsT=J["q"][j * 64:(j + 1) * 64, :], rhs=J["K"](j), start=True, stop=True),
                     J["bq"] + J["bk"], [b_s[j]])
            for j in range(2):
                if J["mask"] is not None:
                    P.op("dve", lambda e, j=j: e.tensor_tensor(out=A["s"][:, j, :], in0=ps_s[j][:, 0:256], in1=J["mask"], op=ALU.add), [self.b_const], [b_s[j], A["bs"][j]])
                else:
                    P.op("dve", lambda e, j=j: e.tensor_copy(out=A["s"][:, j, :], in_=ps_s[j][:, 0:256]), [], [b_s[j], A["bs"][j]])
            for j in range(2):
                P.op("dve", lambda e, j=j: e.reduce_max(out=A["mx"][:, j:j + 1], in_=A["s"][:, j, :], axis=AX.X), [A["bs"][j]], [A["bst"][j]])
                if J["nsink"] is not None:
                    P.op("dve", lambda e, j=j: e.tensor_scalar(out=A["nm"][:, j:j + 1], in0=A["mx"][:, j:j + 1], scalar1=-scale, scalar2=J["nsink"][j], op0=ALU.mult, op1=ALU.min),
                         [self.b_const], [A["bst"][j]])
                else:
                    P.op("dve", lambda e, j=j: e.tensor_scalar(out=A["nm"][:, j:j + 1], in0=A["mx"][:, j:j + 1], scalar1=-scale, scalar2=None, op0=ALU.mult), [], [A["bst"][j]])

        def st1b(i):
            J, A = jobs[i], self.att[i % 2]
            scale = J["scale"]
            for j in range(2):
                P.op("act", lambda e, j=j: e.activation(out=A["s"][:, j, :], in_=A["s"][:, j, :], func=AF.Exp, bias=A["nm"][:, j:j + 1], scale=scale, accum_out=A["rs"][:, j:j + 1]),
                     [A["bst"][j]], [A["bs"][j], A["brs"][j]])
                if J["nsink"] is not None:
                    P.op("act", lambda e, j=j: e.activation(out=A["es"][:, j:j + 1], in_=A["nm"][:, j:j + 1], func=AF.Exp, bias=J["sinkc"][j]), [self.b_const, A["bst"][j]], [A["brs"][j]])
            for j in range(2):
                if J["nsink"] is not None:
                    P.op("dve", lambda e, j=j: e.tensor_tensor(out=A["rs"][:, j:j + 1], in0=A["rs"][:, j:j + 1], in1=A["es"][:, j:j + 1], op=ALU.add), [], [A["brs"][j]])
                P.op("dve", lambda e, j=j: e.reciprocal(out=A["r"][:, j:j + 1], in_=A["rs"][:, j:j + 1]), [], [A["brs"][j]])
            for j in range(2):
                P.op("act", lambda e, j=j: e.activation(out=A["pn"][:, j, :], in_=A["s"][:, j, :], func=AF.Copy, scale=A["r"][:, j:j + 1]), [A["bs"][j], A["brs"][j]], [A["bpn"]])

        def st2(i):
            A = self.att[i % 2]
            for j in range(2):
                for kc in range(2):
                    P.op("pe", lambda e, j=j, kc=kc: e.transpose(ps_t[:, (2 * j + kc) * 128:(2 * j + kc + 1) * 128], A["pn"][:, j, kc * 128:(kc + 1) * 128], self.identb[:]),
                         [A["bpn"], self.b_const], [b_t])
            self.copy("act", A["pT"][:, :, :], ps_t[:, 0:512].rearrange("p (a t) -> p a t", t=128), [], [b_t, A["bpT"]])

        def st3(i):
            J, A = jobs[i], self.att[i % 2]
            for j in range(2):
                for kc in range(2):
                    P.op("pe", lambda e, j=j, kc=kc: e.matmul(ps_o[j * 64:(j + 1) * 64, 0:128], lhsT=J["V"](j, kc), rhs=A["pT"][:, 2 * j + kc, :], start=(kc == 0), stop=(kc == 1)),
                         [A["bpT"]] + J["bv"], [b_o])
            self.copy("dve", J["out"], ps_o[:, 0:128], [], [b_o] + J["bout"])

        n = len(jobs)
        for r in range(n + 3):
            if r < n:
                st1a(r)
            if 0 <= r - 1 < n:
                st1b(r - 1)
            if 0 <= r - 2 < n:
                st2(r - 2)
            if 0 <= r - 3 < n:
                st3(r - 3)

    def attn_alloc(self):
        c = self.carve
        self.att = [dict(s=c([2, 256], F32), pn=c([2, 256], BF16), pT=c([4, 128], BF16),
                         mx=c([2], F32), nm=c([2], F32), rs=c([2], F32), es=c([2], F32), r=c([2], F32),
                         b=self.pbuf("att"), bpn=self.pbuf("attpn"), bpT=self.pbuf("attpT"),
                         bs=self.pbufs("atts", 2), bst=self.pbufs("attst", 2), brs=self.pbufs("attrs", 2)) for _ in range(2)]
        self.att_jobs = []

    def post_norm_add(self, mixT, b_mix, vec, l, g, ncols, scratch):
        P = self.P
        c0 = g * G
        rstd = self.carve([528], F32)
        b_r = self.pbuf("rstd2")
        self.norm_stats(mixT[:, :, 0:ncols], KC, ncols, 1.0 / D, b_mix, rstd, b_r, scratch)
        for c in range(KC):
            P.op("dve", lambda e, c=c: e.scalar_tensor_tensor(out=mixT[:, c, 0:ncols], in0=mixT[:, c, 0:ncols], scalar=self.gcol(vec, l, c), in1=rstd[:, 0:ncols],
                                                              op0=ALU.mult, op1=ALU.mult), [b_r, self.b_const], [b_mix[c]])
            P.op("pool" if c % 2 == 0 else "dve", lambda e, c=c: e.tensor_tensor(out=self.hT[:, c, c0:c0 + ncols], in0=self.hT[:, c, c0:c0 + ncols], in1=mixT[:, c, 0:ncols], op=ALU.add),
                 [b_mix[c]], [self.b_hc[g][c]])

    def mix_out(self, l, g, ncols, catT, b_cat, scratch):
        mixT = self.carve([KC, 528], F32)
        b_mix = self.pbufs("mix", KC)

        def evac(nb, ps, bp, vi):
            self.copy(self.evac_eng(), mixT[:, nb, 0:ncols], ps[:, 0:ncols], [], list(bp) + [b_mix[nb]])
        self.linear_fm(self.w_out[l], D, 0, KC, catT, b_cat, ncols, evac)
        self.post_norm_add(mixT, b_mix, 1, l, g, ncols, scratch)

    def mlp_phase(self, l, g):
        P = self.P
        last = (g == NG - 1)
        ncols = 528 if last else 512
        self.phase_begin()
        uT = self.carve([32, 528], BF16)
        b_u = self.pbufs("u", 32)
        uflat = uT.rearrange("p a t -> p (a t)")
        sq_scr = uflat[:, 0:KC * 528].rearrange("p (c t) -> p c t", t=528)
        sqt_scr = uflat[:, KC * 528:KC * 528 + 1056].bitcast(F32)
        scratch = (sq_scr, sqt_scr, b_u[0:10])
        hnT, b_hn = self.norm_group(g, ncols, 2, l, scratch=scratch)

        def evac_u(nb, ps, bp, vi):
            eng = self.evac_eng()
            if eng == "act":
                P.op("act", lambda e: e.activation(out=uT[:, nb, 0:ncols], in_=ps[:, 0:ncols], func=AF.Relu), [], list(bp) + [b_u[nb]])
            else:
                P.op("dve", lambda e: e.tensor_scalar(out=uT[:, nb, 0:ncols], in0=ps[:, 0:ncols], scalar1=0.0, scalar2=None, op0=ALU.max), [], list(bp) + [b_u[nb]])
            P.op("pool", lambda e: e.tensor_tensor(out=uT[:, nb, 0:ncols], in0=uT[:, nb, 0:ncols], in1=uT[:, nb, 0:ncols], op=ALU.mult), [], [b_u[nb]])
        self.linear_fm(self.w_up[l], D, 0, 32, hnT, b_hn, ncols, evac_u)
        mixT = self.carve([KC, 528], F32)
        b_mix = self.pbufs("mix", KC)

        def evac_d(nb, ps, bp, vi):
            self.copy(self.evac_eng(), mixT[:, nb, 0:ncols], ps[:, 0:ncols], [], list(bp) + [b_mix[nb]])
        self.linear_fm(self.w_down[l], DFF, 0, KC, uT, b_u, ncols, evac_d)
        self.post_norm_add(mixT, b_mix, 3, l, g, ncols, (hnT, sqt_scr, b_hn + b_u[0:10]))
        self.phase_end()

    def mem_attn_prompt(self, l, cqT, b_cq, catT, b_cat):
        for t in range(4):
            for blk in range(2):
                self.attn_pair(cqT[:, blk, t * 128:(t + 1) * 128], b_cq,
                               lambda j, blk=blk: self.mkT[j * 64:(j + 1) * 64, l, blk, :], [self.b_mkv],
                               lambda j, kc, blk=blk: self.mvb[:, l, kc, (2 * blk + j) * 64:(2 * blk + j + 1) * 64], [self.b_mkv],
                               catT[:, 6 + blk, t * 128:(t + 1) * 128], [b_cat[6 + blk]])

    def a_phase2(self, l, g):
        P = self.P
        last = (g == NG - 1)
        ncols = 528 if last else 512
        W = self.w_in_a[l]
        self.phase_begin()
        catT = self.carve([KC, 528], BF16)
        b_cat = self.pbufs("cat", KC)
        sqt_scr = self.carve([528], F32)
        hnT, b_hn = self.norm_group(g, ncols, 0, l, scratch=(catT, sqt_scr, b_cat))
        gateT = self.carve([HH, 528], BF16)
        b_gt = self.pbufs("gate", HH)
        cqT = self.carve([2, 528], BF16)
        b_cq = self.pbufs("cq", 2)

        def evac_g(nb, ps, bp, vi):
            P.op("act", lambda e: e.activation(out=gateT[:, nb, 0:ncols], in_=ps[:, 0:ncols], func=AF.Silu), [], list(bp) + [b_gt[nb]])
        self.linear_fm(W, D, 3 * HW, HH, hnT, b_hn, ncols, evac_g)

        def evac_c(nb, ps, bp, vi):
            self.copy(self.evac_eng(), cqT[:, nb, 0:ncols], ps[:, 0:ncols], [], list(bp) + [b_cq[nb]])
        self.linear_fm(W, D, 4 * HW, 2, hnT, b_hn, ncols, evac_c)
        oloc = self.carve([HH, 528], BF16)
        qsD = self.carve([HH, 512], BF16)
        b_ld = self.pbuf("ld")
        P.dma(out=oloc[:, :, 0:ncols], in_=self.spill_o[g][:, :, 0:ncols], reads=[self.b_spo[g]], writes=[b_ld], sem="ld2o")
        P.dma(out=qsD[:, :, :], in_=self.spill_q[g][:, :, :], reads=[self.b_spq[g]], writes=[b_ld], sem="ld2q")
        o = self.carve([HH, 528], F32)
        b_o = self.pbufs("o", HH)
        rst = self.carve([528], F32)
        b_rst = self.pbuf("rst")
        for h in range(HH):
            ps, bp = self.acc_tile()
            P.op("pe", lambda e, h=h, ps=ps: e.matmul(ps[:, 0:512], lhsT=self.Sst_b[:, h, :], rhs=qsD[:, h, :], start=True, stop=True), [self.b_Sst, b_ld], [bp[0]])
            P.op("dve", lambda e, h=h, ps=ps: e.tensor_tensor(out=o[:, h, 0:512], in0=ps[:, 0:512], in1=oloc[:, h, 0:512], op=ALU.add), [b_ld], [bp[0], b_o[h]])
            if last:
                P.op("pool", lambda e, h=h: e.tensor_copy(out=o[:, h, 512:528], in_=oloc[:, h, 512:528]), [b_ld], [b_o[h]])
            self.norm_stats(o[:, h:h + 1, 0:ncols], 1, ncols, 1.0 / 128, [b_o[h]], rst, b_rst, scratch=(hnT, sqt_scr, b_hn))
            P.op("dve", lambda e, h=h: e.scalar_tensor_tensor(out=o[:, h, 0:ncols], in0=o[:, h, 0:ncols], scalar=self.hgn_col(l, h), in1=rst[:, 0:ncols],
                                                              op0=ALU.mult, op1=ALU.mult), [b_rst, self.b_const], [b_o[h]])
            P.op("pool", lambda e, h=h: e.tensor_tensor(out=catT[:, h, 0:ncols], in0=o[:, h, 0:ncols], in1=gateT[:, h, 0:ncols], op=ALU.mult), [b_o[h], b_gt[h]], [b_cat[h]])
        self.attn_alloc()
        self.mem_attn_prompt(l, cqT, b_cq, catT, b_cat)
        self.attn_run()
        if last:
            P.op("pool", lambda e: e.tensor_copy(out=catT[:, 6:8, 512:528], in_=self.dec_x[:, :, :]), [self.b_decx], [b_cat[6], b_cat[7]])
        self.mix_out(l, g, ncols, catT, b_cat, (hnT, sqt_scr, b_hn))
        if self.stop_after == f"a2_{l}_{g}":
            self.dump("catT", catT[:, :, 0:ncols], [128, KC, ncols], b_cat, BF16)
        self.phase_end()

    def a_exchange(self, l):
        P = self.P
        self.phase_begin()
        X = self.carve([774], F32)
        b_X = self.pbuf("X")
        for h in range(HH):
            P.op("dve", lambda e, h=h: e.tensor_copy(out=X[:, h * 128:(h + 1) * 128], in_=self.Sf[:, h, :]), [self.b_Sf[h]], [b_X])
            P.op("dve", lambda e, h=h: e.tensor_copy(out=X[:, 768 + h:769 + h], in_=self.Gacc[:, h:h + 1]), [self.b_G[h]], [b_X])
        gin = self.dram_tmp(f"gin{l}", [128, 774])
        gout = self.dram_tmp(f"gout{l}", [512, 774])
        b_gin, b_gout = Buf("gin"), Buf("gout")
        P.dma(out=gin[:, :], in_=X[:, :], reads=[b_X], writes=[b_gin], sem="xch")
        P.coll(lambda e: e.collective_compute("AllGather", ALU.bypass, replica_groups=[[0, 1, 2, 3], [4, 5, 6, 7]], ins=[gin[:, :]], outs=[gout[:, :]]),
               reads=[b_gin], writes=[b_gout])
        Y = self.carve([4, 774], F32)
        b_Y = self.pbuf("Y")
        P.dma(out=Y[:, :, :], in_=gout.rearrange("(r p) n -> p r n", p=128), reads=[b_gout], writes=[b_Y], sem="xch")
        T = self.carve([HH, 128], F32)
        Sst = self.carve([HH, 128], F32)
        Fj = self.carve([HH], F32)
        b_T = self.pbuf("T")
        P.op("dve", lambda e: e.memset(T[:, :, :], 0.0), [], [b_T])
        P.op("dve", lambda e: e.memset(Sst[:, :, :], 0.0), [], [b_T])
        for j in range(4):
            P.op("dve", lambda e, j=j: e.scalar_tensor_tensor(out=Sst[:, :, :], in0=T[:, :, :], scalar=self.corc[:, 1 + j:2 + j], in1=Sst[:, :, :], op0=ALU.mult, op1=ALU.add),
                 [self.b_const], [b_T])
            P.op("act", lambda e, j=j: e.activation(out=Fj[:, :], in_=Y[:, j, 768:774], func=AF.Exp), [b_Y], [b_T])
            P.op("dve", lambda e, j=j: e.tensor_tensor(out=T[:, :, :], in0=T[:, :, :], in1=Fj[:, :].unsqueeze(2).to_broadcast([128, HH, 128]), op=ALU.mult), [], [b_T])
            P.op("dve", lambda e, j=j: e.tensor_tensor(out=T[:, :, :], in0=T[:, :, :], in1=Y[:, j, 0:768].rearrange("p (h v) -> p h v", v=128), op=ALU.add), [b_Y], [b_T])
        P.op("dve", lambda e: e.tensor_copy(out=self.Sst_b[:, :, :], in_=Sst[:, :, :]), [b_T], [self.b_Sst])
        P.dma(out=self.hgp[l].rearrange("h k v -> k h v"), in_=T[:, :, :], reads=[b_T], sem="xch")
        self.phase_end()

    def setup_rope_consts(self):
        P, sb = self.P, self.sb
        self.invf = sb("invf", [128, 1], F32)
        self.sgn = sb("sgn", [128, 1], F32)
        self.sinkb = sb("sinkb", [128, 24], F32)
        self.nsinkb = sb("nsinkb", [128, 24], F32)
        pi = sb("rp_i", [128, 1], I32)
        pf = sb("rp_f", [128, 1], F32)
        ge = sb("rp_ge", [128, 1], F32)
        bc = self.b_const
        for half in range(2):
            P.op("pool", lambda e, half=half: e.iota(pi[half * 64:(half + 1) * 64, :], [[0, 1]], 0, 1), [], [bc])
        P.op("pool", lambda e: e.tensor_copy(out=pf[:], in_=pi[:]), [bc], [bc])
        P.op("dve", lambda e: e.tensor_single_scalar(out=ge[:], in_=pf[:], scalar=32.0, op=ALU.is_ge), [bc], [bc])
        P.op("dve", lambda e: e.tensor_scalar(out=self.sgn[:], in0=ge[:], scalar1=2.0, scalar2=-1.0, op0=ALU.mult, op1=ALU.add), [bc], [bc])
        P.op("dve", lambda e: e.scalar_tensor_tensor(out=pf[:], in0=ge[:], scalar=-32.0, in1=pf[:], op0=ALU.mult, op1=ALU.add), [bc], [bc])
        P.op("act", lambda e: e.activation(out=self.invf[:], in_=pf[:], func=AF.Exp, scale=-math.log(10000.0) / 32.0), [bc], [bc])
        P.dma(out=self.sinkb[:], in_=self.sinks.rearrange("a b -> (a b)").partition_broadcast(128), writes=[bc], sem="misc")
        P.op("dve", lambda e: e.tensor_scalar(out=self.nsinkb[:], in0=self.sinkb[:], scalar1=-1.0, scalar2=None, op0=ALU.mult), [bc], [bc])

    def rope_tables(self, g, ncols):
        P = self.P
        Ct = self.carve([528], F32)
        St = self.carve([528], F32)
        pi = self.carve([528], F32).bitcast(I32)
        y = self.carve([528], F32)
        kf = self.carve([528], F32)
        b = self.pbuf("rope")
        P.op("pool", lambda e: e.iota(pi[:, :], [[1, 528]], g * G, 0), [], [b])
        P.op("pool", lambda e: e.tensor_copy(out=y[:, :], in_=pi[:, :]), [b], [b])
        P.op("dve", lambda e: e.tensor_scalar(out=y[:, :], in0=y[:, :], scalar1=self.corc[:, 0:1], scalar2=None, op0=ALU.add), [b, self.b_const], [b])
        if ncols > 512:
            P.op("dve", lambda e: e.memset(y[:, 512:528], float(PAST)), [b], [b])
        P.op("dve", lambda e: e.tensor_scalar(out=y[:, :], in0=y[:, :], scalar1=self.invf[:, 0:1], scalar2=None, op0=ALU.mult), [b, self.b_const], [b])
        P.op("dve", lambda e: e.tensor_scalar(out=y[:, :], in0=y[:, :], scalar1=1.0 / (2 * math.pi), scalar2=0.5, op0=ALU.mult, op1=ALU.add), [b], [b])
        for which, out in ((0, St), (1, Ct)):
            if which == 1:
                P.op("dve", lambda e: e.tensor_scalar(out=y[:, :], in0=y[:, :], scalar1=0.25, scalar2=None, op0=ALU.add), [b], [b])
            P.op("dve", lambda e: e.tensor_copy(out=pi[:, :], in_=y[:, :]), [b], [b])
            P.op("dve", lambda e: e.tensor_copy(out=kf[:, :], in_=pi[:, :]), [b], [b])
            P.op("dve", lambda e: e.tensor_tensor(out=kf[:, :], in0=y[:, :], in1=kf[:, :], op=ALU.subtract), [b], [b])
            P.op("dve", lambda e, out=out: e.tensor_single_scalar(out=out[:, :], in_=kf[:, :], scalar=0.0, op=ALU.is_lt), [b], [b])
            P.op("dve", lambda e, out=out: e.tensor_tensor(out=kf[:, :], in0=kf[:, :], in1=out[:, :], op=ALU.add), [b], [b])
            P.op("dve", lambda e: e.tensor_scalar(out=kf[:, :], in0=kf[:, :], scalar1=-0.5, scalar2=2 * math.pi, op0=ALU.add, op1=ALU.mult), [b], [b])
            if which == 0:
                P.op("act", lambda e, out=out: e.activation(out=out[:, :], in_=kf[:, :], func=AF.Sin), [b], [b])
                P.op("dve", lambda e, out=out: e.tensor_scalar(out=out[:, :], in0=out[:, :], scalar1=self.sgn[:, 0:1], scalar2=None, op0=ALU.mult), [b, self.b_const], [b])
            else:
                P.op("act", lambda e, out=out: e.activation(out=out[:, :], in_=kf[:, :], func=AF.Sin), [b], [b])
        return Ct, St, b

    def kv_pass(self, g):
        P = self.P
        last = (g == NG - 1)
        ncols = 528 if last else 512
        c0 = g * G
        self.phase_begin()
        Ct, St, b_rope = self.rope_tables(g, ncols)
        hnT, b_hn = self.norm_group(g, ncols, 0, 0, gain_fn=self.kvn_col)
        (sn, bn), (ss, bs) = self.get_slab(self.w_kv, 0, 0, 256, swap=True)
        Kr = self.carve([4, 528], BF16)
        b_K = self.pbufs("Kr", 4)
        t1 = self.carve([528], F32)
        t2 = self.carve([528], F32)
        b_t = self.pbuf("kt")
        cols = [(0, 512)] + ([(512, 528)] if last else [])
        if last:
            Kf = self.carve([4, 128 + NSMP], F32)
            b_Kf = self.pbuf("Kf")
        wdup = self.carve([8, 8, 128], BF16)
        b_wd = self.pbufs("wdup", 8)
        for vi, (sl, bl) in enumerate(((sn, bn), (ss, bs))):
            for hd in range(4):
                w = wdup[:, vi * 4 + hd]
                for half in range(2):
                    P.op("pool", lambda e, w=w, sl=sl, hd=hd, half=half: e.tensor_copy(out=w[:, :, half * 64:(half + 1) * 64], in_=sl[:, :, hd * 64:(hd + 1) * 64]),
                         [bl], [b_wd[vi * 4 + hd]])
        for hd in range(4):
            tiles = []
            for vi in range(2):
                ps, bp = self.acc_tile()
                w, bw = wdup[:, vi * 4 + hd], b_wd[vi * 4 + hd]
                for kc in range(8):
                    for (a, b) in cols:
                        P.op("pe", lambda e, ps=ps, w=w, kc=kc, a=a, b=b: e.matmul(ps[:, a:b], lhsT=w[:, kc, :], rhs=hnT[:, kc, a:b],
                                                                                 start=(kc == 0), stop=(kc == 7)), [bw] + b_hn, list(bp))
                tiles.append((ps, bp))
            (pn, bpn), (psw, bpsw) = tiles
            P.op("dve", lambda e, pn=pn: e.tensor_tensor(out=t1[:, 0:ncols], in0=pn[:, 0:ncols], in1=Ct[:, 0:ncols], op=ALU.mult), [b_rope], list(bpn) + [b_t])
            P.op("dve", lambda e, psw=psw: e.tensor_tensor(out=t2[:, 0:ncols], in0=psw[:, 0:ncols], in1=St[:, 0:ncols], op=ALU.mult), [b_rope], list(bpsw) + [b_t])
            P.op("pool", lambda e, hd=hd: e.tensor_tensor(out=Kr[:, hd, 0:ncols], in0=t1[:, 0:ncols], in1=t2[:, 0:ncols], op=ALU.add), [b_t], [b_K[hd]])
            if last:
                P.op("pool", lambda e, hd=hd: e.tensor_tensor(out=Kf[:, hd, :], in0=t1[:, 384:528], in1=t2[:, 384:528], op=ALU.add), [b_t], [b_Kf])
        P.dma(out=self.kT_d[:, :, c0:c0 + ncols], in_=Kr[:, :, 0:ncols], reads=b_K, writes=[self.b_kTd], sem="kvst")
        sv, bv = self.get_slab(self.w_kv, 0, 256, 256)
        Vt = self.carve([5, 256], BF16)
        b_Vt = self.pbuf("Vt")
        if last:
            Vf = self.carve([2, 256], F32)
            b_Vf = self.pbuf("Vf")
        for t in range(4 + (1 if last else 0)):
            rows = 128 if t < 4 else NSMP
            ps, bp = self.psQ[t % 2], self.b_psQ[t % 2]
            for kc in range(8):
                P.op("pe", lambda e, ps=ps, kc=kc, rows=rows, t=t: e.matmul(ps[0:rows, 0:256], lhsT=hnT[:, kc, t * 128:t * 128 + rows], rhs=sv[:, kc, :],
                                                                           start=(kc == 0), stop=(kc == 7)), [bv] + b_hn, [bp])
            if last and t >= 3:
                self.copy("act", Vf[0:rows, t - 3, :], ps[0:rows, 0:256], [], [bp, b_Vf])
                self.copy("dve", Vt[0:rows, t, :], Vf[0:rows, t - 3, :], [b_Vf], [b_Vt])
            else:
                self.copy(self.evac_eng(), Vt[0:rows, t, :], ps[0:rows, 0:256], [], [bp, b_Vt])
        P.dma(out=self.v_d[c0:c0 + 512, :].rearrange("(t p) n -> p t n", p=128), in_=Vt[:, 0:4, :], reads=[b_Vt], writes=[self.b_vd], sem="kvst")
        if last:
            P.dma(out=self.svp[:, :], in_=Vf[:, 0, :], reads=[b_Vf], sem="kvo")
            P.dma(out=self.vnew_d[:, :], in_=Vf[0:NSMP, 1, :], reads=[b_Vf], writes=[self.b_newd], sem="kvo")
            Kt = self.carve([2, 256], F32)
            b_Kt = self.pbuf("Kt")
            ps, bp = self.psQ[2], self.b_psQ[2]
            for hd in range(4):
                P.op("pe", lambda e, hd=hd: e.transpose(ps[:, hd * 64:(hd + 1) * 64], Kf[0:64, hd, 0:128], self.identf[0:64, 0:64]), [b_Kf, self.b_const], [bp])
            self.copy("act", Kt[:, 0, :], ps[:, 0:256], [], [bp, b_Kt])
            ps2, bp2 = self.psQ[3], self.b_psQ[3]
            for hd in range(4):
                P.op("pe", lambda e, hd=hd: e.transpose(ps2[0:NSMP, hd * 64:(hd + 1) * 64], Kf[0:64, hd, 128:128 + NSMP], self.identf[0:64, 0:64]), [b_Kf, self.b_const], [bp2])
            self.copy("dve", Kt[0:NSMP, 1, :], ps2[0:NSMP, 0:256], [], [bp2, b_Kt])
            P.dma(out=self.skp[:, :], in_=Kt[:, 0, :], reads=[b_Kt], sem="kvo")
            P.dma(out=self.knew_d[:, :], in_=Kt[0:NSMP, 1, :], reads=[b_Kt], writes=[self.b_newd], sem="kvo")
            P.dma(out=self.kTnew_d[:, :, :], in_=Kr[:, :, 512:528], reads=b_K, writes=[self.b_newd], sem="kvo")
            P.dma(out=self.hin[:, 0:512].rearrange("p (h t) -> p h t", t=128), in_=Kr[:, :, 384:512], reads=b_K, writes=[self.b_hin], sem="kvo")
            P.dma(out=self.hin[:, 512:768], in_=Vt[:, 3, :], reads=[b_Vt], writes=[self.b_hin], sem="kvo")
        self.phase_end()

    def kv_alloc(self):
        P = self.P
        self.kT_d = self.dram_tmp("kT_d", [128, 4, NTOK + NSMP], BF16)
        self.v_d = self.dram_tmp("v_d", [NTOK, 256], BF16)
        self.knew_d = self.dram_tmp("knew_d", [NSMP, 256], F32)
        self.vnew_d = self.dram_tmp("vnew_d", [NSMP, 256], F32)
        self.kTnew_d = self.dram_tmp("kTnew_d", [128, 4, NSMP], BF16)
        self.hin = self.dram_tmp("hin", [128, 768], BF16)
        self.hout = self.dram_tmp("hout", [512, 768], BF16)
        self.b_kTd, self.b_vd, self.b_newd, self.b_hin, self.b_hout = Buf("kTd"), Buf("vd"), Buf("newd"), Buf("hin"), Buf("hout")

    def kv_exchange(self):
        P = self.P
        P.coll(lambda e: e.collective_compute("AllGather", ALU.bypass, replica_groups=[[0, 1, 2, 3], [4, 5, 6, 7]], ins=[self.hin[:, :]], outs=[self.hout[:, :]]),
               reads=[self.b_hin], writes=[self.b_hout])

    def b_phase(self, l, g):
        P = self.P
        jl = l - NA
        last = (g == NG - 1)
        ncols = 528 if last else 512
        c0 = g * G
        W = self.w_in_b[jl]
        self.phase_begin()
        Ct, St, b_rope = self.rope_tables(g, ncols)
        catT = self.carve([KC, 528], BF16)
        b_cat = self.pbufs("cat", KC)
        sqt_scr = self.carve([528], F32)
        hnT, b_hn = self.norm_group(g, ncols, 0, l, scratch=(catT, sqt_scr, b_cat))
        qr = self.carve([6, 528], BF16)
        b_qr = self.pbufs("qr", 6)
        cqT = self.carve([2, 528], BF16)
        b_cq = self.pbufs("cq", 2)
        tq = self.carve([2, 528], F32)
        b_tq = self.pbufs("tq", 2)
        t2 = self.carve([528], F32)
        b_t2 = self.pbuf("t2")

        def evac_q(nb, ps, bp, vi):
            i = nb % 2
            if vi == 0:
                P.op("dve", lambda e: e.tensor_tensor(out=tq[:, i, 0:ncols], in0=ps[:, 0:ncols], in1=Ct[:, 0:ncols], op=ALU.mult), [b_rope], list(bp) + [b_tq[i]])
            else:
                P.op("dve", lambda e: e.tensor_tensor(out=t2[:, 0:ncols], in0=ps[:, 0:ncols], in1=St[:, 0:ncols], op=ALU.mult), [b_rope], list(bp) + [b_t2])
                P.op("pool", lambda e: e.tensor_tensor(out=qr[:, nb, 0:ncols], in0=tq[:, i, 0:ncols], in1=t2[:, 0:ncols], op=ALU.add), [b_tq[i], b_t2], [b_qr[nb]])
        self.linear_fm(W, D, 0, 6, hnT, b_hn, ncols, evac_q, swap=True)

        def evac_c(nb, ps, bp, vi):
            self.copy(self.evac_eng(), cqT[:, nb, 0:ncols], ps[:, 0:ncols], [], list(bp) + [b_cq[nb]])
        self.linear_fm(W, D, HW, 2, hnT, b_hn, ncols, evac_c)
        Kw = self.carve([4, 640], BF16)
        Vw = self.carve([5, 256], BF16)
        b_Kw, b_Vw = self.pbuf("Kw"), self.pbuf("Vw")
        if g == 0:
            Yh = self.carve([4, 768], BF16)
            b_Yh = self.pbuf("Yh")
            P.dma(out=Yh[:, :, :], in_=self.hout.rearrange("(r p) n -> p r n", p=128), reads=[self.b_hout], writes=[b_Yh], sem="ldk")
            hk = self.carve([768], BF16)
            P.op("dve", lambda e: e.tensor_scalar(out=hk[:, :], in0=Yh[:, 0, :], scalar1=self.corc[:, 5:6], scalar2=None, op0=ALU.mult), [b_Yh, self.b_const], [b_Yh])
            for j in range(1, 4):
                P.op("dve", lambda e, j=j: e.scalar_tensor_tensor(out=hk[:, :], in0=Yh[:, j, :], scalar=self.corc[:, 5 + j:6 + j], in1=hk[:, :], op0=ALU.mult, op1=ALU.add),
                     [self.b_const], [b_Yh])
            P.op("pool", lambda e: e.tensor_copy(out=Kw[:, :, 0:128], in_=hk[:, 0:512].rearrange("p (h t) -> p h t", t=128)), [b_Yh], [b_Kw])
            P.op("pool", lambda e: e.tensor_copy(out=Vw[:, 0, :], in_=hk[:, 512:768]), [b_Yh], [b_Vw])
            P.dma(out=Kw[:, :, 128:640], in_=self.kT_d[:, :, 0:512], reads=[self.b_kTd], writes=[b_Kw], sem="ldk")
            P.dma(out=Vw[:, 1:5, :], in_=self.v_d[0:512, :].rearrange("(t p) n -> p t n", p=128), reads=[self.b_vd], writes=[b_Vw], sem="ldv")
        else:
            P.dma(out=Kw[:, :, :], in_=self.kT_d[:, :, c0 - 128:c0 + 512], reads=[self.b_kTd], writes=[b_Kw], sem="ldk")
            P.dma(out=Vw[:, :, :], in_=self.v_d[c0 - 128:c0 + 512, :].rearrange("(t p) n -> p t n", p=128), reads=[self.b_vd], writes=[b_Vw], sem="ldv")
        self.attn_alloc()
        for t in range(4):
            mask = self.smask0[:, :] if (g == 0 and t == 0) else self.smask[:, :]
            for blk in range(6):
                hs = [2 * blk, 2 * blk + 1]
                self.attn_pair(qr[:, blk, t * 128:(t + 1) * 128], [b_qr[blk]],
                               lambda j, hs=hs, t=t: Kw[j * 64:(j + 1) * 64, hs[j] // 3, t * 128:t * 128 + 256], [b_Kw],
                               lambda j, kc, hs=hs, t=t: Vw[:, t + kc, (hs[j] // 3) * 64:(hs[j] // 3 + 1) * 64], [b_Vw],
                               catT[:, blk, t * 128:(t + 1) * 128], [b_cat[blk]], mask=mask,
                               nsink=[self.nsinkb[:, jl * 12 + h:jl * 12 + h + 1] for h in hs], sinkc=[self.sinkb[:, jl * 12 + h:jl * 12 + h + 1] for h in hs])
        self.mem_attn_prompt(l, cqT, b_cq, catT, b_cat)
        self.attn_run()
        if last:
            P.op("pool", lambda e: e.tensor_copy(out=catT[:, 0:6, 512:528], in_=self.dec_o[:, :, :]), [self.b_deco], b_cat[0:6])
            P.op("pool", lambda e: e.tensor_copy(out=catT[:, 6:8, 512:528], in_=self.dec_x[:, :, :]), [self.b_decx], [b_cat[6], b_cat[7]])
        self.mix_out(l, g, ncols, catT, b_cat, (hnT, sqt_scr, b_hn))
        self.phase_end()

    def write_outputs(self):
        P = self.P
        self.phase_begin()
        st = [self.carve([D], F32) for _ in range(2)]
        b_st = self.pbufs("yst", 2)
        for t in range(17):
            rows = 128 if t < 16 else NSMP
            col0 = t * 128
            i = t % 2
            for half in range(2):
                ps, bp = (self.psA, self.bank[0]) if half == 0 else (self.psB, self.bank[2])
                for j in range(4):
                    c = half * 4 + j
                    P.op("pe", lambda e, ps=ps, j=j, c=c, rows=rows, col0=col0: e.transpose(ps[0:rows, j * 128:(j + 1) * 128], self.hT[:, c, col0:col0 + rows], self.identf[:, :]),
                         self.hb(min(t // 4, NG - 1)) + [self.b_const], [bp])
                self.copy("act" if half == 0 else "dve", st[i][0:rows, half * 512:(half + 1) * 512], ps[0:rows, 0:512], [], [bp, b_st[i]])
            dst = self.y_p[col0:col0 + 128, :] if t < 16 else self.y_s[:, :]
            P.dma(out=dst, in_=st[i][0:rows, :], reads=[b_st[i]], sem=f"yst{i}")
        self.phase_end()

    def dec_alloc(self):
        self.dec_o = self.sb("dec_o", [128, HH, NSMP], BF16)
        self.dec_x = self.sb("dec_x", [128, 2, NSMP], BF16)
        self.b_deco, self.b_decx = Buf("dec_o"), Buf("dec_x")

    def dec_norm(self, vec, l):
        rstd = self.carve([NSMP], F32)
        b_r = self.pbuf("drstd")
        hn = self.carve([KC, NSMP], BF16)
        b_hn = self.pbufs("dhn", KC)
        g = NG - 1
        self.norm_stats(self.hT[:, :, NTOK:TT], KC, NSMP, 1.0 / D, self.hb(g), rstd, b_r)
        self.apply_norm(lambda c: self.hT[:, c, NTOK:TT], KC, NSMP, rstd, b_r, lambda c: self.gcol(vec, l, c), lambda c: hn[:, c, :], self.hb(g), b_hn)
        return hn, b_hn

    def sel_tiles(self):
        self.selt = self.carve([NSMP, 128], F32)
        self.b_sel = self.pbuf("sel")
        for s_ in range(NSMP):
            self.P.op("dve", lambda e, s_=s_, selt=self.selt: e.tensor_copy(out=selt[0:NSMP, s_, :], in_=self.identf[0:NSMP, s_:s_ + 1].to_broadcast([NSMP, 128])),
                      [self.b_const], [self.b_sel])

    def sel(self, s_):
        return self.selt[:, s_, :], self.b_sel

    def a_decode_phase(self, l):
        P = self.P
        W = self.w_in_a[l]
        self.phase_begin()
        hn, b_hn = self.dec_norm(0, l)
        qs_ = self.carve([HH, NSMP], F32)
        fs = self.carve([HH, NSMP], F32)
        kks = self.carve([HH, NSMP], F32)
        tmp = self.carve([NSMP], F32)
        Vs = self.carve([HW], F32)
        b_s = self.pbuf("dsm")
        lb = lambda h: self.lbc[:, l * HH + h:l * HH + h + 1]
        oml = lambda h: self.omlc[:, l * HH + h:l * HH + h + 1]

        def evac_q(nb, ps, bp, vi):
            P.op("act", lambda e: e.activation(out=qs_[:, nb, :], in_=ps[:, 0:NSMP], func=AF.Silu), [], list(bp) + [b_s])
        self.linear_fm(W, D, 0, HH, hn, b_hn, NSMP, evac_q)

        def evac_z(nb, ps, bp, vi):
            P.op("act", lambda e: e.activation(out=tmp[:, :], in_=ps[:, 0:NSMP], func=AF.Sigmoid), [], list(bp) + [b_s])
            P.op("dve", lambda e: e.tensor_scalar(out=fs[:, nb, :], in0=tmp[:, :], scalar1=oml(nb), scalar2=lb(nb), op0=ALU.mult, op1=ALU.add), [self.b_const], [b_s])
            P.op("dve", lambda e: e.tensor_scalar(out=kks[:, nb, :], in0=fs[:, nb, :], scalar1=-1.0, scalar2=1.0, op0=ALU.mult, op1=ALU.add), [], [b_s])
        self.linear_fm(W, D, HW, HH, hn, b_hn, NSMP, evac_z)
        for sidx in range(3):
            sl, bl = self.get_slab(W, 0, 2 * HW + sidx * 256, 256)
            ps, bp = self.psQ[sidx % 2], self.b_psQ[sidx % 2]
            for kc in range(8):
                P.op("pe", lambda e, ps=ps, kc=kc, sl=sl: e.matmul(ps[0:NSMP, 0:256], lhsT=hn[:, kc, :], rhs=sl[:, kc, :], start=(kc == 0), stop=(kc == 7)), [bl] + b_hn, [bp])
            self.copy("dve", Vs[0:NSMP, sidx * 256:(sidx + 1) * 256], ps[0:NSMP, 0:256], [], [bp, b_s])
        Sst = [self.carve([2, HH, 128], F32) for _ in range(4)]
        b_St = self.pbufs("dSt", 4)
        T1_2 = [self.carve([HH, 128], F32) for _ in range(2)]
        b_T1_2 = self.pbufs("dT1", 2)
        self.sel_tiles()
        ps_o, b_po = self.psQ[2], self.b_psQ[2]
        for s_ in range(NSMP):
            ci, si = (s_ // 2) % 4, s_ % 2
            if si == 0:
                P.dma(out=Sst[ci][:, :, :, :], in_=self.shg[l, s_:s_ + 2].rearrange("s h k v -> k s h v"), writes=[b_St[ci]], sem=f"dst{ci}")
            st, bst = self.sel(s_)
            ps, bp = self.acc_tile()
            P.op("pe", lambda e, ps=ps, st=st: e.matmul(ps[:, 0:512], lhsT=st[0:NSMP, :], rhs=Vs[0:NSMP, 0:512], start=True, stop=True), [bst, b_s], [bp[0]])
            P.op("pe", lambda e, ps=ps, st=st: e.matmul(ps[:, 512:768], lhsT=st[0:NSMP, :], rhs=Vs[0:NSMP, 512:768], start=True, stop=True), [bst, b_s], [bp[1]])
            T1, b_T1 = T1_2[s_ % 2], b_T1_2[s_ % 2]
            P.op("dve", lambda e, ps=ps, s_=s_, T1=T1: e.tensor_tensor(out=T1[:, :, :], in0=ps[:, 0:768].rearrange("p (h v) -> p h v", v=128),
                                                                      in1=kks[:, :, s_:s_ + 1].to_broadcast([128, HH, 128]), op=ALU.mult), [b_s], list(bp) + [b_T1])
            S_ = Sst[ci][:, si]
            P.op("dve", lambda e, S_=S_, s_=s_: e.tensor_tensor(out=S_, in0=S_, in1=fs[:, :, s_:s_ + 1].to_broadcast([128, HH, 128]), op=ALU.mult), [b_s], [b_St[ci]])
            P.op("dve", lambda e, S_=S_, T1=T1: e.tensor_tensor(out=S_, in0=S_, in1=T1[:, :, :], op=ALU.add), [b_T1], [b_St[ci]])
            for h in range(HH):
                P.op("pe", lambda e, S_=S_, h=h, s_=s_: e.matmul(ps_o[:, h * NSMP + s_:h * NSMP + s_ + 1], lhsT=S_[:, h, :], rhs=qs_[:, h, s_:s_ + 1], start=True, stop=True),
                     [b_St[ci], b_s], [b_po])
            if si == 1:
                P.dma(out=self.hgs[l, s_ - 1:s_ + 1].rearrange("s h k v -> k s h v"), in_=Sst[ci][:, :, :, :], reads=[b_St[ci]], sem=f"dst{ci}")
        self.copy("act", self.dec_o[:, :, :], ps_o[:, 0:HH * NSMP].rearrange("p (h s) -> p h s", s=NSMP), [], [b_po, self.b_deco])
        self.phase_end()

    def mem_decode_phase(self, l, W, col0):
        P = self.P
        self.phase_begin()
        KVs = [self.carve([NSMP, XW], F32) for _ in range(4)]
        b_KVs = self.pbufs("dKV", 4)
        for i in range(4):
            src = (self.cmk if i < 2 else self.cmv)[l, :, (i % 2) * 128:(i % 2 + 1) * 128, :].rearrange("s m d -> m s d")
            P.dma(out=KVs[i][:, :, :], in_=src, writes=[b_KVs[i]], sem=f"dkv{i}")
        hn, b_hn = self.dec_norm(0, l)
        cq = self.carve([2, NSMP], BF16)
        b_c = self.pbuf("dcq")

        def evac_c(nb, ps, bp, vi):
            self.copy("act", cq[:, nb, :], ps[:, 0:NSMP], [], list(bp) + [b_c])
        self.linear_fm(W, D, col0, 2, hn, b_hn, NSMP, evac_c)
        CQt = self.carve([XW], F32)
        ps_t, b_t = self.psQ[0].bitcast(BF16), self.b_psQ[0]
        for blk in range(2):
            P.op("pe", lambda e, blk=blk: e.transpose(ps_t[0:NSMP, blk * 128:(blk + 1) * 128], cq[:, blk, :], self.identb[:, :]), [b_c, self.b_const], [b_t])
        self.copy("dve", CQt[0:NSMP, :], ps_t[0:NSMP, 0:256], [], [b_t, b_c])
        prod2 = [self.carve([XW], F32) for _ in range(2)]
        prodb2 = [self.carve([XW], BF16) for _ in range(2)]
        b_pr2 = self.pbufs("dprod", 2)
        b_pb2 = self.pbufs("dprodb", 2)
        sc = self.carve([2, 4, NSMP], F32)
        b_sc = self.pbuf("dsc")
        self.sel_tiles()
        for mh in range(2):
            KV, b_KV = KVs[mh], b_KVs[mh]
            for s_ in range(NSMP):
                st, bst = self.sel(s_)
                ps, bp = self.psQ[1 + s_ % 2], self.b_psQ[1 + s_ % 2]
                P.op("pe", lambda e, ps=ps, st=st: e.matmul(ps[:, 0:256], lhsT=st[0:NSMP, :], rhs=CQt[0:NSMP, :], start=True, stop=True), [bst, b_c], [bp])
                prod, b_pr = prod2[s_ % 2], b_pr2[s_ % 2]
                P.op("dve", lambda e, ps=ps, s_=s_, prod=prod, KV=KV: e.tensor_tensor(out=prod[:, :], in0=ps[:, 0:256], in1=KV[:, s_, :], op=ALU.mult), [b_KV], [bp, b_pr])
                P.op("dve", lambda e, mh=mh, s_=s_, prod=prod: e.tensor_reduce(out=sc[:, mh, :, s_], in_=prod[:, :].rearrange("p (h d) -> p h d", d=64), axis=AX.X, op=ALU.add), [b_pr], [b_sc])
        ps_s, b_ps = self.psQ[3], self.b_psQ[3]
        for mh in range(2):
            P.op("pe", lambda e, mh=mh: e.transpose(ps_s[0:64, mh * 128:(mh + 1) * 128], sc[:, mh].rearrange("p h s -> p (h s)"), self.identf[:, :]), [b_sc, self.b_const], [b_ps])
        p_ = self.carve([NMEM], F32)
        st_ = self.carve([4], F32)
        b_p = self.pbuf("dp")
        P.op("dve", lambda e: e.reduce_max(out=st_[0:64, 0:1], in_=ps_s[0:64, 0:256], axis=AX.X), [], [b_ps, b_p])
        P.op("dve", lambda e: e.tensor_scalar(out=st_[0:64, 1:2], in0=st_[0:64, 0:1], scalar1=-0.125, scalar2=None, op0=ALU.mult), [], [b_p])
        P.op("act", lambda e: e.activation(out=p_[0:64, :], in_=ps_s[0:64, 0:256], func=AF.Exp, bias=st_[0:64, 1:2], scale=0.125, accum_out=st_[0:64, 2:3]), [], [b_ps, b_p])
        P.op("dve", lambda e: e.reciprocal(out=st_[0:64, 3:4], in_=st_[0:64, 2:3]), [], [b_p])
        P.op("dve", lambda e: e.tensor_scalar(out=p_[0:64, :], in0=p_[0:64, :], scalar1=st_[0:64, 3:4], scalar2=None, op0=ALU.mult), [], [b_p])
        PT = self.carve([2, 4, NSMP], F32)
        b_PT = self.pbuf("dPT")
        ps_b, b_pb = self.psQ[0], self.b_psQ[0]
        for mh in range(2):
            P.op("pe", lambda e, mh=mh: e.transpose(ps_b[:, mh * 64:(mh + 1) * 64], p_[0:64, mh * 128:(mh + 1) * 128], self.identf[0:64, 0:64]), [b_p, self.b_const], [b_pb])
        self.copy("act", PT[:, :, :, :], ps_b[:, 0:128].rearrange("p (a h s) -> p a h s", h=4, s=NSMP), [], [b_pb, b_PT])
        ps_x, b_px = self.psQ[3], self.b_psQ[3]
        for mh in range(2):
            KV, b_KV = KVs[2 + mh], b_KVs[2 + mh]
            for s_ in range(NSMP):
                prodb, b_pb = prodb2[s_ % 2], b_pb2[s_ % 2]
                P.op("dve", lambda e, mh=mh, s_=s_, prodb=prodb, KV=KV: e.tensor_tensor(out=prodb[:, :].rearrange("p (h d) -> p h d", d=64), in0=KV[:, s_, :].rearrange("p (h d) -> p h d", d=64),
                                                                                 in1=PT[:, mh, :, s_:s_ + 1].to_broadcast([128, 4, 64]), op=ALU.mult), [b_KV, b_PT], [b_pb])
                for blk in range(2):
                    P.op("pe", lambda e, mh=mh, s_=s_, blk=blk, prodb=prodb: e.matmul(ps_x[:, (mh * 2 + blk) * NSMP + s_:(mh * 2 + blk) * NSMP + s_ + 1], lhsT=prodb[:, blk * 128:(blk + 1) * 128],
                                                                                      rhs=self.onesb[:, 0:1], start=True, stop=True), [b_pb, self.b_const], [b_px])
        xt = self.carve([2, NSMP], F32)
        P.op("dve", lambda e: e.tensor_copy(out=xt[:, :, :], in_=ps_x[:, 0:2 * NSMP].rearrange("p (b s) -> p b s", s=NSMP)), [], [b_px, b_p])
        P.op("dve", lambda e: e.tensor_tensor(out=self.dec_x[:, :, :], in0=ps_x[:, 2 * NSMP:4 * NSMP].rearrange("p (b s) -> p b s", s=NSMP), in1=xt[:, :, :], op=ALU.add),
             [b_p], [b_px, self.b_decx])
        self.phase_end()

    def swa_decode_phase(self, jl):
        P = self.P
        l = NA + jl
        W = self.w_in_b[jl]
        self.phase_begin()
        Kx = self.carve([NSMP, 256], F32)
        Vx = self.carve([NSMP, 256], F32)
        b_Kx, b_Vx = self.pbuf("dKx"), self.pbuf("dVx")
        P.dma(out=Kx[0:127, :, :], in_=self.ssk[:, 1:128, :].rearrange("s e d -> e s d"), writes=[b_Kx], sem="dkx")
        P.dma(out=Kx[127:128, :, :], in_=self.knew_d[:, :].rearrange("(o s) d -> o s d", o=1), reads=[self.b_newd], writes=[b_Kx], sem="dkx")
        P.dma(out=Vx[0:127, :, :], in_=self.ssv[:, 1:128, :].rearrange("s e d -> e s d"), writes=[b_Vx], sem="dvx")
        P.dma(out=Vx[127:128, :, :], in_=self.vnew_d[:, :].rearrange("(o s) d -> o s d", o=1), reads=[self.b_newd], writes=[b_Vx], sem="dvx")
        if jl == 0:
            P.dma(out=self.sks.rearrange("s e d -> e s d"), in_=Kx[:, :, :], reads=[b_Kx], sem="dkx")
            P.dma(out=self.svs.rearrange("s e d -> e s d"), in_=Vx[:, :, :], reads=[b_Vx], sem="dvx")
        Ct, St, b_rope = self.rope_tables(NG - 1, 528)
        hn, b_hn = self.dec_norm(0, l)
        qr = self.carve([6, NSMP], F32)
        tq = self.carve([2, NSMP], F32)
        t2 = self.carve([NSMP], F32)
        b_q = self.pbuf("dq")

        def evac_q(nb, ps, bp, vi):
            i = nb % 2
            if vi == 0:
                P.op("dve", lambda e: e.tensor_tensor(out=tq[:, i, :], in0=ps[:, 0:NSMP], in1=Ct[:, 512:528], op=ALU.mult), [b_rope], list(bp) + [b_q])
            else:
                P.op("dve", lambda e: e.tensor_tensor(out=t2[:, :], in0=ps[:, 0:NSMP], in1=St[:, 512:528], op=ALU.mult), [b_rope], list(bp) + [b_q])
                P.op("dve", lambda e: e.tensor_tensor(out=qr[:, nb, :], in0=tq[:, i, :], in1=t2[:, :], op=ALU.add), [], [b_q])
        self.linear_fm(W, D, 0, 6, hn, b_hn, NSMP, evac_q, swap=True)
        Qt = self.carve([HW], F32)
        ps_t, b_t = self.psQ[0], self.b_psQ[0]
        ps_t2, b_t2 = self.psQ[1], self.b_psQ[1]
        for blk in range(6):
            pp, bb = (ps_t, b_t) if blk < 4 else (ps_t2, b_t2)
            P.op("pe", lambda e, blk=blk, pp=pp: e.transpose(pp[0:NSMP, (blk % 4) * 128:(blk % 4 + 1) * 128], qr[:, blk, :], self.identf[:, :]), [b_q, self.b_const], [bb])
        self.copy("dve", Qt[0:NSMP, 0:512], ps_t[0:NSMP, 0:512], [], [b_t, b_q])
        self.copy("dve", Qt[0:NSMP, 512:768], ps_t2[0:NSMP, 0:256], [], [b_t2, b_q])
        prod2 = [self.carve([HW], F32) for _ in range(2)]
        prodb2 = [self.carve([HW], BF16) for _ in range(2)]
        b_pr2 = self.pbufs("dprod", 2)
        b_pb2 = self.pbufs("dprodb", 2)
        sc = self.carve([12, NSMP], F32)
        b_sc = self.pbuf("dsc")
        self.sel_tiles()
        for s_ in range(NSMP):
            st, bst = self.sel(s_)
            ps, bp = self.acc_tile()
            P.op("pe", lambda e, ps=ps, st=st: e.matmul(ps[:, 0:512], lhsT=st[0:NSMP, :], rhs=Qt[0:NSMP, 0:512], start=True, stop=True), [bst, b_q], [bp[0]])
            P.op("pe", lambda e, ps=ps, st=st: e.matmul(ps[:, 512:768], lhsT=st[0:NSMP, :], rhs=Qt[0:NSMP, 512:768], start=True, stop=True), [bst, b_q], [bp[1]])
            prod, b_pr = prod2[s_ % 2], b_pr2[s_ % 2]
            P.op("dve", lambda e, ps=ps, s_=s_, prod=prod: e.tensor_tensor(out=prod[:, :].rearrange("p (g i d) -> p g i d", i=3, d=64), in0=ps[:, 0:768].rearrange("p (g i d) -> p g i d", i=3, d=64),
                                                                          in1=Kx[:, s_, :].rearrange("p (g d) -> p g d", d=64).unsqueeze(2).to_broadcast([128, 4, 3, 64]), op=ALU.mult),
                 [b_Kx], list(bp) + [b_pr])
            P.op("dve", lambda e, s_=s_, prod=prod: e.tensor_reduce(out=sc[:, :, s_], in_=prod[:, :].rearrange("p (h d) -> p h d", d=64), axis=AX.X, op=ALU.add), [b_pr], [b_sc])
        Hsel = self.carve([12 * NSMP], F32)
        skc = self.carve([1], F32)
        b_hs = self.pbuf("dHs")
        P.op("dve", lambda e: e.tensor_copy(out=Hsel[0:12, :].rearrange("p (h s) -> p h s", s=NSMP), in_=self.identf[0:12, 0:12].unsqueeze(2).to_broadcast([12, 12, NSMP])), [self.b_const], [b_hs])
        P.dma(out=skc[0:12, :], in_=self.sinks[jl, :].rearrange("(h o) -> h o", o=1), writes=[b_hs], sem="dsk")
        sk2 = self.carve([4], F32)
        ps_k, b_pk = self.psQ[2], self.b_psQ[2]
        P.op("pe", lambda e: e.matmul(ps_k[:, 0:1], lhsT=Hsel[0:12, 0:128], rhs=skc[0:12, 0:1], start=True, stop=True), [b_hs], [b_pk])
        P.op("pe", lambda e: e.matmul(ps_k[0:64, 1:2], lhsT=Hsel[0:12, 128:192], rhs=skc[0:12, 0:1], start=True, stop=True), [b_hs], [b_pk])
        P.op("dve", lambda e: e.memset(sk2[:, :], 0.0), [], [b_hs])
        P.op("dve", lambda e: e.tensor_copy(out=sk2[:, 0:1], in_=ps_k[:, 0:1]), [], [b_pk, b_hs])
        P.op("dve", lambda e: e.tensor_copy(out=sk2[0:64, 1:2], in_=ps_k[0:64, 1:2]), [], [b_pk, b_hs])
        P.op("dve", lambda e: e.tensor_scalar(out=sk2[:, 2:4], in0=sk2[:, 0:2], scalar1=-1.0, scalar2=None, op0=ALU.mult), [], [b_hs])
        scf = sc.rearrange("p h s -> p (h s)")
        ps_s = [self.psQ[3], self.psQ[2]]
        b_ps = [self.b_psQ[3], self.b_psQ[2]]
        p_ = self.carve([2, 128], F32)
        st_ = self.carve([2, 4], F32)
        b_p = self.pbuf("dp")
        PT = self.carve([12 * NSMP], F32)
        b_PT = self.pbuf("dPT")
        ps_b, b_pb = self.psQ[0], self.b_psQ[0]
        for rb, (r0, nr) in enumerate(((0, 128), (128, 64))):
            P.op("pe", lambda e, rb=rb, r0=r0, nr=nr: e.transpose(ps_s[rb][0:nr, 0:128], scf[:, r0:r0 + nr], self.identf[:, :]), [b_sc, self.b_const], [b_ps[rb]])
            P.op("dve", lambda e, rb=rb, nr=nr: e.reduce_max(out=st_[0:nr, rb, 0:1], in_=ps_s[rb][0:nr, 0:128], axis=AX.X), [], [b_ps[rb], b_p])
            P.op("dve", lambda e, rb=rb, nr=nr: e.tensor_scalar(out=st_[0:nr, rb, 1:2], in0=st_[0:nr, rb, 0:1], scalar1=-0.125, scalar2=sk2[0:nr, 2 + rb:3 + rb], op0=ALU.mult, op1=ALU.min),
                 [b_hs], [b_p])
            P.op("act", lambda e, rb=rb, nr=nr: e.activation(out=p_[0:nr, rb, :], in_=ps_s[rb][0:nr, 0:128], func=AF.Exp, bias=st_[0:nr, rb, 1:2], scale=0.125, accum_out=st_[0:nr, rb, 2:3]),
                 [], [b_ps[rb], b_p])
            P.op("act", lambda e, rb=rb, nr=nr: e.activation(out=st_[0:nr, rb, 3:4], in_=st_[0:nr, rb, 1:2], func=AF.Exp, bias=sk2[0:nr, rb:rb + 1]), [b_hs], [b_p])
            P.op("dve", lambda e, rb=rb, nr=nr: e.tensor_tensor(out=st_[0:nr, rb, 2:3], in0=st_[0:nr, rb, 2:3], in1=st_[0:nr, rb, 3:4], op=ALU.add), [], [b_p])
            P.op("dve", lambda e, rb=rb, nr=nr: e.reciprocal(out=st_[0:nr, rb, 3:4], in_=st_[0:nr, rb, 2:3]), [], [b_p])
            P.op("dve", lambda e, rb=rb, nr=nr: e.tensor_scalar(out=p_[0:nr, rb, :], in0=p_[0:nr, rb, :], scalar1=st_[0:nr, rb, 3:4], scalar2=None, op0=ALU.mult), [], [b_p])
            P.op("pe", lambda e, rb=rb, r0=r0, nr=nr: e.transpose(ps_b[:, r0:r0 + nr], p_[0:nr, rb, :], self.identf[0:nr, 0:nr]), [b_p, self.b_const], [b_pb])
        self.copy("act", PT[:, :], ps_b[:, 0:192], [], [b_pb, b_PT])
        PT3 = PT.rearrange("p (h s) -> p h s", s=NSMP)
        ps_x, b_px = self.psQ[3], self.b_psQ[3]
        for s_ in range(NSMP):
            prodb, b_pb = prodb2[s_ % 2], b_pb2[s_ % 2]
            P.op("dve", lambda e, s_=s_, prodb=prodb: e.tensor_tensor(out=prodb[:, :].rearrange("p (g i d) -> p g i d", i=3, d=64),
                                                                     in0=Vx[:, s_, :].rearrange("p (g d) -> p g d", d=64).unsqueeze(2).to_broadcast([128, 4, 3, 64]),
                                                                     in1=PT3[:, :, s_].rearrange("p (g i) -> p g i", i=3).unsqueeze(3).to_broadcast([128, 4, 3, 64]), op=ALU.mult),
                 [b_Vx, b_PT], [b_pb])
            for blk in range(6):
                P.op("pe", lambda e, s_=s_, blk=blk, prodb=prodb: e.matmul(ps_x[:, blk * NSMP + s_:blk * NSMP + s_ + 1], lhsT=prodb[:, blk * 128:(blk + 1) * 128], rhs=self.onesb[:, 0:1],
                                                                           start=True, stop=True), [b_pb, self.b_const], [b_px])
        self.copy("act", self.dec_o[:, :, :], ps_x[:, 0:6 * NSMP].rearrange("p (h s) -> p h s", s=NSMP), [], [b_px, self.b_deco])
        self.phase_end()

    def build(self):
        P = self.P
        self.declare_io(full=self.stop_after is None)
        self.alloc_common()
        self.phase_begin()
        self.setup_consts()
        if self.stop_after == "consts":
            self.dump("colA", self.colA[:], [128, 128], [self.b_const])
            self.dump("smask0", self.smask0[:], [128, 256], [self.b_const])
            self.dump("hmask", self.hmask[:], [128, 128], [self.b_const])
            self.dump("lbc", self.lbc[:], [128, 12], [self.b_const])
            return self.finish()
        self.load_transpose_tokens(self.xp, NTOK, self.hT, 0, lambda t: self.b_h[t // 4], "xp")
        self.load_transpose_tokens(self.xs, NSMP, self.hT, NTOK, self.b_h[NG - 1], "xs")
        self.dump("hT0", self.hT[:, :, :], [128, KC, TT], self.b_h)
        if self.stop_after == "xT":
            return self.finish()
        self.mem_kv()
        self.phase_end()
        if self.stop_after == "memkv":
            return self.finish()
        self.alloc_a_state()
        self.dec_alloc()
        for l in range(NA):
            self.P.next_epoch()
            for h in range(HH):
                self.P.op("dve", lambda e, h=h: e.memset(self.Sf[:, h, :], 0.0), [], [self.b_Sf[h]])
                self.P.op("dve", lambda e, h=h: e.memset(self.Sb[:, h, 0, :], 0.0), [], [self.b_Sb[h][0]])
                self.P.op("dve", lambda e, h=h: e.memset(self.Gacc[:, h:h + 1], 0.0), [], [self.b_G[h]])
            self.a_decode_phase(l)
            if self.stop_after == f"adec_{l}":
                self.dump("dec_o", self.dec_o[:, :, :], [128, HH, NSMP], [self.b_deco], BF16)
                return self.finish()
            for g in range(NG):
                self.a_phase1(l, g)
                if getattr(self, "cut_hit", False):
                    return self.finish()
                if self.stop_after == f"a1_{l}_{g}":
                    return self.finish()
            self.a_exchange(l)
            self.mem_decode_phase(l, self.w_in_a[l], 4 * HW)
            for g in range(NG):
                self.a_phase2(l, g)
                if self.stop_after == f"a2_{l}_{g}":
                    self.dump("hT", self.hT[:, :, :], [128, KC, TT], self.b_h)
                    return self.finish()
                self.mlp_phase(l, g)
                if self.stop_after == f"a3_{l}_{g}":
                    self.dump("hT", self.hT[:, :, :], [128, KC, TT], self.b_h)
                    return self.finish()
        self.P.next_epoch()
        self.setup_rope_consts()
        self.kv_alloc()
        for g in range(NG):
            self.kv_pass(g)
        self.kv_exchange()
        if self.stop_after == "kv":
            return self.finish()
        for l in range(NA, DEPTH):
            self.P.next_epoch()
            self.swa_decode_phase(l - NA)
            self.mem_decode_phase(l, self.w_in_b[l - NA], HW)
            for g in range(NG):
                self.b_phase(l, g)
                self.mlp_phase(l, g)
                if self.stop_after == f"b_{l}_{g}":
                    self.dump("hT", self.hT[:, :, :], [128, KC, TT], self.b_h)
                    return self.finish()
        self.P.next_epoch()
        self.write_outputs()
        return self.finish()

    def finish(self):
        self.P.emit(self.es)
        return self.nc


def make_in_maps(inp):
    f = lambda a: np.ascontiguousarray(np.asarray(a, dtype=np.float32))
    vecA = np.concatenate([f(inp[k]).reshape(32, 128) for k in ("norm_pre_mix", "norm_post_mix", "norm_pre_mlp", "norm_post_mlp")], 0)
    vecB = np.concatenate([f(inp["mem_norm"]).reshape(32, 128), f(inp["kv_norm"]).reshape(8, 128),
                           f(inp["hgrn_norm"]).reshape(12, 128), f(inp["lb_logits"]).reshape(12, 128)], 0)
    shared = {k: f(inp[k]) for k in ("w_mem_kv", "w_in_a", "w_in_b", "w_kv", "w_out", "w_up", "w_down", "sinks")}
    shared["vecA"] = f(vecA)
    shared["vecB"] = f(vecB)
    maps = []
    for c in range(NCORES):
        b, r = c // 4, c % 4
        m = dict(shared)
        m["xp"] = f(inp["x_prompt"][b, r * NTOK:(r + 1) * NTOK])
        m["xs"] = f(inp["x_sample"][c * NSMP:(c + 1) * NSMP, 0])
        m["memp"] = f(inp["mem_prompt"][b])
        m["cmk"] = f(np.asarray(inp["cache_mem_k"])[:, c * NSMP:(c + 1) * NSMP].reshape(DEPTH, NSMP, NMEM, XW))
        m["cmv"] = f(np.asarray(inp["cache_mem_v"])[:, c * NSMP:(c + 1) * NSMP].reshape(DEPTH, NSMP, NMEM, XW))
        m["shg"] = f(np.asarray(inp["state_hgrn"])[:, c * NSMP:(c + 1) * NSMP])
        m["ssk"] = f(np.asarray(inp["state_swa_k"])[c * NSMP:(c + 1) * NSMP].reshape(NSMP, WIN, 256))
        m["ssv"] = f(np.asarray(inp["state_swa_v"])[c * NSMP:(c + 1) * NSMP].reshape(NSMP, WIN, 256))
        cc = np.zeros((128, 16), np.float32)
        cc[:, 0] = r * NTOK
        cc[:, 1 + r] = 1.0
        if r > 0:
            cc[:, 5 + (r - 1)] = 1.0
        cc[:, 9] = 1.0 if r == 0 else 0.0
        m["corec"] = cc
        maps.append(m)
    return maps


def assemble(results):
    R = results
    y_p = np.stack([np.concatenate([R[b * 4 + r]["y_p"] for r in range(4)], 0) for b in range(2)], 0)
    y_s = np.concatenate([R[c]["y_s"] for c in range(NCORES)], 0)[:, None, :]
    mk = np.stack([R[b * 4]["mkp"] for b in range(2)], 1).reshape(DEPTH, 2, NMEM, 4, 64)
    mv = np.stack([R[b * 4]["mvp"] for b in range(2)], 1).reshape(DEPTH, 2, NMEM, 4, 64)
    hgp = np.stack([R[b * 4 + 3]["hgp"] for b in range(2)], 1)
    skp = np.stack([R[b * 4 + 3]["skp"] for b in range(2)], 0).reshape(2, WIN, 4, 64)
    svp = np.stack([R[b * 4 + 3]["svp"] for b in range(2)], 0).reshape(2, WIN, 4, 64)
    hgs = np.concatenate([R[c]["hgs"] for c in range(NCORES)], 1)
    sks = np.concatenate([R[c]["sks"] for c in range(NCORES)], 0).reshape(NCORES * NSMP, WIN, 4, 64)
    svs = np.concatenate([R[c]["svs"] for c in range(NCORES)], 0).reshape(NCORES * NSMP, WIN, 4, 64)
    outs = (y_p, y_s, mk, mv, hgp, skp, svp, hgs, sks, svs)
    return tuple(np.ascontiguousarray(o, dtype=np.float32) for o in outs)


def kernel(**inputs):
    kb = KB()
    nc = kb.build()
    maps = make_in_maps(inputs)
    res = run_bass_kernel_spmd(nc, maps, core_ids=list(range(NCORES)))
    return assemble(res.results)
```
